# Optimizing a Trainium2 kernel written in Bass

```python
import math
import jax, jax.numpy as jnp
from jax import lax
import numpy as np

D_MODEL = 2048
BATCH = 2
SEQ = 8192
DEPTH = 4

D_MIX = D_MODEL
POOL_WIDTH = D_MIX // 4
POOL_WINDOWS = (2, 4, 8, 16)
POOL_GROUP = POOL_WIDTH // len(POOL_WINDOWS)
SSD_WIDTH = D_MIX // 2
SSD_HEAD_DIM = 64
SSD_HEADS = SSD_WIDTH // SSD_HEAD_DIM
SSD_GROUPS = 2
SSD_STATE = 128
SSD_CONV = 4
SSD_CHUNK = 256
ATTN_WIDTH = D_MIX - POOL_WIDTH - SSD_WIDTH
ATTN_HEAD_DIM = 64
ATTN_HEADS = ATTN_WIDTH // ATTN_HEAD_DIM
ATTN_BLOCK = 128
XBC_WIDTH = SSD_WIDTH + 2 * SSD_GROUPS * SSD_STATE
IN_WIDTH = POOL_WIDTH + SSD_WIDTH + XBC_WIDTH + SSD_HEADS + 3 * ATTN_WIDTH
D_FF = ((8 * D_MODEL // 3 + 255) // 256) * 256
RMS_EPS = 1e-6

kernel_name = "hymba_pool_ssd_stickbreak_macaron"

F32 = jnp.float32


def rms_norm(x, g):
    xf = x.astype(F32)
    y = xf * lax.rsqrt(jnp.mean(xf * xf, axis=-1, keepdims=True) + RMS_EPS)
    return (y * g.astype(F32)).astype(x.dtype)


def swiglu(u, w_gate, w_up, w_down):
    a = jnp.einsum('bsd,df->bsf', u, w_gate)
    b = jnp.einsum('bsd,df->bsf', u, w_up)
    return jnp.einsum('bsf,fd->bsd', jax.nn.silu(a) * b, w_down)


def multiscale_pool(v, w_pool, scale):
    s_len = v.shape[1]
    vf = v.astype(F32)
    cs = jnp.cumsum(vf, axis=1)
    pos = jnp.arange(1, s_len + 1, dtype=F32)
    outs = []
    for i, w in enumerate(POOL_WINDOWS):
        sl = slice(i * POOL_GROUP, (i + 1) * POOL_GROUP)
        c = cs[..., sl]
        prev = jnp.pad(c, ((0, 0), (w, 0), (0, 0)))[:, :s_len]
        mean = (c - prev) / jnp.minimum(pos, float(w))[None, :, None]
        outs.append(jnp.einsum('bsc,cd->bsd', mean - vf[..., sl], w_pool[i].astype(F32)))
    return (jnp.concatenate(outs, axis=-1) * scale.astype(F32)).astype(v.dtype)


def causal_dwconv(x, w, b):
    k_len = w.shape[0]
    s_len = x.shape[1]
    xp = jnp.pad(x, ((0, 0), (k_len - 1, 0), (0, 0)))
    y = xp[:, 0:s_len] * w[0]
    for k in range(1, k_len):
        y = y + xp[:, k:k + s_len] * w[k]
    return y + b


def segsum_exp(a):
    t = a.shape[-1]
    strict = jnp.tril(jnp.ones((t, t), dtype=bool), -1)
    incl = jnp.tril(jnp.ones((t, t), dtype=bool))
    rep = jnp.where(strict, a[..., :, None], 0.0)
    ss = jnp.cumsum(rep, axis=-2)
    return jnp.where(incl, jnp.exp(ss), 0.0)


def ssd_chunked(X, A, Bm, Cm):
    b, s_len, h, p = X.shape
    pad = (-s_len) % SSD_CHUNK
    if pad:
        X = jnp.pad(X, ((0, 0), (0, pad), (0, 0), (0, 0)))
        A = jnp.pad(A, ((0, 0), (0, pad), (0, 0)))
        Bm = jnp.pad(Bm, ((0, 0), (0, pad), (0, 0), (0, 0)))
        Cm = jnp.pad(Cm, ((0, 0), (0, pad), (0, 0), (0, 0)))
    t_len = s_len + pad
    nc, L, g = t_len // SSD_CHUNK, SSD_CHUNK, SSD_GROUPS
    e = h // g
    n = Bm.shape[-1]
    X = X.reshape(b, nc, L, g, e, p)
    A = A.reshape(b, nc, L, g, e).transpose(0, 3, 4, 1, 2)
    Bm = Bm.reshape(b, nc, L, g, n)
    Cm = Cm.reshape(b, nc, L, g, n)
    A_cs = jnp.cumsum(A, axis=-1)
    decay_in = segsum_exp(A)
    CB = jnp.einsum('bclgn,bcsgn->bgcls', Cm, Bm)
    Y_diag = jnp.einsum('bgcls,bgecls,bcsgep->bclgep', CB, decay_in, X)
    decay_states = jnp.exp(A_cs[..., -1:] - A_cs)
    states = jnp.einsum('bclgn,bgecl,bclgep->bcgepn', Bm, decay_states, X)
    chunk_decay = jnp.moveaxis(jnp.exp(A_cs[..., -1]), -1, 0)
    states_c = jnp.moveaxis(states, 1, 0)

    def step(carry, inp):
        s_c, d_c = inp
        return carry * d_c[..., None, None] + s_c, carry

    init = jnp.zeros(states_c.shape[1:], dtype=states_c.dtype)
    _, states_in = lax.scan(step, init, (states_c, chunk_decay))
    Y_off = jnp.einsum('bclgn,cbgepn,bgecl->bclgep', Cm, states_in, jnp.exp(A_cs))
    Y = (Y_diag + Y_off).reshape(b, t_len, h, p)
    return Y[:, :s_len]


def ssd_mixer(z, xbc, dt_raw, conv_w, conv_b, dt_bias, a_log, d_skip, norm_g):
    bsz, s_len, _ = z.shape
    xbc = jax.nn.silu(causal_dwconv(xbc, conv_w, conv_b)).astype(F32)
    gn = SSD_GROUPS * SSD_STATE
    xs, Bm, Cm = jnp.split(xbc, [SSD_WIDTH, SSD_WIDTH + gn], axis=-1)
    xs = xs.reshape(bsz, s_len, SSD_HEADS, SSD_HEAD_DIM)
    Bm = Bm.reshape(bsz, s_len, SSD_GROUPS, SSD_STATE)
    Cm = Cm.reshape(bsz, s_len, SSD_GROUPS, SSD_STATE)
    dt = jax.nn.softplus(dt_raw.astype(F32) + dt_bias.astype(F32))
    A = -jnp.exp(a_log.astype(F32))
    y = ssd_chunked(xs * dt[..., None], A * dt, Bm, Cm)
    y = y + d_skip.astype(F32)[:, None] * xs
    y = y.reshape(bsz, s_len, SSD_WIDTH) * jax.nn.silu(z.astype(F32))
    yg = y.reshape(bsz, s_len, SSD_GROUPS, SSD_WIDTH // SSD_GROUPS)
    yg = yg * lax.rsqrt(jnp.mean(yg * yg, axis=-1, keepdims=True) + RMS_EPS)
    y = yg.reshape(bsz, s_len, SSD_WIDTH) * norm_g.astype(F32)
    return y.astype(z.dtype)


def stick_breaking_attention(q, k, v):
    bsz, s_len, nh, dh = q.shape
    nb = s_len // ATTN_BLOCK
    qf = (q.astype(F32) * (dh ** -0.5)).reshape(bsz, nb, ATTN_BLOCK, nh, dh).transpose(1, 0, 3, 2, 4)
    kf = k.astype(F32).transpose(0, 2, 1, 3)
    vf = v.astype(F32).transpose(0, 2, 1, 3)
    key_pos = jnp.arange(s_len, dtype=jnp.int32)
    starts = jnp.arange(nb, dtype=jnp.int32) * ATTN_BLOCK
    q_off = jnp.arange(ATTN_BLOCK, dtype=jnp.int32)

    def one_block(args):
        qb, q0 = args
        logits = jnp.einsum('bhqd,bhkd->bhqk', qb, kf)
        mask = key_pos[None, :] < (q0 + q_off)[:, None]
        log_1m = jnp.where(mask, jax.nn.log_sigmoid(-logits), 0.0)
        suffix = lax.cumsum(log_1m, axis=3, reverse=True) - log_1m
        weights = jnp.where(mask, jnp.exp(jax.nn.log_sigmoid(logits) + suffix), 0.0)
        return jnp.einsum('bhqk,bhkd->bhqd', weights, vf)

    out = lax.map(one_block, (qf, starts))
    return out.transpose(1, 0, 3, 2, 4).reshape(bsz, s_len, nh * dh).astype(q.dtype)


def hybrid_mixer(u, w_in, pool_w, pool_scale, conv_w, conv_b, dt_bias, a_log, d_skip, ssd_norm, w_out):
    bsz, s_len, _ = u.shape
    proj = jnp.einsum('bsd,dn->bsn', u, w_in)
    idx = np.cumsum([POOL_WIDTH, SSD_WIDTH, XBC_WIDTH, SSD_HEADS, ATTN_WIDTH, ATTN_WIDTH]).tolist()
    pool_in, z, xbc, dt_raw, q, k, v = jnp.split(proj, idx, axis=-1)
    pool_out = multiscale_pool(pool_in, pool_w, pool_scale)
    ssd_out = ssd_mixer(z, xbc, dt_raw, conv_w, conv_b, dt_bias, a_log, d_skip, ssd_norm)
    shp = (bsz, s_len, ATTN_HEADS, ATTN_HEAD_DIM)
    attn_out = stick_breaking_attention(q.reshape(shp), k.reshape(shp), v.reshape(shp))
    mixed = jnp.concatenate([pool_out, ssd_out, attn_out], axis=-1)
    return jnp.einsum('bsm,md->bsd', mixed, w_out)


def setup_inputs(seed: int = 0) -> dict:
    key = jax.random.key(seed)
    ks = jax.random.split(key, 24)

    def nrm(k, shape, scale):
        return jax.random.normal(k, shape, dtype=F32) * scale

    def gain(k, shape):
        return 1.0 + 0.02 * jax.random.normal(k, shape, dtype=F32)

    dt0 = jnp.exp(jax.random.uniform(ks[13], (DEPTH, SSD_HEADS), dtype=F32)
                  * (math.log(0.1) - math.log(0.001)) + math.log(0.001))
    return {
        "x": jax.random.normal(ks[0], (BATCH, SEQ, D_MODEL), dtype=F32),
        "ffn1_norm": gain(ks[1], (DEPTH, D_MODEL)),
        "ffn1_w_gate": nrm(ks[2], (DEPTH, D_MODEL, D_FF), D_MODEL ** -0.5),
        "ffn1_w_up": nrm(ks[3], (DEPTH, D_MODEL, D_FF), D_MODEL ** -0.5),
        "ffn1_w_down": nrm(ks[4], (DEPTH, D_FF, D_MODEL), D_FF ** -0.5),
        "mix_norm": gain(ks[5], (DEPTH, D_MODEL)),
        "w_in": nrm(ks[6], (DEPTH, D_MODEL, IN_WIDTH), D_MODEL ** -0.5),
        "pool_w": nrm(ks[7], (DEPTH, len(POOL_WINDOWS), POOL_GROUP, POOL_GROUP), POOL_GROUP ** -0.5),
        "pool_scale": gain(ks[8], (DEPTH, POOL_WIDTH)),
        "conv_w": nrm(ks[9], (DEPTH, SSD_CONV, XBC_WIDTH), SSD_CONV ** -0.5),
        "conv_b": nrm(ks[10], (DEPTH, XBC_WIDTH), 0.01),
        "dt_bias": dt0 + jnp.log(-jnp.expm1(-dt0)),
        "a_log": jnp.log(jax.random.uniform(ks[11], (DEPTH, SSD_HEADS), dtype=F32, minval=1.0, maxval=16.0)),
        "d_skip": 1.0 + 0.1 * jax.random.normal(ks[12], (DEPTH, SSD_HEADS), dtype=F32),
        "ssd_norm": gain(ks[14], (DEPTH, SSD_WIDTH)),
        "w_out": nrm(ks[15], (DEPTH, D_MIX, D_MODEL), D_MIX ** -0.5),
        "ffn2_norm": gain(ks[16], (DEPTH, D_MODEL)),
        "ffn2_w_gate": nrm(ks[17], (DEPTH, D_MODEL, D_FF), D_MODEL ** -0.5),
        "ffn2_w_up": nrm(ks[18], (DEPTH, D_MODEL, D_FF), D_MODEL ** -0.5),
        "ffn2_w_down": nrm(ks[19], (DEPTH, D_FF, D_MODEL), D_FF ** -0.5),
        "final_norm": gain(ks[20], (D_MODEL,)),
    }


def reference(x, ffn1_norm, ffn1_w_gate, ffn1_w_up, ffn1_w_down, mix_norm, w_in, pool_w, pool_scale,
              conv_w, conv_b, dt_bias, a_log, d_skip, ssd_norm, w_out, ffn2_norm, ffn2_w_gate,
              ffn2_w_up, ffn2_w_down, final_norm):
    h = x
    for l in range(DEPTH):
        u = rms_norm(h, ffn1_norm[l])
        h = h + 0.5 * swiglu(u, ffn1_w_gate[l], ffn1_w_up[l], ffn1_w_down[l])
        u = rms_norm(h, mix_norm[l])
        h = h + hybrid_mixer(u, w_in[l], pool_w[l], pool_scale[l], conv_w[l], conv_b[l], dt_bias[l],
                             a_log[l], d_skip[l], ssd_norm[l], w_out[l])
        u = rms_norm(h, ffn2_norm[l])
        h = h + 0.5 * swiglu(u, ffn2_w_gate[l], ffn2_w_up[l], ffn2_w_down[l])
    return rms_norm(h, final_norm)
```

```python
import numpy as np
import ml_dtypes
from contextlib import ExitStack
import concourse.bass as bass
import concourse.mybir as mybir
from concourse.bass_utils import run_bass_kernel_spmd

F32 = mybir.dt.float32
BF16 = mybir.dt.bfloat16
AF = mybir.ActivationFunctionType
ALU = mybir.AluOpType
AX = mybir.AxisListType

ENG = ("pe", "act", "dve", "pool", "sp")
NDQ = 8


class Tk:
    __slots__ = ("name", "w", "r", "excl")

    def __init__(self, name="", excl=False):
        self.name = name
        self.w = None
        self.r = {}
        self.excl = excl


class Prog:
    def __init__(self, arena_f32=49152):
        self.nc = bass.Bass("TRN2", target_bir_lowering=False)
        self.es = ExitStack()
        self.ops = {e: [] for e in ENG}
        self.cnt = {e: 0 for e in ENG}
        self.dcnt = {}
        self.dnext = {q: 0 for q in ("sp", "act", "pool")}
        self.seen = {e: {} for e in ENG}
        self.sems = {}
        nc = self.nc
        for e in ENG:
            self.sems[e] = self.es.enter_context(nc.semaphore("s_" + e))
        for q in ("sp", "act", "pool"):
            for j in range(NDQ):
                k = "d_%s_%d" % (q, j)
                self.sems[k] = self.es.enter_context(nc.semaphore(k))
                self.dcnt[k] = 0
        self.arena = self.es.enter_context(nc.sbuf_tensor("arena", [128, arena_f32], F32))
        self.arena_n = arena_f32
        self.aoff = 0
        self.psum = []
        self.pbank = []
        for i in range(8):
            t = self.es.enter_context(nc.psum_tensor("ps%d" % i, [128, 512], F32))
            self.psum.append(t)
            self.pbank.append(Tk("ps%d" % i, excl=True))
        self.n_inst = 0

    def reset_arena(self, keep=0):
        self.aoff = keep

    def alloc(self, name, cols, dtype=F32):
        nf = cols if dtype == F32 else (cols + 1) // 2
        nf = (nf + 7) // 8 * 8
        assert self.aoff + nf <= self.arena_n, ("arena overflow", name, self.aoff, nf)
        ap = self.arena[:, self.aoff:self.aoff + nf]
        self.aoff += nf
        if dtype != F32:
            ap = ap.bitcast(dtype)[:, 0:cols]
        else:
            ap = ap[:, 0:cols]
        return ap, Tk(name)

    def dram(self, name, shape, dtype, kind="Internal"):
        return self.nc.dram_tensor(name, list(shape), dtype, kind=kind).ap()

    def _waits(self, e, reads, writes):
        waits = {}

        def need(dep):
            if dep is None:
                return
            k, v = dep
            if k == e and e == "pe":
                return
            if waits.get(k, 0) < v:
                waits[k] = v

        for t in reads:
            need(t.w)
            if t.excl:
                for k, v in t.r.items():
                    need((k, v))
        for t in writes:
            need(t.w)
            for k, v in t.r.items():
                need((k, v))
        wl = []
        for k, v in waits.items():
            if self.seen[e].get(k, 0) < v:
                self.seen[e][k] = v
                wl.append((k, v))
        return wl

    def op(self, e, fn, reads=(), writes=()):
        wl = self._waits(e, reads, writes)
        self.cnt[e] += 1
        c = self.cnt[e]
        sems = self.sems
        semE = sems[e]

        def emit(eng):
            for k, v in wl:
                eng.wait_ge(sems[k], v)
            fn(eng).then_inc(semE, 1)

        self.ops[e].append(emit)
        self.n_inst += 1 + len(wl)
        for t in reads:
            if t.excl:
                t.w = (e, c)
                t.r = {}
            else:
                t.r[e] = c
        for t in writes:
            t.w = (e, c)
            t.r = {}

    def dma(self, q, out_ap, in_ap, reads=(), writes=()):
        wl = self._waits(q, reads, writes)
        j = self.dnext[q]
        self.dnext[q] = (j + 1) % NDQ
        key = "d_%s_%d" % (q, j)
        prev = self.dcnt[key]
        if prev > 0 and self.seen[q].get(key, 0) < prev:
            self.seen[q][key] = prev
            wl.append((key, prev))
        self.dcnt[key] = prev + 16
        v = prev + 16
        sems = self.sems

        def emit(eng):
            for k, vv in wl:
                eng.wait_ge(sems[k], vv)
            eng.dma_start(out=out_ap, in_=in_ap).then_inc(sems[key], 16)

        self.ops[q].append(emit)
        self.n_inst += 1 + len(wl)
        for t in reads:
            t.r[key] = v
        for t in writes:
            t.w = (key, v)
            t.r = {}

    def dump(self, name, ap, tk, dtype=F32):
        if not getattr(self, "debug", False):
            return
        d = self.dram("dbg_" + name, [ap.shape[0], ap.shape[1]], dtype, "ExternalOutput")
        self.dma("sp", d, ap, [tk], [Tk()])

    def barrier(self):
        cur = dict(self.cnt)
        cur.update(self.dcnt)
        sems = self.sems
        for e in ENG:
            wl = []
            for k, v in cur.items():
                if k != e and v > self.seen[e].get(k, 0):
                    self.seen[e][k] = v
                    wl.append((k, v))

            def emit(eng, wl=wl):
                for k, v in wl:
                    eng.wait_ge(sems[k], v)

            self.ops[e].append(emit)
            self.n_inst += len(wl)

    def finish(self):
        self.barrier()
        nc = self.nc
        ops = self.ops
        with nc.Block() as block:
            @block.tensor
            def _(eng):
                for f in ops["pe"]:
                    f(eng)

            @block.scalar
            def _(eng):
                for f in ops["act"]:
                    f(eng)

            @block.vector
            def _(eng):
                for f in ops["dve"]:
                    f(eng)

            @block.gpsimd
            def _(eng):
                for f in ops["pool"]:
                    f(eng)

            @block.sync
            def _(eng):
                for f in ops["sp"]:
                    f(eng)
        self.es.close()
        return nc


D = 2048
DFF = 5632
NT = 2048
T = 512
KD = D // 128
KF = DFF // 128
EPS = 1e-6
WB = 8192
NWB = 5


def v3(ap, k):
    return ap.rearrange("p (k t) -> p k t", k=k)


class Chain:
    def __init__(self, P, n_ffn, has_mix, epilogue):
        self.P = P
        nc = P.nc
        self.n_ffn, self.has_mix, self.epi = n_ffn, has_mix, epilogue
        self.h_in = P.dram("h_in", [D, NT], F32, "ExternalInput")
        self.t_hin = Tk("h_in")
        ncv = 16 * (n_ffn + 1) + 8
        self.ncv = ncv
        self.cv_d = P.dram("cvec", [128, ncv], F32, "ExternalInput")
        self.w32 = []
        self.wbf = []
        self.twb = []
        for i in range(n_ffn):
            for nm, shp in (("wg", [D, DFF]), ("wu", [D, DFF]), ("wd", [DFF, D])):
                self.w32.append(P.dram("%s%d" % (nm, i), shp, F32, "ExternalInput"))
                self.wbf.append(P.dram("%s%d_bf" % (nm, i), shp, BF16))
                self.twb.append([Tk() for _ in range(shp[0] // 128)])
        if has_mix:
            self.mix_d = P.dram("mixT", [D, NT], BF16, "ExternalInput")
            self.wo32 = P.dram("wout", [D, D], F32, "ExternalInput")
            self.wobf = P.dram("wout_bf", [D, D], BF16)
            self.two = [Tk() for _ in range(KD)]
        if epilogue == "u":
            self.h_out = P.dram("h_out", [D, NT], F32, "ExternalOutput")
            self.u_out = P.dram("u_out", [D, NT], BF16, "ExternalOutput")
        else:
            self.o_out = P.dram("o_out", [D, NT], F32, "ExternalOutput")
        self.t_out = Tk("out")
        self.cv, self.tcv = P.alloc("cv", ncv)
        self.ones, self.tones = P.alloc("ones", 128, BF16)
        self.h, self.th = P.alloc("h", KD * T)
        self.u, self.tu = P.alloc("u", KD * T, BF16)
        self.act, self.tact = P.alloc("act", KF * T, BF16)
        self.sq = [P.alloc("sq%d" % i, T, BF16) for i in range(2)]
        self.sg = [P.alloc("sg%d" % i, T) for i in range(2)]
        self.rs, self.trs = P.alloc("rs", T)
        self.wb = [P.alloc("wb%d" % i, WB, BF16) for i in range(NWB)]
        self.wbi = 0
        self.tk_h = [Tk("h%d" % k) for k in range(KD)]
        self.tk_u = [Tk("u%d" % k) for k in range(KD)]
        self.tk_a = [Tk("a%d" % k) for k in range(KF)]
        self.alt = 0

    def nextwb(self):
        w = self.wb[self.wbi]
        self.wbi = (self.wbi + 1) % NWB
        return w

    def ew(self):
        self.alt ^= 1
        return "dve" if self.alt else "pool"

    def cast_weights(self):
        P = self.P
        P.op("pool", lambda e: e.memset(self.ones, 1.0), [], [self.tones])
        P.dma("sp", self.cv, self.cv_d, [], [self.tcv])
        if self.has_mix:
            for k in range(KD):
                P.dma("pool", self.wobf[k * 128:(k + 1) * 128, :], self.wo32[k * 128:(k + 1) * 128, :],
                      [], [self.two[k]])
        for i in range(len(self.w32)):
            n = self.w32[i].shape[0] // 128
            for k in range(n):
                P.dma("pool", self.wbf[i][k * 128:(k + 1) * 128, :], self.w32[i][k * 128:(k + 1) * 128, :],
                      [], [self.twb[i][k]])

    def norm_stats(self, src3, tks, idxs, nfeat, bank):
        P = self.P
        ps, tps = P.psum[bank], P.pbank[bank]
        n = len(idxs)
        for i, k in enumerate(idxs):
            sq, tsq = self.sq[i % 2]
            P.op("act", lambda e, k=k, sq=sq: e.activation(sq, src3[:, k, :], AF.Square), [tks[k]], [tsq])
            P.op("pe", lambda e, i=i, sq=sq: e.matmul(ps[:, :], self.ones, sq, start=(i == 0), stop=(i == n - 1)),
                 [self.tones, tsq], [tps])
        P.op("act", lambda e: e.activation(self.rs, ps[:, :], AF.Sqrt, bias=EPS, scale=1.0 / nfeat), [tps], [self.trs])
        P.op("dve", lambda e: e.reciprocal(self.rs, self.rs), [self.trs], [self.trs])

    def rmsnorm(self, gcol0, dst3, tdst):
        P = self.P
        h3 = v3(self.h, KD)
        self.norm_stats(h3, self.tk_h, list(range(KD)), D, 4)
        for k in range(KD):
            P.op("dve", lambda e, k=k: e.scalar_tensor_tensor(
                dst3[:, k, :], h3[:, k, :], self.cv[:, gcol0 + k:gcol0 + k + 1], self.rs, ALU.mult, ALU.mult),
                [self.tk_h[k], self.tcv, self.trs], tdst[k] if isinstance(tdst[k], list) else [tdst[k]])

    def ffn(self, i):
        P = self.P
        h3 = v3(self.h, KD)
        u3 = v3(self.u, KD)
        a3 = v3(self.act, KF)
        wg, wu, wd = self.wbf[3 * i], self.wbf[3 * i + 1], self.wbf[3 * i + 2]
        twg, twu, twd = self.twb[3 * i], self.twb[3 * i + 1], self.twb[3 * i + 2]
        self.rmsnorm(16 * i, u3, self.tk_u)
        wg3 = wg.rearrange("(k p) f -> p k f", p=128)
        wu3 = wu.rearrange("(k p) f -> p k f", p=128)
        wd3 = wd.rearrange("(k p) f -> p k f", p=128)
        gi = 0
        for fg in range(KF // 4):
            (wa, twa), (wb_, twb_) = self.nextwb(), self.nextwb()
            wa3, wb3 = v3(wa, KD), v3(wb_, KD)
            P.dma("sp", wa3, wg3[:, :, fg * 512:(fg + 1) * 512], twg, [twa])
            P.dma("sp", wb3, wu3[:, :, fg * 512:(fg + 1) * 512], twu, [twb_])
            for f4 in range(4):
                f = fg * 4 + f4
                bg, bu = (0, 1) if gi % 2 == 0 else (2, 3)
                gi += 1
                pg, pu = P.psum[bg], P.psum[bu]
                for k in range(KD):
                    P.op("pe", lambda e, k=k, f4=f4, pg=pg, wa3=wa3: e.matmul(
                        pg[:, :], wa3[:, k, f4 * 128:(f4 + 1) * 128], u3[:, k, :], start=(k == 0), stop=(k == KD - 1)),
                        [twa, self.tk_u[k]], [P.pbank[bg]])
                for k in range(KD):
                    P.op("pe", lambda e, k=k, f4=f4, pu=pu, wb3=wb3: e.matmul(
                        pu[:, :], wb3[:, k, f4 * 128:(f4 + 1) * 128], u3[:, k, :], start=(k == 0), stop=(k == KD - 1)),
                        [twb_, self.tk_u[k]], [P.pbank[bu]])
                sg, tsg = self.sg[f % 2]
                P.op("act", lambda e, sg=sg, pg=pg: e.activation(sg, pg[:, :], AF.Silu), [P.pbank[bg]], [tsg])
                P.op("dve", lambda e, sg=sg, pu=pu, f=f: e.tensor_tensor(a3[:, f, :], sg, pu[:, :], ALU.mult),
                     [tsg, P.pbank[bu]], [self.tk_a[f]])
        FD = 11
        for dg in range(4):
            banks = [4, 5, 6, 7] if dg % 2 == 0 else [0, 1, 2, 3]
            for fgd in range(KF // FD):
                w, tw = self.nextwb()
                w3 = w[:, 0:FD * 512].rearrange("p (k t) -> p k t", k=FD)
                P.dma("sp", w3, wd3[:, fgd * FD:(fgd + 1) * FD, dg * 512:(dg + 1) * 512],
                      twd[fgd * FD:(fgd + 1) * FD], [tw])
                for j in range(4):
                    pb = P.psum[banks[j]]
                    for f in range(FD):
                        ff = fgd * FD + f
                        P.op("pe", lambda e, j=j, f=f, ff=ff, pb=pb, w3=w3: e.matmul(
                            pb[:, :], w3[:, f, j * 128:(j + 1) * 128], a3[:, ff, :],
                            start=(ff == 0), stop=(ff == KF - 1)),
                            [tw, self.tk_a[ff]], [P.pbank[banks[j]]])
            for j in range(4):
                c = dg * 4 + j
                pb = P.psum[banks[j]]
                P.op("dve", lambda e, c=c, pb=pb: e.scalar_tensor_tensor(
                    h3[:, c, :], pb[:, :], 0.5, h3[:, c, :], ALU.mult, ALU.add),
                    [P.pbank[banks[j]], self.tk_h[c]], [self.tk_h[c]])

    def mix_stage(self, t0):
        P = self.P
        h3 = v3(self.h, KD)
        m3 = v3(self.u, KD)
        P.dma("sp", m3, self.mix_d.rearrange("(k p) t -> p k t", p=128)[:, :, t0:t0 + T], [], self.tk_u)
        gc0 = 16 * (self.n_ffn + 1)
        for grp in range(2):
            idxs = [4 + grp * 4 + c for c in range(4)]
            self.norm_stats(m3, self.tk_u, idxs, 512, 4)
            for c in idxs:
                P.op("dve", lambda e, c=c: e.scalar_tensor_tensor(
                    m3[:, c, :], m3[:, c, :], self.cv[:, gc0 + c - 4:gc0 + c - 3], self.rs, ALU.mult, ALU.mult),
                    [self.tk_u[c], self.tcv, self.trs], [self.tk_u[c]])
        wo3 = self.wobf.rearrange("(k p) f -> p k f", p=128)
        for dg in range(4):
            banks = [0, 1, 2, 3] if dg % 2 == 0 else [4, 5, 6, 7]
            w, tw = self.nextwb()
            w3 = v3(w, KD)
            P.dma("sp", w3, wo3[:, :, dg * 512:(dg + 1) * 512], self.two, [tw])
            for j in range(4):
                pb = P.psum[banks[j]]
                for k in range(KD):
                    P.op("pe", lambda e, j=j, k=k, pb=pb, w3=w3: e.matmul(
                        pb[:, :], w3[:, k, j * 128:(j + 1) * 128], m3[:, k, :], start=(k == 0), stop=(k == KD - 1)),
                        [tw, self.tk_u[k]], [P.pbank[banks[j]]])
            for j in range(4):
                c = dg * 4 + j
                pb = P.psum[banks[j]]
                P.op("dve", lambda e, c=c, pb=pb: e.tensor_tensor(h3[:, c, :], pb[:, :], h3[:, c, :], ALU.add),
                     [P.pbank[banks[j]], self.tk_h[c]], [self.tk_h[c]])

    def emit(self):
        P = self.P
        self.cast_weights()
        h3 = v3(self.h, KD)
        hin3 = self.h_in.rearrange("(k p) t -> p k t", p=128)
        for it in range(NT // T):
            t0 = it * T
            P.dma("sp", h3, hin3[:, :, t0:t0 + T], [self.t_hin], self.tk_h)
            if self.has_mix:
                self.mix_stage(t0)
            for i in range(self.n_ffn):
                self.ffn(i)
            gc = 16 * self.n_ffn
            if self.epi == "u":
                P.dma("act", self.h_out.rearrange("(k p) t -> p k t", p=128)[:, :, t0:t0 + T], h3, self.tk_h, [self.t_out])
                u3 = v3(self.u, KD)
                self.rmsnorm(gc, u3, self.tk_u)
                P.dma("act", self.u_out.rearrange("(k p) t -> p k t", p=128)[:, :, t0:t0 + T], u3, self.tk_u, [self.t_out])
            else:
                o3 = v3(self.act.bitcast(F32)[:, 0:KD * T], KD)
                self.rmsnorm(gc, o3, [[self.tk_a[2 * k], self.tk_a[2 * k + 1]] for k in range(KD)])
                P.dma("act", self.o_out.rearrange("(k p) t -> p k t", p=128)[:, :, t0:t0 + T], o3, self.tk_a[0:2 * KD], [self.t_out])


SEQ = 8192
NSEQ = 2
NTILE = SEQ // T
WSEL = 834
C_POOL, C_Z, C_X, C_B, C_C, C_Q, C_K, C_V, C_DT = 0, 128, 256, 384, 512, 640, 704, 768, 832
NMC = 38
NEG = -30000.0


class Mixer:
    def __init__(self, P, nseq=NSEQ, ntile=NTILE):
        self.P = P
        self.nseq, self.ntile = nseq, ntile
        ntok = nseq * SEQ
        self.u_d = P.dram("uT", [D, ntok], BF16, "ExternalInput")
        self.w32 = P.dram("wsel", [D, WSEL], F32, "ExternalInput")
        self.wbf_d = P.dram("wsel_bf", [D, WSEL], BF16)
        self.pw_d = P.dram("poolw", [128, 64], F32, "ExternalInput")
        self.mc_d = P.dram("mc", [128, NMC], F32, "ExternalInput")
        self.invc_d = P.dram("invc", [128, T], F32, "ExternalInput")
        self.cf_d = P.dram("cf", [128, 4 * 128], F32, "ExternalInput")
        self.cb_d = P.dram("cb", [128, 3 * 128 + 4 * T], BF16, "ExternalInput")
        self.out_d = P.dram("mixo", [256, ntok], BF16, "ExternalOutput")
        self.t_out = Tk("mixo")
        self.twd = [Tk() for _ in range(KD)]
        A = P.alloc
        self.wsb, self.twsb = A("wsb", KD * WSEL, BF16)
        self.ub = [A("ub%d" % i, KD * T, BF16) for i in range(2)]
        self.QT, self.tQT = A("QT", SEQ, BF16)
        self.KT, self.tKT = A("KT", SEQ, BF16)
        self.V, self.tV = A("V", 64 * 64, BF16)
        self.mc, self.tmc = A("mc", NMC)
        self.invc, self.tinvc = A("invc", T)
        self.cf, self.tcf = A("cf", 4 * 128)
        self.cb, self.tcb = A("cb", 3 * 128 + 4 * T, BF16)
        self.pw32, self.tpw32 = A("pw32", 64)
        self.pwb, self.tpwb = A("pwb", 64, BF16)
        self.Abc, self.tAbc = A("Abc", 8)
        self.ve, self.tve = A("ve", 15 + T)
        self.s = [A("s%d" % i, 15 + T) for i in range(4)]
        self.res, self.tres = A("res", T)
        self.pdiff, self.tpdiff = A("pdiff", T, BF16)
        self.po, self.tpo = A("po", T, BF16)
        self.xe = [A("xe%d" % i, 3 + T) for i in range(3)]
        self.acc = [A("acc%d" % i, T) for i in range(3)]
        self.xc, self.txc = A("xc", T)
        self.BTb, self.tBTb = A("BTb", T, BF16)
        self.CTf, self.tCTf = A("CTf", T)
        self.CTb, self.tCTb = A("CTb", T, BF16)
        self.sz, self.tsz = A("sz", T)
        self.dtr, self.tdtr = A("dtr", 8)
        self.dx, self.tdx = A("dx", 8)
        self.dax, self.tdax = A("dax", 8)
        self.dt, self.tdt = A("dt", 8)
        self.aa, self.taa = A("aa", 8)
        self.abc = [A("abc%d" % i, 128) for i in range(2)]
        self.nacs, self.tnacs = A("nacs", 2)
        self.d2, self.td2 = A("d2", 2)
        self.w2, self.tw2 = A("w2", 2)
        self.dtw, self.tdtw = A("dtw", 2)
        self.E = [A("E%d" % i, 128) for i in range(2)]
        self.Dm = [A("Dm%d" % i, 128) for i in range(2)]
        self.M = [A("M%d" % i, 128, BF16) for i in range(2)]
        self.Cs = [A("Cs%d" % i, 128, BF16) for i in range(2)]
        self.xdtp = [A("xdtp%d" % i, 128, BF16) for i in range(2)]
        self.xdtw, self.txdtw = A("xdtw", 128, BF16)
        self.Btok, self.tBtok = A("Btok", 128, BF16)
        self.S, self.tS = A("S", 128)
        self.Sbp = [A("Sbp%d" % i, 128, BF16) for i in range(2)]
        self.yt, self.tyt = A("yt", T)
        self.yg, self.tyg = A("yg", T, BF16)
        self.ez = [A("ez%d" % i, T) for i in range(2)]
        self.L = [A("L%d" % i, T, BF16) for i in range(2)]
        self.W = [A("W%d" % i, T, BF16) for i in range(2)]
        self.Lsum, self.tLsum = A("Lsum", T, BF16)
        self.ob, self.tob = A("ob", T, BF16)
        self.blk = 0

    def setup(self):
        P = self.P
        for k in range(KD):
            P.dma("pool", self.wbf_d[k * 128:(k + 1) * 128, :], self.w32[k * 128:(k + 1) * 128, :], [], [self.twd[k]])
        P.dma("sp", v3(self.wsb, KD), self.wbf_d.rearrange("(k p) f -> p k f", p=128), self.twd, [self.twsb])
        P.dma("sp", self.mc, self.mc_d, [], [self.tmc])
        P.dma("sp", self.invc, self.invc_d, [], [self.tinvc])
        P.dma("sp", self.cf, self.cf_d, [], [self.tcf])
        P.dma("sp", self.cb, self.cb_d, [], [self.tcb])
        P.dma("sp", self.pw32, self.pw_d, [], [self.tpw32])
        P.op("act", lambda e: e.copy(self.pwb, self.pw32), [self.tpw32], [self.tpwb])
        P.op("act", lambda e: e.activation(self.Abc, self.mc[:, 30:38], AF.Exp), [self.tmc], [self.tAbc])
        P.op("dve", lambda e: e.tensor_scalar(self.Abc, self.Abc, -1.0, None, ALU.mult), [self.tAbc], [self.tAbc])
        for i in range(2):
            x, t = self.xdtp[i]
            P.op("pool", lambda e, x=x: e.memset(x, 0.0), [], [t])
        self.triu = self.cf[:, 0:128]
        self.identf = self.cf[:, 128:256]
        self.smask = self.cf[:, 256:384]
        self.onesf = self.cf[:, 384:512]
        self.identb = self.cb[:, 0:128]
        self.ntril = self.cb[:, 128:256]
        self.nones = self.cb[:, 256:384]
        self.amask = [self.cb[:, 384 + j * T:384 + (j + 1) * T] for j in range(4)]

    def tile(self, b, i):
        P = self.P
        it = b * self.ntile + i
        tok0 = b * SEQ + i * T
        ub, tub = self.ub[it % 2]
        ub3 = v3(ub, KD)
        wsb3 = v3(self.wsb, KD)
        P.dma("sp", ub3, self.u_d.rearrange("(k p) t -> p k t", p=128)[:, :, tok0:tok0 + T], [], [tub])
        first = (i == 0)
        if first:
            P.op("pool", lambda e: e.memset(self.ve[:, 0:15], 0.0), [], [self.tve])
            for g in range(3):
                xe, txe = self.xe[g]
                P.op("pool", lambda e, xe=xe: e.memset(xe[:, 0:3], 0.0), [], [txe])
            P.op("pool", lambda e: e.memset(self.S, 0.0), [], [self.tS])
            for h in range(2):
                sb, tsb = self.Sbp[h]
                P.op("pool", lambda e, sb=sb: e.memset(sb, 0.0), [], [tsb])
        pb = [0]

        def bank():
            bnk = pb[0] % 4
            pb[0] += 1
            return bnk

        def fm(c0, ncols):
            bnk = bank()
            ps = P.psum[bnk]
            for k in range(KD):
                P.op("pe", lambda e, k=k, ps=ps: e.matmul(ps[0:ncols, :], wsb3[:, k, c0:c0 + ncols], ub3[:, k, :],
                                                           start=(k == 0), stop=(k == KD - 1)),
                     [self.twsb, tub], [P.pbank[bnk]])
            return ps, P.pbank[bnk]

        ps, tp = fm(C_POOL, 128)
        P.op("act", lambda e, ps=ps: e.copy(self.ve[:, 15:15 + T], ps[:, :]), [tp], [self.tve])
        ps, tp = fm(C_Z, 128)
        P.op("act", lambda e, ps=ps: e.activation(self.sz, ps[:, :], AF.Silu), [tp], [self.tsz])
        for g, c0 in enumerate((C_X, C_B, C_C)):
            ps, tp = fm(c0, 128)
            xe, txe = self.xe[g]
            P.op("dve", lambda e, ps=ps, xe=xe: e.tensor_copy(xe[:, 3:3 + T], ps[:, :]), [tp], [txe])
        ps, tp = fm(C_Q, 64)
        P.op("act", lambda e, ps=ps: e.mul(self.QT[0:64, i * T:(i + 1) * T], ps[0:64, :], 0.125), [tp], [self.tQT])
        ps, tp = fm(C_K, 64)
        P.op("act", lambda e, ps=ps: e.copy(self.KT[0:64, i * T:(i + 1) * T], ps[0:64, :]), [tp], [self.tKT])
        bnk = bank()
        ps = P.psum[bnk]
        for j in range(4):
            for k in range(KD):
                P.op("pe", lambda e, k=k, j=j, ps=ps: e.matmul(ps[:, j * 66:(j + 1) * 66], ub3[:, k, j * 128:(j + 1) * 128],
                                                              wsb3[:, k, C_V:C_V + 66], start=(k == 0), stop=(k == KD - 1)),
                     [self.twsb, tub], [P.pbank[bnk]])
        V3 = self.V.rearrange("p (n d) -> p n d", d=64)
        ps3 = ps[:, 0:264].rearrange("p (j c) -> p j c", c=66)
        P.op("act", lambda e, ps3=ps3: e.copy(V3[:, i * 4:(i + 1) * 4, :], ps3[:, :, 0:64]), [P.pbank[bnk]], [self.tV])
        P.op("dve", lambda e, ps3=ps3: e.tensor_copy(self.dtr.rearrange("p (j c) -> p j c", c=2), ps3[:, :, 64:66]),
             [P.pbank[bnk]], [self.tdtr])
        ve = self.ve
        sh = [1, 2, 4, 8]
        lo = [1, 3, 7, 15]
        prev, tprev = ve, self.tve
        for q in range(4):
            s, ts = self.s[q]
            P.op("pool", lambda e, s=s, prev=prev, q=q: e.tensor_tensor(
                s[:, lo[q]:15 + T], prev[:, lo[q]:15 + T], prev[:, lo[q] - sh[q]:15 + T - sh[q]], ALU.add),
                [tprev], [ts])
            prev, tprev = s, ts
        s0, ts0 = self.s[0]
        P.op("dve", lambda e: e.tensor_scalar(self.res, s0[:, 15:15 + T], self.mc[:, 16:17], None, ALU.mult),
             [ts0, self.tmc], [self.tres])
        for q in range(1, 4):
            s, ts = self.s[q]
            P.op("dve", lambda e, s=s, q=q: e.scalar_tensor_tensor(self.res, s[:, 15:15 + T], self.mc[:, 16 + q:17 + q],
                                                                    self.res, ALU.mult, ALU.add),
                 [ts, self.tmc, self.tres], [self.tres])
        if first:
            P.op("dve", lambda e: e.tensor_tensor(self.res, self.res, self.invc, ALU.mult), [self.tres, self.tinvc], [self.tres])
            P.op("dve", lambda e: e.tensor_tensor(self.pdiff, self.res, ve[:, 15:15 + T], ALU.subtract),
                 [self.tres, self.tve], [self.tpdiff])
        else:
            P.op("dve", lambda e: e.scalar_tensor_tensor(self.pdiff, self.res, self.mc[:, 20:21], ve[:, 15:15 + T],
                                                          ALU.mult, ALU.subtract),
                 [self.tres, self.tmc, self.tve], [self.tpdiff])
        P.op("pool", lambda e: e.tensor_copy(ve[:, 0:15], ve[:, T:T + 15]), [self.tve], [self.tve])
        bnk = bank()
        ps = P.psum[bnk]
        P.op("pe", lambda e, ps=ps: e.matmul(ps[0:64, :], self.pwb, self.pdiff, start=True, stop=True),
             [self.tpwb, self.tpdiff], [P.pbank[bnk]])
        P.op("dve", lambda e, ps=ps: e.tensor_scalar(self.po[0:64, :], ps[0:64, :], self.mc[0:64, 15:16], None, ALU.mult),
             [P.pbank[bnk], self.tmc], [self.tpo])
        P.dma("act", self.out_d[0:64, tok0:tok0 + T], self.po[0:64, :], [self.tpo], [self.t_out])
        for g in range(3):
            xe, txe = self.xe[g]
            acc, tacc = self.acc[g]
            P.op("dve", lambda e, xe=xe, acc=acc, g=g: e.tensor_scalar(
                acc, xe[:, 3:3 + T], self.mc[:, 4 * g + 3:4 * g + 4], self.mc[:, 12 + g:13 + g], ALU.mult, ALU.add),
                [txe, self.tmc], [tacc])
            for kk in (2, 1, 0):
                P.op("dve", lambda e, xe=xe, acc=acc, g=g, kk=kk: e.scalar_tensor_tensor(
                    acc, xe[:, kk:kk + T], self.mc[:, 4 * g + kk:4 * g + kk + 1], acc, ALU.mult, ALU.add),
                    [txe, self.tmc, tacc], [tacc])
            P.op("pool", lambda e, xe=xe: e.tensor_copy(xe[:, 0:3], xe[:, T:T + 3]), [txe], [txe])
        P.op("act", lambda e: e.activation(self.xc, self.acc[0][0], AF.Silu), [self.acc[0][1]], [self.txc])
        P.op("act", lambda e: e.activation(self.BTb, self.acc[1][0], AF.Silu), [self.acc[1][1]], [self.tBTb])
        P.op("act", lambda e: e.activation(self.CTf, self.acc[2][0], AF.Silu), [self.acc[2][1]], [self.tCTf])
        P.op("pool", lambda e: e.tensor_copy(self.CTb, self.CTf), [self.tCTf], [self.tCTb])
        P.op("dve", lambda e: e.tensor_tensor(self.dx, self.dtr, self.mc[:, 22:30], ALU.add), [self.tdtr, self.tmc], [self.tdx])
        P.op("dve", lambda e: e.scalar_tensor_tensor(self.dax, self.dx, -1.0, self.dx, ALU.mult, ALU.max), [self.tdx], [self.tdax])
        P.op("act", lambda e: e.activation(self.dax, self.dax, AF.Exp, scale=-1.0), [self.tdax], [self.tdax])
        P.op("act", lambda e: e.activation(self.dax, self.dax, AF.Ln, bias=1.0), [self.tdax], [self.tdax])
        P.op("dve", lambda e: e.scalar_tensor_tensor(self.dt, self.dx, 0.0, self.dax, ALU.max, ALU.add),
             [self.tdx, self.tdax], [self.tdt])
        P.op("dve", lambda e: e.tensor_tensor(self.aa, self.dt, self.Abc, ALU.mult), [self.tdt, self.tAbc], [self.taa])
        b4, b5, b6 = P.psum[4], P.psum[5], P.psum[6]
        t4, t5, t6 = P.pbank[4], P.pbank[5], P.pbank[6]
        for ci in range(4):
            self.ssd_chunk(b, i, ci)
        self.post(b, i, tok0)

    def ssd_chunk(self, b, i, ci):
        P = self.P
        b4, b5, b6 = P.psum[4], P.psum[5], P.psum[6]
        t4, t5, t6 = P.pbank[4], P.pbank[5], P.pbank[6]
        if True:
            c0 = ci * 128
            for h in range(2):
                abc, tabc = self.abc[h]
                P.op("pool", lambda e, abc=abc, h=h: e.tensor_scalar(
                    abc, self.onesf, self.aa[:, ci * 2 + h:ci * 2 + h + 1], None, ALU.mult),
                    [self.tcf, self.taa], [tabc])
            for h in range(2):
                abc, tabc = self.abc[h]
                P.op("pe", lambda e, abc=abc, h=h: e.matmul(b4[:, h * 128:(h + 1) * 128], abc, self.triu, start=True, stop=True),
                     [tabc, self.tcf], [t4])
            for h in range(2):
                abc, tabc = self.abc[h]
                P.op("pe", lambda e, abc=abc, h=h: e.matmul(b4[:, 256 + h * 128:256 + (h + 1) * 128], abc, self.triu,
                                                            start=True, stop=False), [tabc, self.tcf], [t4])
                P.op("pe", lambda e, h=h: e.matmul(b4[:, 256 + h * 128:256 + (h + 1) * 128], self.identf, self.smask,
                                                   start=False, stop=True), [self.tcf], [t4])
            P.op("pe", lambda e: e.matmul(b5[:, 256:258], self.triu, self.aa[:, ci * 2:ci * 2 + 2], start=True, stop=True),
                 [self.tcf, self.taa], [t5])
            P.op("dve", lambda e: e.tensor_scalar(self.nacs, b5[:, 256:258], -1.0, None, ALU.mult), [t5], [self.tnacs])
            for h in range(2):
                E, tE = self.E[h]
                Dm, tDm = self.Dm[h]
                P.op("act", lambda e, E=E, h=h: e.activation(E, b4[:, h * 128:(h + 1) * 128], AF.Exp), [t4], [tE])
                P.op("act", lambda e, Dm=Dm, h=h: e.activation(Dm, b4[:, 256 + h * 128:256 + (h + 1) * 128], AF.Exp,
                                                              bias=self.nacs[:, h:h + 1]), [t4, self.tnacs], [tDm])
                P.op("dve", lambda e, h=h: e.tensor_tensor(self.d2[:, h:h + 1], b4[:, h * 128 + 127:h * 128 + 128],
                                                          self.nacs[:, h:h + 1], ALU.add), [t4, self.tnacs], [self.td2])
            P.op("act", lambda e: e.activation(self.w2, self.d2, AF.Exp), [self.td2], [self.tw2])
            P.op("dve", lambda e: e.tensor_tensor(self.dtw, self.dt[:, ci * 2:ci * 2 + 2], self.w2, ALU.mult),
                 [self.tdt, self.tw2], [self.tdtw])
            P.op("pe", lambda e: e.matmul(b5[:, 0:128], self.BTb[:, c0:c0 + 128], self.CTb[:, c0:c0 + 128], start=True, stop=True),
                 [self.tBTb, self.tCTb], [t5])
            P.op("pe", lambda e: e.transpose(b5[:, 128:256], self.xc[:, c0:c0 + 128], self.identf), [self.txc, self.tcf], [t5])
            btp = b5[:, 392:456].bitcast(BF16)
            P.op("pe", lambda e: e.transpose(btp, self.BTb[:, c0:c0 + 128], self.identb), [self.tBTb, self.tcb], [t5])
            for h in range(2):
                M, tM = self.M[h]
                Dm, tDm = self.Dm[h]
                E, tE = self.E[h]
                Cs, tCs = self.Cs[h]
                xp, txp = self.xdtp[h]
                P.op("dve", lambda e, M=M, Dm=Dm: e.tensor_tensor(M, b5[:, 0:128], Dm, ALU.mult), [t5, tDm], [tM])
                P.op("pool", lambda e, Cs=Cs, E=E: e.tensor_tensor(Cs, self.CTf[:, c0:c0 + 128], E, ALU.mult),
                     [self.tCTf, tE], [tCs])
                P.op("dve", lambda e, xp=xp, h=h: e.tensor_scalar(
                    xp[:, h * 64:(h + 1) * 64], b5[:, 128 + h * 64:128 + (h + 1) * 64],
                    self.dt[:, ci * 2 + h:ci * 2 + h + 1], None, ALU.mult), [t5, self.tdt], [txp])
                P.op("dve", lambda e, h=h: e.tensor_scalar(
                    self.xdtw[:, h * 64:(h + 1) * 64], b5[:, 128 + h * 64:128 + (h + 1) * 64],
                    self.dtw[:, h:h + 1], None, ALU.mult), [t5, self.tdtw], [self.txdtw])
            P.op("act", lambda e: e.copy(self.Btok, btp), [t5], [self.tBtok])
            if b == 0 and i == 0 and ci == 0:
                P.dump("dt", self.dt, self.tdt); P.dump("aa", self.aa, self.taa); P.dump("nacs", self.nacs, self.tnacs)
                P.dump("E0", self.E[0][0], self.E[0][1]); P.dump("Dm0", self.Dm[0][0], self.Dm[0][1])
                P.dump("M0", self.M[0][0], self.M[0][1], BF16); P.dump("Cs0", self.Cs[0][0], self.Cs[0][1], BF16)
                P.dump("xdtp0", self.xdtp[0][0], self.xdtp[0][1], BF16); P.dump("xdtp1", self.xdtp[1][0], self.xdtp[1][1], BF16)
                P.dump("xc", self.xc, self.txc); P.dump("BTb", self.BTb, self.tBTb, BF16); P.dump("CTf", self.CTf, self.tCTf)
                P.dump("sz", self.sz, self.tsz); P.dump("dtw", self.dtw, self.tdtw); P.dump("Btok", self.Btok, self.tBtok, BF16)
            seqm = [(self.xdtp[0], self.M[0]), (self.xdtp[1], self.M[1]), (self.Sbp[0], self.Cs[0]), (self.Sbp[1], self.Cs[1])]
            for n, ((l, tl), (r, tr)) in enumerate(seqm):
                P.op("pe", lambda e, l=l, r=r, n=n: e.matmul(b6[:, c0:c0 + 128], l, r, start=(n == 0), stop=(n == 3)),
                     [tl, tr], [t6])
            P.op("pe", lambda e: e.matmul(b5[:, 264:392], self.Btok, self.xdtw, start=True, stop=True),
                 [self.tBtok, self.txdtw], [t5])
            for h in range(2):
                E, tE = self.E[h]
                sb, tsb = self.Sbp[h]
                P.op("dve", lambda e, E=E, h=h: e.scalar_tensor_tensor(
                    self.S[:, h * 64:(h + 1) * 64], self.S[:, h * 64:(h + 1) * 64], E[:, 127:128],
                    b5[:, 264 + h * 64:264 + (h + 1) * 64], ALU.mult, ALU.add), [self.tS, tE, t5], [self.tS])
                P.op("pool", lambda e, sb=sb, h=h: e.tensor_copy(sb[:, h * 64:(h + 1) * 64], self.S[:, h * 64:(h + 1) * 64]),
                     [self.tS], [tsb])
    def post(self, b, i, tok0):
        P = self.P
        b6, t6 = P.psum[6], P.pbank[6]
        P.op("dve", lambda e: e.scalar_tensor_tensor(self.yt, self.xc, self.mc[:, 21:22], b6[:, :], ALU.mult, ALU.add),
             [self.txc, self.tmc, t6], [self.tyt])
        if b == 0 and i == 0:
            P.dump("yt", self.yt, self.tyt); P.dump("S", self.S, self.tS)
        P.op("pool", lambda e: e.tensor_tensor(self.yg, self.yt, self.sz, ALU.mult), [self.tyt, self.tsz], [self.tyg])
        P.dma("act", self.out_d[64:192, tok0:tok0 + T], self.yg, [self.tyg], [self.t_out])
        b7, t7 = P.psum[7], P.pbank[7]
        nblk = 4 * i + 4
        qs = self.QT[0:64, i * T:(i + 1) * T]
        for n in range(nblk):
            kb = nblk - 1 - n
            j = kb - 4 * i
            diag = j >= 0
            ks = self.KT[0:64, kb * 128:(kb + 1) * 128]
            zb = self.blk % 2
            ab = 2 + self.blk % 2
            ez, tez = self.ez[self.blk % 2]
            L, tL = self.L[self.blk % 2]
            W, tW = self.W[self.blk % 2]
            self.blk += 1
            pz, tz = P.psum[zb], P.pbank[zb]
            pa, ta = P.psum[ab], P.pbank[ab]
            P.op("pe", lambda e, pz=pz, ks=ks, diag=diag: e.matmul(pz[:, :], ks, qs, start=True, stop=not diag),
                 [self.tKT, self.tQT], [tz])
            if diag:
                P.op("pe", lambda e, pz=pz, j=j: e.matmul(pz[:, :], self.identb, self.amask[j], start=False, stop=True),
                     [self.tcb], [tz])
            P.op("act", lambda e, ez=ez, pz=pz: e.activation(ez, pz[:, :], AF.Exp), [tz], [tez])
            P.op("act", lambda e, ez=ez, L=L: e.activation(L, ez, AF.Ln, bias=1.0), [tez], [tL])
            P.op("pe", lambda e, pa=pa, ks=ks: e.matmul(pa[:, :], ks, qs, start=True, stop=False), [self.tKT, self.tQT], [ta])
            if diag:
                P.op("pe", lambda e, pa=pa, j=j: e.matmul(pa[:, :], self.identb, self.amask[j], start=False, stop=False),
                     [self.tcb], [ta])
            P.op("pe", lambda e, pa=pa, L=L, n=n: e.matmul(pa[:, :], self.ntril, L, start=False, stop=(n == 0)),
                 [self.tcb, tL], [ta])
            if n > 0:
                P.op("pe", lambda e, pa=pa: e.matmul(pa[:, :], self.nones, self.Lsum, start=False, stop=True),
                     [self.tcb, self.tLsum], [ta])
            P.op("act", lambda e, W=W, pa=pa: e.activation(W, pa[:, :], AF.Exp), [ta], [tW])
            if kb > 0:
                if n == 0:
                    P.op("pool", lambda e, L=L: e.tensor_copy(self.Lsum, L), [tL], [self.tLsum])
                else:
                    P.op("pool", lambda e, L=L: e.tensor_tensor(self.Lsum, self.Lsum, L, ALU.add), [tL, self.tLsum], [self.tLsum])
            V3 = self.V.rearrange("p (n d) -> p n d", d=64)
            P.op("pe", lambda e, W=W, kb=kb, n=n: e.matmul(b7[0:64, :], V3[:, kb, :], W, start=(n == 0), stop=(n == nblk - 1)),
                 [self.tV, tW], [t7])
        P.op("act", lambda e: e.copy(self.ob[0:64, :], b7[0:64, :]), [t7], [self.tob])
        P.dma("act", self.out_d[192:256, tok0:tok0 + T], self.ob[0:64, :], [self.tob], [self.t_out])

    def emit(self):
        self.setup()
        for b in range(self.nseq):
            for i in range(self.ntile):
                self.tile(b, i)


def colsT(v):
    return np.ascontiguousarray(np.asarray(v, np.float32).reshape(-1, 128).T)


_CONST = {}


def mixer_consts():
    if "cf" not in _CONST:
        k = np.arange(128)
        triu = (k[:, None] <= k[None, :]).astype(np.float32)
        ident = np.eye(128, dtype=np.float32)
        smask = np.where(k[:, None] > k[None, :], NEG, 0.0).astype(np.float32)
        ones = np.ones((128, 128), np.float32)
        _CONST["cf"] = np.concatenate([triu, ident, smask, ones], 1)
        ntril = -(k[:, None] >= k[None, :]).astype(np.float32)
        t = np.arange(T)
        am = [np.where(128 * j + k[:, None] >= t[None, :], NEG, 0.0).astype(np.float32) for j in range(4)]
        _CONST["cb"] = np.concatenate([ident, ntril, -ones] + am, 1).astype(ml_dtypes.bfloat16)
    return _CONST["cf"], _CONST["cb"]


def mixer_inputs(c, w_in, pool_w, pool_scale, conv_w, conv_b, dt_bias, a_log, d_skip):
    g = c // 2
    bc = c // 4
    XB = 1536
    colsel = np.concatenate([
        np.arange(128 * g, 128 * g + 128),
        np.arange(512 + 128 * c, 512 + 128 * c + 128),
        np.arange(XB + 128 * c, XB + 128 * c + 128),
        np.arange(XB + 1024 + 128 * bc, XB + 1024 + 128 * bc + 128),
        np.arange(XB + 1280 + 128 * bc, XB + 1280 + 128 * bc + 128),
        np.arange(3088 + 64 * c, 3088 + 64 * c + 64),
        np.arange(3600 + 64 * c, 3600 + 64 * c + 64),
        np.arange(4112 + 64 * c, 4112 + 64 * c + 64),
        np.arange(3072 + 2 * c, 3072 + 2 * c + 2),
    ])
    wsel = np.ascontiguousarray(w_in[:, colsel])
    poolw = np.ascontiguousarray(pool_w[g][:, 64 * (c % 2):64 * (c % 2) + 64])
    mc = np.zeros((128, NMC), np.float32)
    chx = np.arange(128 * c, 128 * c + 128)
    chB = np.arange(1024 + 128 * bc, 1024 + 128 * bc + 128)
    chC = np.arange(1280 + 128 * bc, 1280 + 128 * bc + 128)
    for gi, ch in enumerate((chx, chB, chC)):
        for kk in range(4):
            mc[:, 4 * gi + kk] = conv_w[kk, ch]
        mc[:, 12 + gi] = conv_b[ch]
    mc[0:64, 15] = pool_scale[128 * g + 64 * (c % 2):128 * g + 64 * (c % 2) + 64]
    mc[:, 16 + g] = 1.0
    w = 2 ** (g + 1)
    mc[:, 20] = 1.0 / w
    mc[0:64, 21] = d_skip[2 * c]
    mc[64:128, 21] = d_skip[2 * c + 1]
    for j in range(4):
        for h in range(2):
            mc[:, 22 + 2 * j + h] = dt_bias[2 * c + h]
            mc[:, 30 + 2 * j + h] = a_log[2 * c + h]
    invc = np.broadcast_to(1.0 / np.minimum(np.arange(1, T + 1), w).astype(np.float32), (128, T)).copy()
    cf, cb = mixer_consts()
    return {"wsel": wsel, "poolw": poolw, "mc": mc, "invc": invc, "cf": cf, "cb": cb}


_PROGS = {}


def get_chain(n_ffn, has_mix, epi):
    key = ("chain", n_ffn, has_mix, epi)
    if key not in _PROGS:
        P = Prog()
        Chain(P, n_ffn, has_mix, epi).emit()
        _PROGS[key] = P.finish()
    return _PROGS[key]


def get_mixer():
    key = ("mixer",)
    if key not in _PROGS:
        P = Prog()
        Mixer(P).emit()
        _PROGS[key] = P.finish()
    return _PROGS[key]


NCORE = 8


def kernel(x, ffn1_norm, ffn1_w_gate, ffn1_w_up, ffn1_w_down, mix_norm, w_in, pool_w, pool_scale,
           conv_w, conv_b, dt_bias, a_log, d_skip, ssd_norm, w_out, ffn2_norm, ffn2_w_gate,
           ffn2_w_up, ffn2_w_down, final_norm):
    f = lambda a: np.asarray(a, dtype=np.float32)
    x = f(x)
    depth = w_in.shape[0]
    xt = x.reshape(-1, D)
    cores = list(range(NCORE))
    z8 = np.zeros((128, 8), np.float32)
    nc = get_chain(1, False, "u")
    cv = np.concatenate([colsT(f(ffn1_norm[0])), colsT(f(mix_norm[0])), z8], 1)
    maps = []
    for c in cores:
        maps.append({"h_in": np.ascontiguousarray(xt[c * NT:(c + 1) * NT].T), "cvec": cv,
                     "wg0": f(ffn1_w_gate[0]), "wu0": f(ffn1_w_up[0]), "wd0": f(ffn1_w_down[0])})
    res = run_bass_kernel_spmd(nc, maps, core_ids=cores)
    h = [res.results[c]["h_out"] for c in cores]
    u = [res.results[c]["u_out"] for c in cores]
    out = None
    for l in range(depth):
        uT = np.ascontiguousarray(np.concatenate([np.asarray(a) for a in u], axis=1))
        nc = get_mixer()
        maps = []
        for c in cores:
            m = mixer_inputs(c, f(w_in[l]), f(pool_w[l]), f(pool_scale[l]), f(conv_w[l]), f(conv_b[l]),
                             f(dt_bias[l]), f(a_log[l]), f(d_skip[l]))
            m["uT"] = uT
            maps.append(m)
        res = run_bass_kernel_spmd(nc, maps, core_ids=cores)
        mixT = np.empty((D, NCORE * NT), dtype=ml_dtypes.bfloat16)
        for c in cores:
            mo = np.asarray(res.results[c]["mixo"])
            mixT[64 * c:64 * c + 64] = mo[0:64]
            mixT[512 + 128 * c:512 + 128 * c + 128] = mo[64:192]
            mixT[1536 + 64 * c:1536 + 64 * c + 64] = mo[192:256]
        last = (l == depth - 1)
        if not last:
            nc = get_chain(2, True, "u")
            cv = np.concatenate([colsT(f(ffn2_norm[l])), colsT(f(ffn1_norm[l + 1])), colsT(f(mix_norm[l + 1])),
                                 colsT(f(ssd_norm[l]))], 1)
        else:
            nc = get_chain(1, True, "final")
            cv = np.concatenate([colsT(f(ffn2_norm[l])), colsT(f(final_norm)), colsT(f(ssd_norm[l]))], 1)
        maps = []
        for c in cores:
            m = {"h_in": h[c], "cvec": cv, "mixT": np.ascontiguousarray(mixT[:, c * NT:(c + 1) * NT]),
                 "wout": f(w_out[l]),
                 "wg0": f(ffn2_w_gate[l]), "wu0": f(ffn2_w_up[l]), "wd0": f(ffn2_w_down[l])}
            if not last:
                m.update({"wg1": f(ffn1_w_gate[l + 1]), "wu1": f(ffn1_w_up[l + 1]), "wd1": f(ffn1_w_down[l + 1])})
            maps.append(m)
        res = run_bass_kernel_spmd(nc, maps, core_ids=cores)
        if not last:
            h = [res.results[c]["h_out"] for c in cores]
            u = [res.results[c]["u_out"] for c in cores]
        else:
            out = np.concatenate([np.asarray(res.results[c]["o_out"]).T for c in cores], axis=0)
    return np.ascontiguousarray(out.reshape(x.shape).astype(np.float32))
```

```python
import numpy as np
import ml_dtypes
from contextlib import ExitStack
import concourse.bass as bass
import concourse.mybir as mybir
from concourse.bass_utils import run_bass_kernel_spmd

F32 = mybir.dt.float32
BF16 = mybir.dt.bfloat16
AF = mybir.ActivationFunctionType
ALU = mybir.AluOpType
AX = mybir.AxisListType

ENG = ("pe", "act", "dve", "pool", "sp")
NDQ = 8
SAME_ENGINE_SYNC = True


class Tk:
    __slots__ = ("name", "w", "r", "excl")

    def __init__(self, name="", excl=False):
        self.name = name
        self.w = None
        self.r = {}
        self.excl = excl


class Prog:
    def __init__(self, arena_f32=49152):
        self.nc = bass.Bass("TRN2", target_bir_lowering=False)
        self.es = ExitStack()
        self.ops = {e: [] for e in ENG}
        self.cnt = {e: 0 for e in ENG}
        self.dcnt = {}
        self.dnext = {q: 0 for q in ("sp", "act", "pool")}
        self.seen = {e: {} for e in ENG}
        self.sems = {}
        nc = self.nc
        for e in ENG:
            self.sems[e] = self.es.enter_context(nc.semaphore("s_" + e))
        for q in ("sp", "act", "pool"):
            for j in range(NDQ):
                k = "d_%s_%d" % (q, j)
                self.sems[k] = self.es.enter_context(nc.semaphore(k))
                self.dcnt[k] = 0
        self.arena = self.es.enter_context(nc.sbuf_tensor("arena", [128, arena_f32], F32))
        self.arena_n = arena_f32
        self.aoff = 0
        self.psum = []
        self.pbank = []
        for i in range(8):
            t = self.es.enter_context(nc.psum_tensor("ps%d" % i, [128, 512], F32))
            self.psum.append(t)
            self.pbank.append(Tk("ps%d" % i, excl=True))
        self.n_inst = 0

    def reset_arena(self, keep=0):
        self.aoff = keep

    def alloc(self, name, cols, dtype=F32):
        nf = cols if dtype == F32 else (cols + 1) // 2
        nf = (nf + 7) // 8 * 8
        assert self.aoff + nf <= self.arena_n, ("arena overflow", name, self.aoff, nf)
        ap = self.arena[:, self.aoff:self.aoff + nf]
        self.aoff += nf
        if dtype != F32:
            ap = ap.bitcast(dtype)[:, 0:cols]
        else:
            ap = ap[:, 0:cols]
        return ap, Tk(name)

    def dram(self, name, shape, dtype, kind="Internal"):
        return self.nc.dram_tensor(name, list(shape), dtype, kind=kind).ap()

    def _waits(self, e, reads, writes):
        waits = {}

        def need(dep):
            if dep is None:
                return
            k, v = dep
            if k == e and (e == "pe" or not SAME_ENGINE_SYNC):
                return
            if waits.get(k, 0) < v:
                waits[k] = v

        for t in reads:
            need(t.w)
            if t.excl:
                for k, v in t.r.items():
                    need((k, v))
        for t in writes:
            need(t.w)
            for k, v in t.r.items():
                need((k, v))
        wl = []
        for k, v in waits.items():
            if self.seen[e].get(k, 0) < v:
                self.seen[e][k] = v
                wl.append((k, v))
        return wl

    def op(self, e, fn, reads=(), writes=()):
        wl = self._waits(e, reads, writes)
        self.cnt[e] += 1
        c = self.cnt[e]
        sems = self.sems
        semE = sems[e]

        def emit(eng):
            for k, v in wl:
                eng.wait_ge(sems[k], v)
            fn(eng).then_inc(semE, 1)

        self.ops[e].append(emit)
        self.n_inst += 1 + len(wl)
        for t in reads:
            if t.excl:
                t.w = (e, c)
                t.r = {}
            else:
                t.r[e] = c
        for t in writes:
            t.w = (e, c)
            t.r = {}

    def dma(self, q, out_ap, in_ap, reads=(), writes=()):
        wl = self._waits(q, reads, writes)
        j = self.dnext[q]
        self.dnext[q] = (j + 1) % NDQ
        key = "d_%s_%d" % (q, j)
        prev = self.dcnt[key]
        if prev > 0 and self.seen[q].get(key, 0) < prev:
            self.seen[q][key] = prev
            wl.append((key, prev))
        self.dcnt[key] = prev + 16
        v = prev + 16
        sems = self.sems

        def emit(eng):
            for k, vv in wl:
                eng.wait_ge(sems[k], vv)
            eng.dma_start(out=out_ap, in_=in_ap).then_inc(sems[key], 16)

        self.ops[q].append(emit)
        self.n_inst += 1 + len(wl)
        for t in reads:
            t.r[key] = v
        for t in writes:
            t.w = (key, v)
            t.r = {}

    def dump(self, name, ap, tk, dtype=F32):
        if not getattr(self, "debug", False):
            return
        d = self.dram("dbg_" + name, [ap.shape[0], ap.shape[1]], dtype, "ExternalOutput")
        self.dma("sp", d, ap, [tk], [Tk()])

    def barrier(self):
        cur = dict(self.cnt)
        cur.update(self.dcnt)
        sems = self.sems
        for e in ENG:
            wl = []
            for k, v in cur.items():
                if k != e and v > self.seen[e].get(k, 0):
                    self.seen[e][k] = v
                    wl.append((k, v))

            def emit(eng, wl=wl):
                for k, v in wl:
                    eng.wait_ge(sems[k], v)

            self.ops[e].append(emit)
            self.n_inst += len(wl)

    def finish(self):
        self.barrier()
        nc = self.nc
        ops = self.ops
        with nc.Block() as block:
            @block.tensor
            def _(eng):
                for f in ops["pe"]:
                    f(eng)

            @block.scalar
            def _(eng):
                for f in ops["act"]:
                    f(eng)

            @block.vector
            def _(eng):
                for f in ops["dve"]:
                    f(eng)

            @block.gpsimd
            def _(eng):
                for f in ops["pool"]:
                    f(eng)

            @block.sync
            def _(eng):
                for f in ops["sp"]:
                    f(eng)
        self.es.close()
        return nc


D = 2048
DFF = 5632
NT = 2048
T = 512
KD = D // 128
KF = DFF // 128
EPS = 1e-6
WB = 8192
NWB = 5


def v3(ap, k):
    return ap.rearrange("p (k t) -> p k t", k=k)


class Chain:
    def __init__(self, P, n_ffn, has_mix, epilogue):
        self.P = P
        nc = P.nc
        self.n_ffn, self.has_mix, self.epi = n_ffn, has_mix, epilogue
        self.h_in = P.dram("h_in", [D, NT], F32, "ExternalInput")
        self.t_hin = Tk("h_in")
        ncv = 16 * (n_ffn + 1) + 8
        self.ncv = ncv
        self.cv_d = P.dram("cvec", [128, ncv], F32, "ExternalInput")
        self.w32 = []
        self.wbf = []
        self.twb = []
        for i in range(n_ffn):
            for nm, shp in (("wg", [D, DFF]), ("wu", [D, DFF]), ("wd", [DFF, D])):
                self.w32.append(P.dram("%s%d" % (nm, i), shp, F32, "ExternalInput"))
                self.wbf.append(P.dram("%s%d_bf" % (nm, i), shp, BF16))
                self.twb.append([Tk() for _ in range(44)])
        if has_mix:
            self.mix_d = P.dram("mixT", [D, NT], BF16, "ExternalInput")
            self.wo32 = P.dram("wout", [D, D], F32, "ExternalInput")
            self.wobf = P.dram("wout_bf", [D, D], BF16)
            self.two = [Tk() for _ in range(KD)]
        if epilogue == "u":
            self.h_out = P.dram("h_out", [D, NT], F32, "ExternalOutput")
            self.u_out = P.dram("u_out", [D, NT], BF16, "ExternalOutput")
        else:
            self.o_out = P.dram("o_out", [D, NT], F32, "ExternalOutput")
        self.t_out = Tk("out")
        self.cv, self.tcv = P.alloc("cv", ncv)
        self.ones, self.tones = P.alloc("ones", 128, BF16)
        self.h, self.th = P.alloc("h", KD * T)
        self.u, self.tu = P.alloc("u", KD * T, BF16)
        self.act, self.tact = P.alloc("act", KF * T, BF16)
        self.sq = [P.alloc("sq%d" % i, T, BF16) for i in range(2)]
        self.sg = [P.alloc("sg%d" % i, T) for i in range(2)]
        self.rs, self.trs = P.alloc("rs", T)
        self.wb = [P.alloc("wb%d" % i, WB, BF16) for i in range(NWB)]
        self.wbi = 0
        self.tk_h = [Tk("h%d" % k) for k in range(KD)]
        self.tk_u = [Tk("u%d" % k) for k in range(KD)]
        self.tk_a = [Tk("a%d" % k) for k in range(KF)]
        self.alt = 0

    def nextwb(self):
        w = self.wb[self.wbi]
        self.wbi = (self.wbi + 1) % NWB
        return w

    def ew(self):
        self.alt ^= 1
        return "dve" if self.alt else "pool"

    def cast_weights(self):
        P = self.P
        P.op("pool", lambda e: e.memset(self.ones, 1.0), [], [self.tones])
        P.dma("sp", self.cv, self.cv_d, [], [self.tcv])
        if self.has_mix:
            for k in range(KD):
                P.dma("pool", self.wobf[k * 128:(k + 1) * 128, :], self.wo32[k * 128:(k + 1) * 128, :],
                      [], [self.two[k]])
        for fi in range(self.n_ffn):
            for fg in range(KF // 4):
                for mi in (3 * fi, 3 * fi + 1):
                    for rq in range(4):
                        P.dma("pool", self.wbf[mi][rq * 512:(rq + 1) * 512, fg * 512:(fg + 1) * 512],
                              self.w32[mi][rq * 512:(rq + 1) * 512, fg * 512:(fg + 1) * 512], [], [self.twb[mi][fg * 4 + rq]])
            mi = 3 * fi + 2
            for k in range(KF):
                P.dma("pool", self.wbf[mi][k * 128:(k + 1) * 128, :], self.w32[mi][k * 128:(k + 1) * 128, :],
                      [], [self.twb[mi][k]])

    def norm_stats(self, src3, tks, idxs, nfeat, bank):
        P = self.P
        ps, tps = P.psum[bank], P.pbank[bank]
        n = len(idxs)
        for i, k in enumerate(idxs):
            sq, tsq = self.sq[i % 2]
            P.op("act", lambda e, k=k, sq=sq: e.activation(sq, src3[:, k, :], AF.Square), [tks[k]], [tsq])
            P.op("pe", lambda e, i=i, sq=sq: e.matmul(ps[:, :], self.ones, sq, start=(i == 0), stop=(i == n - 1)),
                 [self.tones, tsq], [tps])
        P.op("act", lambda e: e.activation(self.rs, ps[:, :], AF.Sqrt, bias=EPS, scale=1.0 / nfeat), [tps], [self.trs])
        P.op("dve", lambda e: e.reciprocal(self.rs, self.rs), [self.trs], [self.trs])

    def rmsnorm(self, gcol0, dst3, tdst):
        P = self.P
        h3 = v3(self.h, KD)
        self.norm_stats(h3, self.tk_h, list(range(KD)), D, 4)
        for k in range(KD):
            P.op("dve", lambda e, k=k: e.scalar_tensor_tensor(
                dst3[:, k, :], h3[:, k, :], self.cv[:, gcol0 + k:gcol0 + k + 1], self.rs, ALU.mult, ALU.mult),
                [self.tk_h[k], self.tcv, self.trs], tdst[k] if isinstance(tdst[k], list) else [tdst[k]])

    def ffn(self, i):
        P = self.P
        h3 = v3(self.h, KD)
        u3 = v3(self.u, KD)
        a3 = v3(self.act, KF)
        wg, wu, wd = self.wbf[3 * i], self.wbf[3 * i + 1], self.wbf[3 * i + 2]
        twg, twu, twd = self.twb[3 * i], self.twb[3 * i + 1], self.twb[3 * i + 2]
        self.rmsnorm(16 * i, u3, self.tk_u)
        wg3 = wg.rearrange("(k p) f -> p k f", p=128)
        wu3 = wu.rearrange("(k p) f -> p k f", p=128)
        wd3 = wd.rearrange("(k p) f -> p k f", p=128)
        gi = 0
        for fg in range(KF // 4):
            (wa, twa), (wb_, twb_) = self.nextwb(), self.nextwb()
            wa3, wb3 = v3(wa, KD), v3(wb_, KD)
            P.dma("sp", wa3, wg3[:, :, fg * 512:(fg + 1) * 512], twg[fg * 4:fg * 4 + 4], [twa])
            P.dma("sp", wb3, wu3[:, :, fg * 512:(fg + 1) * 512], twu[fg * 4:fg * 4 + 4], [twb_])
            for f4 in range(4):
                f = fg * 4 + f4
                bg, bu = (0, 1) if gi % 2 == 0 else (2, 3)
                gi += 1
                pg, pu = P.psum[bg], P.psum[bu]
                for k in range(KD):
                    P.op("pe", lambda e, k=k, f4=f4, pg=pg, wa3=wa3: e.matmul(
                        pg[:, :], wa3[:, k, f4 * 128:(f4 + 1) * 128], u3[:, k, :], start=(k == 0), stop=(k == KD - 1)),
                        [twa, self.tk_u[k]], [P.pbank[bg]])
                for k in range(KD):
                    P.op("pe", lambda e, k=k, f4=f4, pu=pu, wb3=wb3: e.matmul(
                        pu[:, :], wb3[:, k, f4 * 128:(f4 + 1) * 128], u3[:, k, :], start=(k == 0), stop=(k == KD - 1)),
                        [twb_, self.tk_u[k]], [P.pbank[bu]])
                sg, tsg = self.sg[f % 2]
                P.op("act", lambda e, sg=sg, pg=pg: e.activation(sg, pg[:, :], AF.Silu), [P.pbank[bg]], [tsg])
                P.op("dve", lambda e, sg=sg, pu=pu, f=f: e.tensor_tensor(a3[:, f, :], sg, pu[:, :], ALU.mult),
                     [tsg, P.pbank[bu]], [self.tk_a[f]])
        FD = 11
        for dg in range(4):
            banks = [4, 5, 6, 7] if dg % 2 == 0 else [0, 1, 2, 3]
            for fgd in range(KF // FD):
                w, tw = self.nextwb()
                w3 = w[:, 0:FD * 512].rearrange("p (k t) -> p k t", k=FD)
                P.dma("sp", w3, wd3[:, fgd * FD:(fgd + 1) * FD, dg * 512:(dg + 1) * 512],
                      twd[fgd * FD:(fgd + 1) * FD], [tw])
                for j in range(4):
                    pb = P.psum[banks[j]]
                    for f in range(FD):
                        ff = fgd * FD + f
                        P.op("pe", lambda e, j=j, f=f, ff=ff, pb=pb, w3=w3: e.matmul(
                            pb[:, :], w3[:, f, j * 128:(j + 1) * 128], a3[:, ff, :],
                            start=(ff == 0), stop=(ff == KF - 1)),
                            [tw, self.tk_a[ff]], [P.pbank[banks[j]]])
            for j in range(4):
                c = dg * 4 + j
                pb = P.psum[banks[j]]
                P.op("dve", lambda e, c=c, pb=pb: e.scalar_tensor_tensor(
                    h3[:, c, :], pb[:, :], 0.5, h3[:, c, :], ALU.mult, ALU.add),
                    [P.pbank[banks[j]], self.tk_h[c]], [self.tk_h[c]])

    def mix_stage(self, t0):
        P = self.P
        h3 = v3(self.h, KD)
        m3 = v3(self.u, KD)
        P.dma("sp", m3, self.mix_d.rearrange("(k p) t -> p k t", p=128)[:, :, t0:t0 + T], [], self.tk_u)
        gc0 = 16 * (self.n_ffn + 1)
        for grp in range(2):
            idxs = [4 + grp * 4 + c for c in range(4)]
            self.norm_stats(m3, self.tk_u, idxs, 512, 4)
            for c in idxs:
                P.op("dve", lambda e, c=c: e.scalar_tensor_tensor(
                    m3[:, c, :], m3[:, c, :], self.cv[:, gc0 + c - 4:gc0 + c - 3], self.rs, ALU.mult, ALU.mult),
                    [self.tk_u[c], self.tcv, self.trs], [self.tk_u[c]])
        wo3 = self.wobf.rearrange("(k p) f -> p k f", p=128)
        for dg in range(4):
            banks = [0, 1, 2, 3] if dg % 2 == 0 else [4, 5, 6, 7]
            w, tw = self.nextwb()
            w3 = v3(w, KD)
            P.dma("sp", w3, wo3[:, :, dg * 512:(dg + 1) * 512], self.two, [tw])
            for j in range(4):
                pb = P.psum[banks[j]]
                for k in range(KD):
                    P.op("pe", lambda e, j=j, k=k, pb=pb, w3=w3: e.matmul(
                        pb[:, :], w3[:, k, j * 128:(j + 1) * 128], m3[:, k, :], start=(k == 0), stop=(k == KD - 1)),
                        [tw, self.tk_u[k]], [P.pbank[banks[j]]])
            for j in range(4):
                c = dg * 4 + j
                pb = P.psum[banks[j]]
                P.op("dve", lambda e, c=c, pb=pb: e.tensor_tensor(h3[:, c, :], pb[:, :], h3[:, c, :], ALU.add),
                     [P.pbank[banks[j]], self.tk_h[c]], [self.tk_h[c]])

    def emit(self):
        P = self.P
        self.cast_weights()
        h3 = v3(self.h, KD)
        hin3 = self.h_in.rearrange("(k p) t -> p k t", p=128)
        for it in range(NT // T):
            t0 = it * T
            P.dma("sp", h3, hin3[:, :, t0:t0 + T], [self.t_hin], self.tk_h)
            if self.has_mix:
                self.mix_stage(t0)
            for i in range(self.n_ffn):
                self.ffn(i)
            gc = 16 * self.n_ffn
            if self.epi == "u":
                P.dma("act", self.h_out.rearrange("(k p) t -> p k t", p=128)[:, :, t0:t0 + T], h3, self.tk_h, [self.t_out])
                u3 = v3(self.u, KD)
                self.rmsnorm(gc, u3, self.tk_u)
                P.dma("act", self.u_out.rearrange("(k p) t -> p k t", p=128)[:, :, t0:t0 + T], u3, self.tk_u, [self.t_out])
            else:
                o3 = v3(self.act.bitcast(F32)[:, 0:KD * T], KD)
                self.rmsnorm(gc, o3, [[self.tk_a[2 * k], self.tk_a[2 * k + 1]] for k in range(KD)])
                P.dma("act", self.o_out.rearrange("(k p) t -> p k t", p=128)[:, :, t0:t0 + T], o3, self.tk_a[0:2 * KD], [self.t_out])


SEQ = 8192
NSEQ = 2
NTILE = SEQ // T
WSEL = 834
C_POOL, C_Z, C_X, C_B, C_C, C_Q, C_K, C_V, C_DT = 0, 128, 256, 384, 512, 640, 704, 768, 832
NMC = 38
NEG = -30000.0


class Mixer:
    def __init__(self, P, nseq=NSEQ, ntile=NTILE):
        self.P = P
        self.nseq, self.ntile = nseq, ntile
        ntok = nseq * SEQ
        self.u_d = P.dram("uT", [D, ntok], BF16, "ExternalInput")
        self.w32 = P.dram("wsel", [D, WSEL], F32, "ExternalInput")
        self.wbf_d = P.dram("wsel_bf", [D, WSEL], BF16)
        self.pw_d = P.dram("poolw", [128, 64], F32, "ExternalInput")
        self.mc_d = P.dram("mc", [128, NMC], F32, "ExternalInput")
        self.invc_d = P.dram("invc", [128, T], F32, "ExternalInput")
        self.cf_d = P.dram("cf", [128, 4 * 128], F32, "ExternalInput")
        self.cb_d = P.dram("cb", [128, 3 * 128 + 4 * T], BF16, "ExternalInput")
        self.out_d = P.dram("mixo", [256, ntok], BF16, "ExternalOutput")
        self.t_out = Tk("mixo")
        self.twd = [Tk() for _ in range(KD)]
        A = P.alloc
        self.wsb, self.twsb = A("wsb", KD * WSEL, BF16)
        self.ub = [A("ub%d" % i, KD * T, BF16) for i in range(2)]
        self.QT, _ = A("QT", SEQ, BF16)
        self.KT, _ = A("KT", SEQ, BF16)
        self.V, _ = A("V", 64 * 64, BF16)
        self.tQT = [Tk() for _ in range(NTILE)]
        self.tKT = [Tk() for _ in range(NTILE)]
        self.tV = [Tk() for _ in range(NTILE)]
        self.pbk = 0
        self.mc, self.tmc = A("mc", NMC)
        self.invc, self.tinvc = A("invc", T)
        self.cf, self.tcf = A("cf", 4 * 128)
        self.cb, self.tcb = A("cb", 3 * 128 + 4 * T, BF16)
        self.pw32, self.tpw32 = A("pw32", 64)
        self.pwb, self.tpwb = A("pwb", 64, BF16)
        self.Abc, self.tAbc = A("Abc", 8)
        self.ve, self.tve = A("ve", 15 + T)
        self.s = [A("s%d" % i, 15 + T) for i in range(4)]
        self.res, self.tres = A("res", T)
        self.pdiff, self.tpdiff = A("pdiff", T, BF16)
        self.po, self.tpo = A("po", T, BF16)
        self.xe = [A("xe%d" % i, 3 + T) for i in range(3)]
        self.acc = [A("acc%d" % i, T) for i in range(3)]
        self.xc, self.txc = A("xc", T)
        self.BTb, self.tBTb = A("BTb", T, BF16)
        self.CTf, self.tCTf = A("CTf", T)
        self.CTb, self.tCTb = A("CTb", T, BF16)
        self.sz, self.tsz = A("sz", T)
        self.dtr, self.tdtr = A("dtr", 8)
        self.dx, self.tdx = A("dx", 8)
        self.dax, self.tdax = A("dax", 8)
        self.dt, self.tdt = A("dt", 8)
        self.aa, self.taa = A("aa", 8)
        self.abc = [[A("abc%d%d" % (sl, i), 128) for i in range(2)] for sl in range(2)]
        self.nacs = [A("nacs%d" % sl, 2) for sl in range(2)]
        self.d2 = [A("d2%d" % sl, 2) for sl in range(2)]
        self.w2 = [A("w2%d" % sl, 2) for sl in range(2)]
        self.dtw = [A("dtw%d" % sl, 2) for sl in range(2)]
        self.E = [[A("E%d%d" % (sl, i), 128) for i in range(2)] for sl in range(2)]
        self.Dm = [[A("Dm%d%d" % (sl, i), 128) for i in range(2)] for sl in range(2)]
        self.M = [[A("M%d%d" % (sl, i), 128, BF16) for i in range(2)] for sl in range(2)]
        self.Cs = [[A("Cs%d%d" % (sl, i), 128, BF16) for i in range(2)] for sl in range(2)]
        self.xdtp = [[A("xdtp%d%d" % (sl, i), 128, BF16) for i in range(2)] for sl in range(2)]
        self.xdtw = [A("xdtw%d" % sl, 128, BF16) for sl in range(2)]
        self.Btok = [A("Btok%d" % sl, 128, BF16) for sl in range(2)]
        self.S, self.tS = A("S", 128)
        self.Sbp = [A("Sbp%d" % i, 128, BF16) for i in range(2)]
        self.yt, self.tyt = A("yt", T)
        self.yg, self.tyg = A("yg", T, BF16)
        self.ez = [A("ez%d" % i, T) for i in range(2)]
        self.L = [A("L%d" % i, T, BF16) for i in range(3)]
        self.W = [A("W%d" % i, T, BF16) for i in range(3)]
        self.Lsum, self.tLsum = A("Lsum", T, BF16)
        self.ob, self.tob = A("ob", T, BF16)
        self.blk = 0

    def setup(self):
        P = self.P
        for k in range(KD):
            P.dma("pool", self.wbf_d[k * 128:(k + 1) * 128, :], self.w32[k * 128:(k + 1) * 128, :], [], [self.twd[k]])
        P.dma("sp", v3(self.wsb, KD), self.wbf_d.rearrange("(k p) f -> p k f", p=128), self.twd, [self.twsb])
        P.dma("sp", self.mc, self.mc_d, [], [self.tmc])
        P.dma("sp", self.invc, self.invc_d, [], [self.tinvc])
        P.dma("sp", self.cf, self.cf_d, [], [self.tcf])
        P.dma("sp", self.cb, self.cb_d, [], [self.tcb])
        P.dma("sp", self.pw32, self.pw_d, [], [self.tpw32])
        P.op("act", lambda e: e.copy(self.pwb, self.pw32), [self.tpw32], [self.tpwb])
        P.op("act", lambda e: e.activation(self.Abc, self.mc[:, 30:38], AF.Exp), [self.tmc], [self.tAbc])
        P.op("dve", lambda e: e.tensor_scalar(self.Abc, self.Abc, -1.0, None, ALU.mult), [self.tAbc], [self.tAbc])
        for sl in range(2):
            for i in range(2):
                x, t = self.xdtp[sl][i]
                P.op("pool", lambda e, x=x: e.memset(x, 0.0), [], [t])
        self.triu = self.cf[:, 0:128]
        self.identf = self.cf[:, 128:256]
        self.smask = self.cf[:, 256:384]
        self.onesf = self.cf[:, 384:512]
        self.identb = self.cb[:, 0:128]
        self.ntril = self.cb[:, 128:256]
        self.nones = self.cb[:, 256:384]
        self.amask = [self.cb[:, 384 + j * T:384 + (j + 1) * T] for j in range(4)]

    def proj_units(self, b, i):
        P = self.P
        it = b * self.ntile + i
        tok0 = b * SEQ + i * T
        ub, tub = self.ub[it % 2]
        ub3 = v3(ub, KD)
        wsb3 = v3(self.wsb, KD)
        first = (i == 0)
        units = []

        def u_dma():
            P.dma("sp", ub3, self.u_d.rearrange("(k p) t -> p k t", p=128)[:, :, tok0:tok0 + T], [], [tub])
            if first:
                P.op("pool", lambda e: e.memset(self.ve[:, 0:15], 0.0), [], [self.tve])
                for g in range(3):
                    xe, txe = self.xe[g]
                    P.op("pool", lambda e, xe=xe: e.memset(xe[:, 0:3], 0.0), [], [txe])
                P.op("pool", lambda e: e.memset(self.S, 0.0), [], [self.tS])
                for h in range(2):
                    sb, tsb = self.Sbp[h]
                    P.op("pool", lambda e, sb=sb: e.memset(sb, 0.0), [], [tsb])
        units.append(u_dma)

        def bank():
            bnk = [4, 5, 6][self.pbk % 3]
            self.pbk += 1
            return bnk

        def fm(c0, ncols, evac):
            def unit():
                bnk = bank()
                ps = P.psum[bnk]
                for k in range(KD):
                    P.op("pe", lambda e, k=k: e.matmul(ps[0:ncols, :], wsb3[:, k, c0:c0 + ncols], ub3[:, k, :],
                                                       start=(k == 0), stop=(k == KD - 1)),
                         [self.twsb, tub], [P.pbank[bnk]])
                evac(ps, P.pbank[bnk])
            units.append(unit)

        fm(C_POOL, 128, lambda ps, tp: P.op("dve", lambda e: e.tensor_copy(self.ve[:, 15:15 + T], ps[:, :]), [tp], [self.tve]))
        fm(C_Z, 128, lambda ps, tp: P.op("act", lambda e: e.activation(self.sz, ps[:, :], AF.Silu), [tp], [self.tsz]))
        for g, c0 in enumerate((C_X, C_B, C_C)):
            xe, txe = self.xe[g]
            fm(c0, 128, lambda ps, tp, xe=xe, txe=txe: P.op(
                "dve", lambda e: e.tensor_copy(xe[:, 3:3 + T], ps[:, :]), [tp], [txe]))
        fm(C_Q, 64, lambda ps, tp: P.op("dve", lambda e: e.tensor_scalar(
            self.QT[0:64, i * T:(i + 1) * T], ps[0:64, :], 0.125, None, ALU.mult), [tp], [self.tQT[i]]))
        fm(C_K, 64, lambda ps, tp: P.op("dve", lambda e: e.tensor_copy(
            self.KT[0:64, i * T:(i + 1) * T], ps[0:64, :]), [tp], [self.tKT[i]]))

        def tm():
            bnk = bank()
            ps = P.psum[bnk]
            for j in range(4):
                for k in range(KD):
                    P.op("pe", lambda e, k=k, j=j: e.matmul(ps[:, j * 66:(j + 1) * 66], ub3[:, k, j * 128:(j + 1) * 128],
                                                            wsb3[:, k, C_V:C_V + 66], start=(k == 0), stop=(k == KD - 1)),
                         [self.twsb, tub], [P.pbank[bnk]])
            V3 = self.V.rearrange("p (n d) -> p n d", d=64)
            ps3 = ps[:, 0:264].rearrange("p (j c) -> p j c", c=66)
            P.op("dve", lambda e: e.tensor_copy(V3[:, i * 4:(i + 1) * 4, :], ps3[:, :, 0:64]), [P.pbank[bnk]], [self.tV[i]])
            P.op("dve", lambda e: e.tensor_copy(self.dtr.rearrange("p (j c) -> p j c", c=2), ps3[:, :, 64:66]),
                 [P.pbank[bnk]], [self.tdtr])
        units.append(tm)
        return units

    def mid(self, b, i):
        P = self.P
        tok0 = b * SEQ + i * T
        first = (i == 0)
        ve = self.ve
        sh = [1, 2, 4, 8]
        lo = [1, 3, 7, 15]
        prev, tprev = ve, self.tve
        for q in range(4):
            s, ts = self.s[q]
            P.op("pool", lambda e, s=s, prev=prev, q=q: e.tensor_tensor(
                s[:, lo[q]:15 + T], prev[:, lo[q]:15 + T], prev[:, lo[q] - sh[q]:15 + T - sh[q]], ALU.add),
                [tprev], [ts])
            prev, tprev = s, ts
        s0, ts0 = self.s[0]
        P.op("dve", lambda e: e.tensor_scalar(self.res, s0[:, 15:15 + T], self.mc[:, 16:17], None, ALU.mult),
             [ts0, self.tmc], [self.tres])
        for q in range(1, 4):
            s, ts = self.s[q]
            P.op("dve", lambda e, s=s, q=q: e.scalar_tensor_tensor(self.res, s[:, 15:15 + T], self.mc[:, 16 + q:17 + q],
                                                                    self.res, ALU.mult, ALU.add),
                 [ts, self.tmc, self.tres], [self.tres])
        if first:
            P.op("dve", lambda e: e.tensor_tensor(self.res, self.res, self.invc, ALU.mult), [self.tres, self.tinvc], [self.tres])
            P.op("dve", lambda e: e.tensor_tensor(self.pdiff, self.res, ve[:, 15:15 + T], ALU.subtract),
                 [self.tres, self.tve], [self.tpdiff])
        else:
            P.op("dve", lambda e: e.scalar_tensor_tensor(self.pdiff, self.res, self.mc[:, 20:21], ve[:, 15:15 + T],
                                                          ALU.mult, ALU.subtract),
                 [self.tres, self.tmc, self.tve], [self.tpdiff])
        P.op("pool", lambda e: e.tensor_copy(ve[:, 0:15], ve[:, T:T + 15]), [self.tve], [self.tve])
        bnk = 6
        ps = P.psum[bnk]
        P.op("pe", lambda e, ps=ps: e.matmul(ps[0:64, :], self.pwb, self.pdiff, start=True, stop=True),
             [self.tpwb, self.tpdiff], [P.pbank[bnk]])
        P.op("dve", lambda e, ps=ps: e.tensor_scalar(self.po[0:64, :], ps[0:64, :], self.mc[0:64, 15:16], None, ALU.mult),
             [P.pbank[bnk], self.tmc], [self.tpo])
        P.dma("act", self.out_d[0:64, tok0:tok0 + T], self.po[0:64, :], [self.tpo], [self.t_out])
        for g in range(3):
            xe, txe = self.xe[g]
            acc, tacc = self.acc[g]
            P.op("dve", lambda e, xe=xe, acc=acc, g=g: e.tensor_scalar(
                acc, xe[:, 3:3 + T], self.mc[:, 4 * g + 3:4 * g + 4], self.mc[:, 12 + g:13 + g], ALU.mult, ALU.add),
                [txe, self.tmc], [tacc])
            for kk in (2, 1, 0):
                P.op("dve", lambda e, xe=xe, acc=acc, g=g, kk=kk: e.scalar_tensor_tensor(
                    acc, xe[:, kk:kk + T], self.mc[:, 4 * g + kk:4 * g + kk + 1], acc, ALU.mult, ALU.add),
                    [txe, self.tmc, tacc], [tacc])
            P.op("pool", lambda e, xe=xe: e.tensor_copy(xe[:, 0:3], xe[:, T:T + 3]), [txe], [txe])
        P.op("act", lambda e: e.activation(self.xc, self.acc[0][0], AF.Silu), [self.acc[0][1]], [self.txc])
        P.op("act", lambda e: e.activation(self.BTb, self.acc[1][0], AF.Silu), [self.acc[1][1]], [self.tBTb])
        P.op("act", lambda e: e.activation(self.CTf, self.acc[2][0], AF.Silu), [self.acc[2][1]], [self.tCTf])
        P.op("pool", lambda e: e.tensor_copy(self.CTb, self.CTf), [self.tCTf], [self.tCTb])
        P.op("dve", lambda e: e.tensor_tensor(self.dx, self.dtr, self.mc[:, 22:30], ALU.add), [self.tdtr, self.tmc], [self.tdx])
        P.op("dve", lambda e: e.scalar_tensor_tensor(self.dax, self.dx, -1.0, self.dx, ALU.mult, ALU.max), [self.tdx], [self.tdax])
        P.op("act", lambda e: e.activation(self.dax, self.dax, AF.Exp, scale=-1.0), [self.tdax], [self.tdax])
        P.op("act", lambda e: e.activation(self.dax, self.dax, AF.Ln, bias=1.0), [self.tdax], [self.tdax])
        P.op("dve", lambda e: e.scalar_tensor_tensor(self.dt, self.dx, 0.0, self.dax, ALU.max, ALU.add),
             [self.tdx, self.tdax], [self.tdt])
        P.op("dve", lambda e: e.tensor_tensor(self.aa, self.dt, self.Abc, ALU.mult), [self.tdt, self.tAbc], [self.taa])
        b4, b5, b6 = P.psum[4], P.psum[5], P.psum[6]
        t4, t5, t6 = P.pbank[4], P.pbank[5], P.pbank[6]
        for pair in ((0, 1), (2, 3)):
            for st in range(6):
                for ci in pair:
                    self.ssd_stage(st, ci)
            for ci in pair:
                self.ssd_rec(ci)
        self.post(b, i, tok0)

    def ssd_stage(self, st, ci):
        P = self.P
        sl = ci % 2
        bA, bB = (4, 5) if sl == 0 else (2, 3)
        b4, b5 = P.psum[bA], P.psum[bB]
        t4, t5 = P.pbank[bA], P.pbank[bB]
        c0 = ci * 128
        abc = self.abc[sl]
        E, Dm, M, Cs, xdtp = self.E[sl], self.Dm[sl], self.M[sl], self.Cs[sl], self.xdtp[sl]
        nacs, tnacs = self.nacs[sl]
        d2, td2 = self.d2[sl]
        w2, tw2 = self.w2[sl]
        dtw, tdtw = self.dtw[sl]
        xdtw, txdtw = self.xdtw[sl]
        Btok, tBtok = self.Btok[sl]
        btp = b5[:, 392:456].bitcast(BF16)
        if st == 0:
            for h in range(2):
                a_, ta_ = abc[h]
                P.op("pool", lambda e, a_=a_, h=h: e.tensor_scalar(
                    a_, self.onesf, self.aa[:, ci * 2 + h:ci * 2 + h + 1], None, ALU.mult), [self.tcf, self.taa], [ta_])
        elif st == 1:
            for h in range(2):
                a_, ta_ = abc[h]
                P.op("pe", lambda e, a_=a_, h=h: e.matmul(b4[:, h * 128:(h + 1) * 128], a_, self.triu, start=True, stop=True),
                     [ta_, self.tcf], [t4])
            for h in range(2):
                a_, ta_ = abc[h]
                P.op("pe", lambda e, a_=a_, h=h: e.matmul(b4[:, 256 + h * 128:256 + (h + 1) * 128], a_, self.triu,
                                                          start=True, stop=False), [ta_, self.tcf], [t4])
                P.op("pe", lambda e, h=h: e.matmul(b4[:, 256 + h * 128:256 + (h + 1) * 128], self.identf, self.smask,
                                                   start=False, stop=True), [self.tcf], [t4])
            P.op("pe", lambda e: e.matmul(b5[:, 256:258], self.triu, self.aa[:, ci * 2:ci * 2 + 2], start=True, stop=True),
                 [self.tcf, self.taa], [t5])
            P.op("pe", lambda e: e.matmul(b5[:, 0:128], self.BTb[:, c0:c0 + 128], self.CTb[:, c0:c0 + 128], start=True, stop=True),
                 [self.tBTb, self.tCTb], [t5])
            P.op("pe", lambda e: e.transpose(b5[:, 128:256], self.xc[:, c0:c0 + 128], self.identf), [self.txc, self.tcf], [t5])
            P.op("pe", lambda e: e.transpose(btp, self.BTb[:, c0:c0 + 128], self.identb), [self.tBTb, self.tcb], [t5])
        elif st == 2:
            P.op("dve", lambda e: e.tensor_scalar(nacs, b5[:, 256:258], -1.0, None, ALU.mult), [t5], [tnacs])
            P.op("act", lambda e: e.copy(Btok, btp), [t5], [tBtok])
        elif st == 3:
            for h in range(2):
                E_, tE = E[h]
                Dm_, tDm = Dm[h]
                P.op("act", lambda e, E_=E_, h=h: e.activation(E_, b4[:, h * 128:(h + 1) * 128], AF.Exp), [t4], [tE])
                P.op("act", lambda e, Dm_=Dm_, h=h: e.activation(Dm_, b4[:, 256 + h * 128:256 + (h + 1) * 128], AF.Exp,
                                                                bias=nacs[:, h:h + 1]), [t4, tnacs], [tDm])
                P.op("dve", lambda e, h=h: e.tensor_tensor(d2[:, h:h + 1], b4[:, h * 128 + 127:h * 128 + 128],
                                                          nacs[:, h:h + 1], ALU.add), [t4, tnacs], [td2])
        elif st == 4:
            P.op("act", lambda e: e.activation(w2, d2, AF.Exp), [td2], [tw2])
            P.op("dve", lambda e: e.tensor_tensor(dtw, self.dt[:, ci * 2:ci * 2 + 2], w2, ALU.mult), [self.tdt, tw2], [tdtw])
        elif st == 5:
            for h in range(2):
                M_, tM = M[h]
                Dm_, tDm = Dm[h]
                E_, tE = E[h]
                Cs_, tCs = Cs[h]
                xp, txp = xdtp[h]
                P.op("dve", lambda e, M_=M_, Dm_=Dm_: e.tensor_tensor(M_, b5[:, 0:128], Dm_, ALU.mult), [t5, tDm], [tM])
                P.op("pool", lambda e, Cs_=Cs_, E_=E_: e.tensor_tensor(Cs_, self.CTf[:, c0:c0 + 128], E_, ALU.mult),
                     [self.tCTf, tE], [tCs])
                P.op("dve", lambda e, xp=xp, h=h: e.tensor_scalar(
                    xp[:, h * 64:(h + 1) * 64], b5[:, 128 + h * 64:128 + (h + 1) * 64],
                    self.dt[:, ci * 2 + h:ci * 2 + h + 1], None, ALU.mult), [t5, self.tdt], [txp])
                P.op("dve", lambda e, h=h: e.tensor_scalar(
                    xdtw[:, h * 64:(h + 1) * 64], b5[:, 128 + h * 64:128 + (h + 1) * 64],
                    dtw[:, h:h + 1], None, ALU.mult), [t5, tdtw], [txdtw])

    def ssd_rec(self, ci):
        P = self.P
        sl = ci % 2
        bB = 5 if sl == 0 else 3
        b5, t5 = P.psum[bB], P.pbank[bB]
        b6, t6 = P.psum[6], P.pbank[6]
        c0 = ci * 128
        E, M, Cs, xdtp = self.E[sl], self.M[sl], self.Cs[sl], self.xdtp[sl]
        xdtw, txdtw = self.xdtw[sl]
        Btok, tBtok = self.Btok[sl]
        seqm = [(xdtp[0], M[0]), (xdtp[1], M[1]), (self.Sbp[0], Cs[0]), (self.Sbp[1], Cs[1])]
        for n, ((l, tl), (r, tr)) in enumerate(seqm):
            P.op("pe", lambda e, l=l, r=r, n=n: e.matmul(b6[:, c0:c0 + 128], l, r, start=(n == 0), stop=(n == 3)),
                 [tl, tr], [t6])
        P.op("pe", lambda e: e.matmul(b5[:, 264:392], Btok, xdtw, start=True, stop=True), [tBtok, txdtw], [t5])
        for h in range(2):
            E_, tE = E[h]
            sb, tsb = self.Sbp[h]
            P.op("dve", lambda e, E_=E_, h=h: e.scalar_tensor_tensor(
                self.S[:, h * 64:(h + 1) * 64], self.S[:, h * 64:(h + 1) * 64], E_[:, 127:128],
                b5[:, 264 + h * 64:264 + (h + 1) * 64], ALU.mult, ALU.add), [self.tS, tE, t5], [self.tS])
            P.op("pool", lambda e, sb=sb, h=h: e.tensor_copy(sb[:, h * 64:(h + 1) * 64], self.S[:, h * 64:(h + 1) * 64]),
                 [self.tS], [tsb])

    def post(self, b, i, tok0):
        P = self.P
        b6, t6 = P.psum[6], P.pbank[6]
        P.op("dve", lambda e: e.scalar_tensor_tensor(self.yt, self.xc, self.mc[:, 21:22], b6[:, :], ALU.mult, ALU.add),
             [self.txc, self.tmc, t6], [self.tyt])
        if b == 0 and i == 0:
            P.dump("yt", self.yt, self.tyt); P.dump("S", self.S, self.tS)
        P.op("pool", lambda e: e.tensor_tensor(self.yg, self.yt, self.sz, ALU.mult), [self.tyt, self.tsz], [self.tyg])
        P.dma("act", self.out_d[64:192, tok0:tok0 + T], self.yg, [self.tyg], [self.t_out])

    def attention(self, b, i, filler):
        P = self.P
        tok0 = b * SEQ + i * T
        b7, t7 = P.psum[7], P.pbank[7]
        nblk = 4 * i + 4
        qs = self.QT[0:64, i * T:(i + 1) * T]
        V3 = self.V.rearrange("p (n d) -> p n d", d=64)
        abanks = [0, 1, 2, 3]

        def st_z(n):
            kb = nblk - 1 - n
            j = kb - 4 * i
            diag = j >= 0
            ks = self.KT[0:64, kb * 128:(kb + 1) * 128]
            ab = abanks[n % 4]
            pa, ta = P.psum[ab], P.pbank[ab]
            ez, tez = self.ez[n % 2]
            L, tL = self.L[n % 3]
            P.op("pe", lambda e: e.matmul(pa[:, :], ks, qs, start=True, stop=False), [self.tKT[kb // 4], self.tQT[i]], [ta])
            if diag:
                P.op("pe", lambda e: e.matmul(pa[:, :], self.identb, self.amask[j], start=False, stop=False), [self.tcb], [ta])
            P.op("act", lambda e: e.activation(ez, pa[:, :], AF.Exp), [ta], [tez])
            P.op("act", lambda e: e.activation(L, ez, AF.Ln, bias=1.0), [tez], [tL])

        def st_a(n):
            kb = nblk - 1 - n
            ab = abanks[n % 4]
            pa, ta = P.psum[ab], P.pbank[ab]
            L, tL = self.L[n % 3]
            W, tW = self.W[n % 3]
            P.op("pe", lambda e: e.matmul(pa[:, :], self.ntril, L, start=False, stop=(n == 0)), [self.tcb, tL], [ta])
            if n > 0:
                P.op("pe", lambda e: e.matmul(pa[:, :], self.nones, self.Lsum, start=False, stop=True),
                     [self.tcb, self.tLsum], [ta])
            P.op("act", lambda e: e.activation(W, pa[:, :], AF.Exp), [ta], [tW])
            if kb > 0:
                if n == 0:
                    P.op("pool", lambda e: e.tensor_copy(self.Lsum, L), [tL], [self.tLsum])
                else:
                    P.op("pool", lambda e: e.tensor_tensor(self.Lsum, self.Lsum, L, ALU.add), [tL, self.tLsum], [self.tLsum])

        def st_v(n):
            kb = nblk - 1 - n
            W, tW = self.W[n % 3]
            P.op("pe", lambda e: e.matmul(b7[0:64, :], V3[:, kb, :], W, start=(n == 0), stop=(n == nblk - 1)),
                 [self.tV[kb // 4], tW], [t7])

        units = list(filler)
        per = -(-len(units) // nblk) if units else 0
        for sidx in range(nblk + 2):
            if sidx < nblk:
                st_z(sidx)
            if 1 <= sidx <= nblk:
                st_a(sidx - 1)
            if sidx >= 2:
                st_v(sidx - 2)
            for _ in range(per):
                if units:
                    units.pop(0)()
        while units:
            units.pop(0)()
        P.op("act", lambda e: e.copy(self.ob[0:64, :], b7[0:64, :]), [t7], [self.tob])
        P.dma("act", self.out_d[192:256, tok0:tok0 + T], self.ob[0:64, :], [self.tob], [self.t_out])

    def emit(self):
        self.setup()
        for b in range(self.nseq):
            for u in self.proj_units(b, 0):
                u()
            for i in range(self.ntile):
                self.mid(b, i)
                nxt = self.proj_units(b, i + 1) if i + 1 < self.ntile else []
                self.attention(b, i, nxt)


def colsT(v):
    return np.ascontiguousarray(np.asarray(v, np.float32).reshape(-1, 128).T)


_CONST = {}


def mixer_consts():
    if "cf" not in _CONST:
        k = np.arange(128)
        triu = (k[:, None] <= k[None, :]).astype(np.float32)
        ident = np.eye(128, dtype=np.float32)
        smask = np.where(k[:, None] > k[None, :], NEG, 0.0).astype(np.float32)
        ones = np.ones((128, 128), np.float32)
        _CONST["cf"] = np.concatenate([triu, ident, smask, ones], 1)
        ntril = -(k[:, None] >= k[None, :]).astype(np.float32)
        t = np.arange(T)
        am = [np.where(128 * j + k[:, None] >= t[None, :], NEG, 0.0).astype(np.float32) for j in range(4)]
        _CONST["cb"] = np.concatenate([ident, ntril, -ones] + am, 1).astype(ml_dtypes.bfloat16)
    return _CONST["cf"], _CONST["cb"]


def mixer_inputs(c, w_in, pool_w, pool_scale, conv_w, conv_b, dt_bias, a_log, d_skip):
    g = c // 2
    bc = c // 4
    XB = 1536
    colsel = np.concatenate([
        np.arange(128 * g, 128 * g + 128),
        np.arange(512 + 128 * c, 512 + 128 * c + 128),
        np.arange(XB + 128 * c, XB + 128 * c + 128),
        np.arange(XB + 1024 + 128 * bc, XB + 1024 + 128 * bc + 128),
        np.arange(XB + 1280 + 128 * bc, XB + 1280 + 128 * bc + 128),
        np.arange(3088 + 64 * c, 3088 + 64 * c + 64),
        np.arange(3600 + 64 * c, 3600 + 64 * c + 64),
        np.arange(4112 + 64 * c, 4112 + 64 * c + 64),
        np.arange(3072 + 2 * c, 3072 + 2 * c + 2),
    ])
    wsel = np.ascontiguousarray(w_in[:, colsel])
    poolw = np.ascontiguousarray(pool_w[g][:, 64 * (c % 2):64 * (c % 2) + 64])
    mc = np.zeros((128, NMC), np.float32)
    chx = np.arange(128 * c, 128 * c + 128)
    chB = np.arange(1024 + 128 * bc, 1024 + 128 * bc + 128)
    chC = np.arange(1280 + 128 * bc, 1280 + 128 * bc + 128)
    for gi, ch in enumerate((chx, chB, chC)):
        for kk in range(4):
            mc[:, 4 * gi + kk] = conv_w[kk, ch]
        mc[:, 12 + gi] = conv_b[ch]
    mc[0:64, 15] = pool_scale[128 * g + 64 * (c % 2):128 * g + 64 * (c % 2) + 64]
    mc[:, 16 + g] = 1.0
    w = 2 ** (g + 1)
    mc[:, 20] = 1.0 / w
    mc[0:64, 21] = d_skip[2 * c]
    mc[64:128, 21] = d_skip[2 * c + 1]
    for j in range(4):
        for h in range(2):
            mc[:, 22 + 2 * j + h] = dt_bias[2 * c + h]
            mc[:, 30 + 2 * j + h] = a_log[2 * c + h]
    invc = np.broadcast_to(1.0 / np.minimum(np.arange(1, T + 1), w).astype(np.float32), (128, T)).copy()
    cf, cb = mixer_consts()
    return {"wsel": wsel, "poolw": poolw, "mc": mc, "invc": invc, "cf": cf, "cb": cb}


_PROGS = {}


def get_chain(n_ffn, has_mix, epi):
    key = ("chain", n_ffn, has_mix, epi)
    if key not in _PROGS:
        P = Prog()
        Chain(P, n_ffn, has_mix, epi).emit()
        _PROGS[key] = P.finish()
    return _PROGS[key]


def get_mixer():
    key = ("mixer",)
    if key not in _PROGS:
        P = Prog()
        Mixer(P).emit()
        _PROGS[key] = P.finish()
    return _PROGS[key]


NCORE = 8


def kernel(x, ffn1_norm, ffn1_w_gate, ffn1_w_up, ffn1_w_down, mix_norm, w_in, pool_w, pool_scale,
           conv_w, conv_b, dt_bias, a_log, d_skip, ssd_norm, w_out, ffn2_norm, ffn2_w_gate,
           ffn2_w_up, ffn2_w_down, final_norm):
    f = lambda a: np.asarray(a, dtype=np.float32)
    x = f(x)
    depth = w_in.shape[0]
    xt = x.reshape(-1, D)
    cores = list(range(NCORE))
    z8 = np.zeros((128, 8), np.float32)
    nc = get_chain(1, False, "u")
    cv = np.concatenate([colsT(f(ffn1_norm[0])), colsT(f(mix_norm[0])), z8], 1)
    maps = []
    for c in cores:
        maps.append({"h_in": np.ascontiguousarray(xt[c * NT:(c + 1) * NT].T), "cvec": cv,
                     "wg0": f(ffn1_w_gate[0]), "wu0": f(ffn1_w_up[0]), "wd0": f(ffn1_w_down[0])})
    res = run_bass_kernel_spmd(nc, maps, core_ids=cores)
    h = [res.results[c]["h_out"] for c in cores]
    u = [res.results[c]["u_out"] for c in cores]
    out = None
    for l in range(depth):
        uT = np.ascontiguousarray(np.concatenate([np.asarray(a) for a in u], axis=1))
        nc = get_mixer()
        maps = []
        for c in cores:
            m = mixer_inputs(c, f(w_in[l]), f(pool_w[l]), f(pool_scale[l]), f(conv_w[l]), f(conv_b[l]),
                             f(dt_bias[l]), f(a_log[l]), f(d_skip[l]))
            m["uT"] = uT
            maps.append(m)
        res = run_bass_kernel_spmd(nc, maps, core_ids=cores)
        mixT = np.empty((D, NCORE * NT), dtype=ml_dtypes.bfloat16)
        for c in cores:
            mo = np.asarray(res.results[c]["mixo"])
            mixT[64 * c:64 * c + 64] = mo[0:64]
            mixT[512 + 128 * c:512 + 128 * c + 128] = mo[64:192]
            mixT[1536 + 64 * c:1536 + 64 * c + 64] = mo[192:256]
        last = (l == depth - 1)
        if not last:
            nc = get_chain(2, True, "u")
            cv = np.concatenate([colsT(f(ffn2_norm[l])), colsT(f(ffn1_norm[l + 1])), colsT(f(mix_norm[l + 1])),
                                 colsT(f(ssd_norm[l]))], 1)
        else:
            nc = get_chain(1, True, "final")
            cv = np.concatenate([colsT(f(ffn2_norm[l])), colsT(f(final_norm)), colsT(f(ssd_norm[l]))], 1)
        maps = []
        for c in cores:
            m = {"h_in": h[c], "cvec": cv, "mixT": np.ascontiguousarray(mixT[:, c * NT:(c + 1) * NT]),
                 "wout": f(w_out[l]),
                 "wg0": f(ffn2_w_gate[l]), "wu0": f(ffn2_w_up[l]), "wd0": f(ffn2_w_down[l])}
            if not last:
                m.update({"wg1": f(ffn1_w_gate[l + 1]), "wu1": f(ffn1_w_up[l + 1]), "wd1": f(ffn1_w_down[l + 1])})
            maps.append(m)
        res = run_bass_kernel_spmd(nc, maps, core_ids=cores)
        if not last:
            h = [res.results[c]["h_out"] for c in cores]
            u = [res.results[c]["u_out"] for c in cores]
        else:
            out = np.concatenate([np.asarray(res.results[c]["o_out"]).T for c in cores], axis=0)
    return np.ascontiguousarray(out.reshape(x.shape).astype(np.float32))
```

```python
import numpy as np
import ml_dtypes
from contextlib import ExitStack
import concourse.bass as bass
import concourse.mybir as mybir
from concourse.bass_utils import run_bass_kernel_spmd

F32 = mybir.dt.float32
BF16 = mybir.dt.bfloat16
AF = mybir.ActivationFunctionType
ALU = mybir.AluOpType
AX = mybir.AxisListType

ENG = ("pe", "act", "dve", "pool", "sp")
NDQ = 8
SAME_ENGINE_SYNC = True


class Tk:
    __slots__ = ("name", "w", "r", "excl")

    def __init__(self, name="", excl=False):
        self.name = name
        self.w = None
        self.r = {}
        self.excl = excl


class Prog:
    def __init__(self, arena_f32=49152):
        self.nc = bass.Bass("TRN2", target_bir_lowering=False)
        self.es = ExitStack()
        self.ops = {e: [] for e in ENG}
        self.cnt = {e: 0 for e in ENG}
        self.dcnt = {}
        self.dnext = {q: 0 for q in ("sp", "act", "pool")}
        self.seen = {e: {} for e in ENG}
        self.sems = {}
        nc = self.nc
        for e in ENG:
            self.sems[e] = self.es.enter_context(nc.semaphore("s_" + e))
        for q in ("sp", "act", "pool"):
            for j in range(NDQ):
                k = "d_%s_%d" % (q, j)
                self.sems[k] = self.es.enter_context(nc.semaphore(k))
                self.dcnt[k] = 0
        self.arena = self.es.enter_context(nc.sbuf_tensor("arena", [128, arena_f32], F32))
        self.arena_n = arena_f32
        self.aoff = 0
        self.psum = []
        self.pbank = []
        for i in range(8):
            t = self.es.enter_context(nc.psum_tensor("ps%d" % i, [128, 512], F32))
            self.psum.append(t)
            self.pbank.append(Tk("ps%d" % i, excl=True))
        self.n_inst = 0

    def reset_arena(self, keep=0):
        self.aoff = keep

    def alloc(self, name, cols, dtype=F32):
        nf = cols if dtype == F32 else (cols + 1) // 2
        nf = (nf + 7) // 8 * 8
        assert self.aoff + nf <= self.arena_n, ("arena overflow", name, self.aoff, nf)
        ap = self.arena[:, self.aoff:self.aoff + nf]
        self.aoff += nf
        if dtype != F32:
            ap = ap.bitcast(dtype)[:, 0:cols]
        else:
            ap = ap[:, 0:cols]
        return ap, Tk(name)

    def dram(self, name, shape, dtype, kind="Internal"):
        return self.nc.dram_tensor(name, list(shape), dtype, kind=kind).ap()

    def _waits(self, e, reads, writes):
        waits = {}

        def need(dep):
            if dep is None:
                return
            k, v = dep
            if k == e and (e == "pe" or not SAME_ENGINE_SYNC):
                return
            if waits.get(k, 0) < v:
                waits[k] = v

        for t in reads:
            need(t.w)
            if t.excl:
                for k, v in t.r.items():
                    need((k, v))
        for t in writes:
            need(t.w)
            for k, v in t.r.items():
                need((k, v))
        wl = []
        for k, v in waits.items():
            if self.seen[e].get(k, 0) < v:
                self.seen[e][k] = v
                wl.append((k, v))
        return wl

    def op(self, e, fn, reads=(), writes=()):
        wl = self._waits(e, reads, writes)
        self.cnt[e] += 1
        c = self.cnt[e]
        sems = self.sems
        semE = sems[e]

        def emit(eng):
            for k, v in wl:
                eng.wait_ge(sems[k], v)
            fn(eng).then_inc(semE, 1)

        self.ops[e].append(emit)
        self.n_inst += 1 + len(wl)
        for t in reads:
            if t.excl:
                t.w = (e, c)
                t.r = {}
            else:
                t.r[e] = c
        for t in writes:
            t.w = (e, c)
            t.r = {}

    def dma(self, q, out_ap, in_ap, reads=(), writes=()):
        wl = self._waits(q, reads, writes)
        j = self.dnext[q]
        self.dnext[q] = (j + 1) % NDQ
        key = "d_%s_%d" % (q, j)
        prev = self.dcnt[key]
        if prev > 0 and self.seen[q].get(key, 0) < prev:
            self.seen[q][key] = prev
            wl.append((key, prev))
        self.dcnt[key] = prev + 16
        v = prev + 16
        sems = self.sems

        def emit(eng):
            for k, vv in wl:
                eng.wait_ge(sems[k], vv)
            eng.dma_start(out=out_ap, in_=in_ap).then_inc(sems[key], 16)

        self.ops[q].append(emit)
        self.n_inst += 1 + len(wl)
        for t in reads:
            t.r[key] = v
        for t in writes:
            t.w = (key, v)
            t.r = {}

    def dump(self, name, ap, tk, dtype=F32):
        if not getattr(self, "debug", False):
            return
        d = self.dram("dbg_" + name, [ap.shape[0], ap.shape[1]], dtype, "ExternalOutput")
        self.dma("sp", d, ap, [tk], [Tk()])

    def barrier(self):
        cur = dict(self.cnt)
        cur.update(self.dcnt)
        sems = self.sems
        for e in ENG:
            wl = []
            for k, v in cur.items():
                if k != e and v > self.seen[e].get(k, 0):
                    self.seen[e][k] = v
                    wl.append((k, v))

            def emit(eng, wl=wl):
                for k, v in wl:
                    eng.wait_ge(sems[k], v)

            self.ops[e].append(emit)
            self.n_inst += len(wl)

    def finish(self):
        self.barrier()
        nc = self.nc
        ops = self.ops
        with nc.Block() as block:
            @block.tensor
            def _(eng):
                for f in ops["pe"]:
                    f(eng)

            @block.scalar
            def _(eng):
                for f in ops["act"]:
                    f(eng)

            @block.vector
            def _(eng):
                for f in ops["dve"]:
                    f(eng)

            @block.gpsimd
            def _(eng):
                for f in ops["pool"]:
                    f(eng)

            @block.sync
            def _(eng):
                for f in ops["sp"]:
                    f(eng)
        self.es.close()
        return nc


D = 2048
DFF = 5632
NT = 2048
T = 512
KD = D // 128
KF = DFF // 128
EPS = 1e-6
WB = 8192
NWB = 5


def v3(ap, k):
    return ap.rearrange("p (k t) -> p k t", k=k)


class Chain:
    def __init__(self, P, n_ffn, has_mix, epilogue):
        self.P = P
        nc = P.nc
        self.n_ffn, self.has_mix, self.epi = n_ffn, has_mix, epilogue
        self.h_in = P.dram("h_in", [D, NT], F32, "ExternalInput")
        self.t_hin = Tk("h_in")
        ncv = 16 * (n_ffn + 1) + 8
        self.ncv = ncv
        self.cv_d = P.dram("cvec", [128, ncv], F32, "ExternalInput")
        self.w32 = []
        self.wbf = []
        self.twb = []
        for i in range(n_ffn):
            for nm, shp in (("wg", [D, DFF]), ("wu", [D, DFF]), ("wd", [DFF, D])):
                self.w32.append(P.dram("%s%d" % (nm, i), shp, F32, "ExternalInput"))
                self.wbf.append(P.dram("%s%d_bf" % (nm, i), shp, BF16))
                self.twb.append([Tk() for _ in range(44)])
        if has_mix:
            self.mix_d = P.dram("mixT", [D, NT], BF16, "ExternalInput")
            self.wo32 = P.dram("wout", [D, D], F32, "ExternalInput")
            self.wobf = P.dram("wout_bf", [D, D], BF16)
            self.two = [Tk() for _ in range(KD)]
        if epilogue == "u":
            self.h_out = P.dram("h_out", [D, NT], F32, "ExternalOutput")
            self.u_out = P.dram("u_out", [D, NT], BF16, "ExternalOutput")
        else:
            self.o_out = P.dram("o_out", [D, NT], F32, "ExternalOutput")
        self.t_out = Tk("out")
        self.cv, self.tcv = P.alloc("cv", ncv)
        self.ones, self.tones = P.alloc("ones", 128, BF16)
        self.h, self.th = P.alloc("h", KD * T)
        self.u, self.tu = P.alloc("u", KD * T, BF16)
        self.act, self.tact = P.alloc("act", KF * T, BF16)
        self.sq = [P.alloc("sq%d" % i, T, BF16) for i in range(2)]
        self.sg = [P.alloc("sg%d" % i, T) for i in range(2)]
        self.rs, self.trs = P.alloc("rs", T)
        self.wb = [P.alloc("wb%d" % i, WB, BF16) for i in range(NWB)]
        self.wbi = 0
        self.tk_h = [Tk("h%d" % k) for k in range(KD)]
        self.tk_u = [Tk("u%d" % k) for k in range(KD)]
        self.tk_a = [Tk("a%d" % k) for k in range(KF)]
        self.alt = 0

    def nextwb(self):
        w = self.wb[self.wbi]
        self.wbi = (self.wbi + 1) % NWB
        return w

    def ew(self):
        self.alt ^= 1
        return "dve" if self.alt else "pool"

    def cast_weights(self):
        P = self.P
        P.op("pool", lambda e: e.memset(self.ones, 1.0), [], [self.tones])
        P.dma("sp", self.cv, self.cv_d, [], [self.tcv])
        if self.has_mix:
            for k in range(KD):
                P.dma("pool", self.wobf[k * 128:(k + 1) * 128, :], self.wo32[k * 128:(k + 1) * 128, :],
                      [], [self.two[k]])
        for fi in range(self.n_ffn):
            for fg in range(KF // 4):
                for mi in (3 * fi, 3 * fi + 1):
                    for rq in range(4):
                        P.dma("pool", self.wbf[mi][rq * 512:(rq + 1) * 512, fg * 512:(fg + 1) * 512],
                              self.w32[mi][rq * 512:(rq + 1) * 512, fg * 512:(fg + 1) * 512], [], [self.twb[mi][fg * 4 + rq]])
            mi = 3 * fi + 2
            for k in range(KF):
                P.dma("pool", self.wbf[mi][k * 128:(k + 1) * 128, :], self.w32[mi][k * 128:(k + 1) * 128, :],
                      [], [self.twb[mi][k]])

    def norm_stats(self, src3, tks, idxs, nfeat, bank):
        P = self.P
        ps, tps = P.psum[bank], P.pbank[bank]
        n = len(idxs)
        for i, k in enumerate(idxs):
            sq, tsq = self.sq[i % 2]
            P.op("act", lambda e, k=k, sq=sq: e.activation(sq, src3[:, k, :], AF.Square), [tks[k]], [tsq])
            P.op("pe", lambda e, i=i, sq=sq: e.matmul(ps[:, :], self.ones, sq, start=(i == 0), stop=(i == n - 1)),
                 [self.tones, tsq], [tps])
        P.op("act", lambda e: e.activation(self.rs, ps[:, :], AF.Sqrt, bias=EPS, scale=1.0 / nfeat), [tps], [self.trs])
        P.op("dve", lambda e: e.reciprocal(self.rs, self.rs), [self.trs], [self.trs])

    def rmsnorm(self, gcol0, dst3, tdst):
        P = self.P
        h3 = v3(self.h, KD)
        self.norm_stats(h3, self.tk_h, list(range(KD)), D, 4)
        for k in range(KD):
            P.op("dve", lambda e, k=k: e.scalar_tensor_tensor(
                dst3[:, k, :], h3[:, k, :], self.cv[:, gcol0 + k:gcol0 + k + 1], self.rs, ALU.mult, ALU.mult),
                [self.tk_h[k], self.tcv, self.trs], tdst[k] if isinstance(tdst[k], list) else [tdst[k]])

    def ffn(self, i):
        P = self.P
        h3 = v3(self.h, KD)
        u3 = v3(self.u, KD)
        a3 = v3(self.act, KF)
        wg, wu, wd = self.wbf[3 * i], self.wbf[3 * i + 1], self.wbf[3 * i + 2]
        twg, twu, twd = self.twb[3 * i], self.twb[3 * i + 1], self.twb[3 * i + 2]
        self.rmsnorm(16 * i, u3, self.tk_u)
        wg3 = wg.rearrange("(k p) f -> p k f", p=128)
        wu3 = wu.rearrange("(k p) f -> p k f", p=128)
        wd3 = wd.rearrange("(k p) f -> p k f", p=128)
        gi = 0
        for fg in range(KF // 4):
            (wa, twa), (wb_, twb_) = self.nextwb(), self.nextwb()
            wa3, wb3 = v3(wa, KD), v3(wb_, KD)
            P.dma("sp", wa3, wg3[:, :, fg * 512:(fg + 1) * 512], twg[fg * 4:fg * 4 + 4], [twa])
            P.dma("sp", wb3, wu3[:, :, fg * 512:(fg + 1) * 512], twu[fg * 4:fg * 4 + 4], [twb_])
            for f4 in range(4):
                f = fg * 4 + f4
                bg, bu = (0, 1) if gi % 2 == 0 else (2, 3)
                gi += 1
                pg, pu = P.psum[bg], P.psum[bu]
                for k in range(KD):
                    P.op("pe", lambda e, k=k, f4=f4, pg=pg, wa3=wa3: e.matmul(
                        pg[:, :], wa3[:, k, f4 * 128:(f4 + 1) * 128], u3[:, k, :], start=(k == 0), stop=(k == KD - 1)),
                        [twa, self.tk_u[k]], [P.pbank[bg]])
                for k in range(KD):
                    P.op("pe", lambda e, k=k, f4=f4, pu=pu, wb3=wb3: e.matmul(
                        pu[:, :], wb3[:, k, f4 * 128:(f4 + 1) * 128], u3[:, k, :], start=(k == 0), stop=(k == KD - 1)),
                        [twb_, self.tk_u[k]], [P.pbank[bu]])
                sg, tsg = self.sg[f % 2]
                P.op("act", lambda e, sg=sg, pg=pg: e.activation(sg, pg[:, :], AF.Silu), [P.pbank[bg]], [tsg])
                P.op("dve", lambda e, sg=sg, pu=pu, f=f: e.tensor_tensor(a3[:, f, :], sg, pu[:, :], ALU.mult),
                     [tsg, P.pbank[bu]], [self.tk_a[f]])
        FD = 11
        for dg in range(4):
            banks = [4, 5, 6, 7] if dg % 2 == 0 else [0, 1, 2, 3]
            for fgd in range(KF // FD):
                w, tw = self.nextwb()
                w3 = w[:, 0:FD * 512].rearrange("p (k t) -> p k t", k=FD)
                P.dma("sp", w3, wd3[:, fgd * FD:(fgd + 1) * FD, dg * 512:(dg + 1) * 512],
                      twd[fgd * FD:(fgd + 1) * FD], [tw])
                for j in range(4):
                    pb = P.psum[banks[j]]
                    for f in range(FD):
                        ff = fgd * FD + f
                        P.op("pe", lambda e, j=j, f=f, ff=ff, pb=pb, w3=w3: e.matmul(
                            pb[:, :], w3[:, f, j * 128:(j + 1) * 128], a3[:, ff, :],
                            start=(ff == 0), stop=(ff == KF - 1)),
                            [tw, self.tk_a[ff]], [P.pbank[banks[j]]])
            for j in range(4):
                c = dg * 4 + j
                pb = P.psum[banks[j]]
                P.op("dve", lambda e, c=c, pb=pb: e.scalar_tensor_tensor(
                    h3[:, c, :], pb[:, :], 0.5, h3[:, c, :], ALU.mult, ALU.add),
                    [P.pbank[banks[j]], self.tk_h[c]], [self.tk_h[c]])

    def mix_stage(self, t0):
        P = self.P
        h3 = v3(self.h, KD)
        m3 = v3(self.u, KD)
        P.dma("sp", m3, self.mix_d.rearrange("(k p) t -> p k t", p=128)[:, :, t0:t0 + T], [], self.tk_u)
        gc0 = 16 * (self.n_ffn + 1)
        for grp in range(2):
            idxs = [4 + grp * 4 + c for c in range(4)]
            self.norm_stats(m3, self.tk_u, idxs, 512, 4)
            for c in idxs:
                P.op("dve", lambda e, c=c: e.scalar_tensor_tensor(
                    m3[:, c, :], m3[:, c, :], self.cv[:, gc0 + c - 4:gc0 + c - 3], self.rs, ALU.mult, ALU.mult),
                    [self.tk_u[c], self.tcv, self.trs], [self.tk_u[c]])
        wo3 = self.wobf.rearrange("(k p) f -> p k f", p=128)
        for dg in range(4):
            banks = [0, 1, 2, 3] if dg % 2 == 0 else [4, 5, 6, 7]
            w, tw = self.nextwb()
            w3 = v3(w, KD)
            P.dma("sp", w3, wo3[:, :, dg * 512:(dg + 1) * 512], self.two, [tw])
            for j in range(4):
                pb = P.psum[banks[j]]
                for k in range(KD):
                    P.op("pe", lambda e, j=j, k=k, pb=pb, w3=w3: e.matmul(
                        pb[:, :], w3[:, k, j * 128:(j + 1) * 128], m3[:, k, :], start=(k == 0), stop=(k == KD - 1)),
                        [tw, self.tk_u[k]], [P.pbank[banks[j]]])
            for j in range(4):
                c = dg * 4 + j
                pb = P.psum[banks[j]]
                P.op("dve", lambda e, c=c, pb=pb: e.tensor_tensor(h3[:, c, :], pb[:, :], h3[:, c, :], ALU.add),
                     [P.pbank[banks[j]], self.tk_h[c]], [self.tk_h[c]])

    def emit(self):
        P = self.P
        self.cast_weights()
        h3 = v3(self.h, KD)
        hin3 = self.h_in.rearrange("(k p) t -> p k t", p=128)
        for it in range(NT // T):
            t0 = it * T
            P.dma("sp", h3, hin3[:, :, t0:t0 + T], [self.t_hin], self.tk_h)
            if self.has_mix:
                self.mix_stage(t0)
            for i in range(self.n_ffn):
                self.ffn(i)
            gc = 16 * self.n_ffn
            if self.epi == "u":
                P.dma("act", self.h_out.rearrange("(k p) t -> p k t", p=128)[:, :, t0:t0 + T], h3, self.tk_h, [self.t_out])
                u3 = v3(self.u, KD)
                self.rmsnorm(gc, u3, self.tk_u)
                P.dma("act", self.u_out.rearrange("(k p) t -> p k t", p=128)[:, :, t0:t0 + T], u3, self.tk_u, [self.t_out])
            else:
                o3 = v3(self.act.bitcast(F32)[:, 0:KD * T], KD)
                self.rmsnorm(gc, o3, [[self.tk_a[2 * k], self.tk_a[2 * k + 1]] for k in range(KD)])
                P.dma("act", self.o_out.rearrange("(k p) t -> p k t", p=128)[:, :, t0:t0 + T], o3, self.tk_a[0:2 * KD], [self.t_out])


SEQ = 8192
NSEQ = 2
NTILE = SEQ // T
WSEL = 834
C_POOL, C_Z, C_X, C_B, C_C, C_Q, C_K, C_V, C_DT = 0, 128, 256, 384, 512, 640, 704, 768, 832
NMC = 38
NEG = -30000.0


class Mixer:
    def __init__(self, P, nseq=NSEQ, ntile=NTILE):
        self.P = P
        self.nseq, self.ntile = nseq, ntile
        ntok = nseq * SEQ
        self.u_d = P.dram("uT", [D, ntok], BF16, "ExternalInput")
        self.w32 = P.dram("wsel", [D, WSEL], F32, "ExternalInput")
        self.wbf_d = P.dram("wsel_bf", [D, WSEL], BF16)
        self.pw_d = P.dram("poolw", [128, 64], F32, "ExternalInput")
        self.mc_d = P.dram("mc", [128, NMC], F32, "ExternalInput")
        self.invc_d = P.dram("invc", [128, T], F32, "ExternalInput")
        self.cf_d = P.dram("cf", [128, 4 * 128], F32, "ExternalInput")
        self.cb_d = P.dram("cb", [128, 3 * 128 + 4 * T], BF16, "ExternalInput")
        self.out_d = P.dram("mixo", [256, ntok], BF16, "ExternalOutput")
        self.t_out = Tk("mixo")
        self.twd = [Tk() for _ in range(KD)]
        A = P.alloc
        self.wsb, self.twsb = A("wsb", KD * WSEL, BF16)
        self.ub = [A("ub%d" % i, KD * T, BF16) for i in range(2)]
        self.QT, _ = A("QT", SEQ, BF16)
        self.KT, _ = A("KT", SEQ, BF16)
        self.V, _ = A("V", 64 * 64, BF16)
        self.tQT = [Tk() for _ in range(NTILE)]
        self.tKT = [Tk() for _ in range(NTILE)]
        self.tV = [Tk() for _ in range(NTILE)]
        self.pbk = 0
        self.mc, self.tmc = A("mc", NMC)
        self.invc, self.tinvc = A("invc", T)
        self.cf, self.tcf = A("cf", 4 * 128)
        self.cb, self.tcb = A("cb", 3 * 128 + 4 * T, BF16)
        self.pw32, self.tpw32 = A("pw32", 64)
        self.pwb, self.tpwb = A("pwb", 64, BF16)
        self.Abc, self.tAbc = A("Abc", 8)
        self.ve, self.tve = A("ve", 15 + T)
        self.s = [A("s%d" % i, 15 + T) for i in range(4)]
        self.res, self.tres = A("res", T)
        self.pdiff, self.tpdiff = A("pdiff", T, BF16)
        self.po, self.tpo = A("po", T, BF16)
        self.xe = [A("xe%d" % i, 3 + T) for i in range(3)]
        self.acc = [A("acc%d" % i, T) for i in range(3)]
        self.xc, self.txc = A("xc", T)
        self.BTb, self.tBTb = A("BTb", T, BF16)
        self.CTf, self.tCTf = A("CTf", T)
        self.CTb, self.tCTb = A("CTb", T, BF16)
        self.sz, self.tsz = A("sz", T)
        self.dtr, self.tdtr = A("dtr", 8)
        self.dx, self.tdx = A("dx", 8)
        self.dax, self.tdax = A("dax", 8)
        self.dt, self.tdt = A("dt", 8)
        self.aa, self.taa = A("aa", 8)
        self.abc = [[A("abc%d%d" % (sl, i), 128) for i in range(2)] for sl in range(2)]
        self.nacs = [A("nacs%d" % sl, 2) for sl in range(2)]
        self.d2 = [A("d2%d" % sl, 2) for sl in range(2)]
        self.w2 = [A("w2%d" % sl, 2) for sl in range(2)]
        self.dtw = [A("dtw%d" % sl, 2) for sl in range(2)]
        self.E = [[A("E%d%d" % (sl, i), 128) for i in range(2)] for sl in range(2)]
        self.Dm = [[A("Dm%d%d" % (sl, i), 128) for i in range(2)] for sl in range(2)]
        self.M = [[A("M%d%d" % (sl, i), 128, BF16) for i in range(2)] for sl in range(2)]
        self.Cs = [[A("Cs%d%d" % (sl, i), 128, BF16) for i in range(2)] for sl in range(2)]
        self.xdtp = [[A("xdtp%d%d" % (sl, i), 128, BF16) for i in range(2)] for sl in range(2)]
        self.xdtw = [A("xdtw%d" % sl, 128, BF16) for sl in range(2)]
        self.Btok = [A("Btok%d" % sl, 128, BF16) for sl in range(2)]
        self.S, self.tS = A("S", 128)
        self.Sbp = [A("Sbp%d" % i, 128, BF16) for i in range(2)]
        self.yt, self.tyt = A("yt", T)
        self.yg, self.tyg = A("yg", T, BF16)
        self.ez = [A("ez%d" % i, T) for i in range(2)]
        self.L = [A("L%d" % i, T, BF16) for i in range(4)]
        self.W = [A("W%d" % i, T, BF16) for i in range(4)]
        self.Lsum = [A("Lsum%d" % i, T, BF16) for i in range(3)]
        self.ob, self.tob = A("ob", T, BF16)
        self.blk = 0

    def setup(self):
        P = self.P
        for k in range(KD):
            P.dma("pool", self.wbf_d[k * 128:(k + 1) * 128, :], self.w32[k * 128:(k + 1) * 128, :], [], [self.twd[k]])
        P.dma("sp", v3(self.wsb, KD), self.wbf_d.rearrange("(k p) f -> p k f", p=128), self.twd, [self.twsb])
        P.dma("sp", self.mc, self.mc_d, [], [self.tmc])
        P.dma("sp", self.invc, self.invc_d, [], [self.tinvc])
        P.dma("sp", self.cf, self.cf_d, [], [self.tcf])
        P.dma("sp", self.cb, self.cb_d, [], [self.tcb])
        P.dma("sp", self.pw32, self.pw_d, [], [self.tpw32])
        P.op("act", lambda e: e.copy(self.pwb, self.pw32), [self.tpw32], [self.tpwb])
        P.op("act", lambda e: e.activation(self.Abc, self.mc[:, 30:38], AF.Exp), [self.tmc], [self.tAbc])
        P.op("dve", lambda e: e.tensor_scalar(self.Abc, self.Abc, -1.0, None, ALU.mult), [self.tAbc], [self.tAbc])
        for sl in range(2):
            for i in range(2):
                x, t = self.xdtp[sl][i]
                P.op("pool", lambda e, x=x: e.memset(x, 0.0), [], [t])
        self.triu = self.cf[:, 0:128]
        self.identf = self.cf[:, 128:256]
        self.smask = self.cf[:, 256:384]
        self.onesf = self.cf[:, 384:512]
        self.identb = self.cb[:, 0:128]
        self.ntril = self.cb[:, 128:256]
        self.nones = self.cb[:, 256:384]
        self.amask = [self.cb[:, 384 + j * T:384 + (j + 1) * T] for j in range(4)]

    def proj_units(self, b, i):
        P = self.P
        it = b * self.ntile + i
        tok0 = b * SEQ + i * T
        ub, tub = self.ub[it % 2]
        ub3 = v3(ub, KD)
        wsb3 = v3(self.wsb, KD)
        first = (i == 0)
        units = []

        def u_dma():
            P.dma("sp", ub3, self.u_d.rearrange("(k p) t -> p k t", p=128)[:, :, tok0:tok0 + T], [], [tub])
            if first:
                P.op("pool", lambda e: e.memset(self.ve[:, 0:15], 0.0), [], [self.tve])
                for g in range(3):
                    xe, txe = self.xe[g]
                    P.op("pool", lambda e, xe=xe: e.memset(xe[:, 0:3], 0.0), [], [txe])
                P.op("pool", lambda e: e.memset(self.S, 0.0), [], [self.tS])
                for h in range(2):
                    sb, tsb = self.Sbp[h]
                    P.op("pool", lambda e, sb=sb: e.memset(sb, 0.0), [], [tsb])
        units.append(u_dma)

        def bank():
            bnk = [4, 5, 6][self.pbk % 3]
            self.pbk += 1
            return bnk

        def fm(c0, ncols, evac):
            def unit():
                bnk = bank()
                ps = P.psum[bnk]
                for k in range(KD):
                    P.op("pe", lambda e, k=k: e.matmul(ps[0:ncols, :], wsb3[:, k, c0:c0 + ncols], ub3[:, k, :],
                                                       start=(k == 0), stop=(k == KD - 1)),
                         [self.twsb, tub], [P.pbank[bnk]])
                evac(ps, P.pbank[bnk])
            units.append(unit)

        fm(C_POOL, 128, lambda ps, tp: P.op("dve", lambda e: e.tensor_copy(self.ve[:, 15:15 + T], ps[:, :]), [tp], [self.tve]))
        fm(C_Z, 128, lambda ps, tp: P.op("act", lambda e: e.activation(self.sz, ps[:, :], AF.Silu), [tp], [self.tsz]))
        for g, c0 in enumerate((C_X, C_B, C_C)):
            xe, txe = self.xe[g]
            fm(c0, 128, lambda ps, tp, xe=xe, txe=txe: P.op(
                "dve", lambda e: e.tensor_copy(xe[:, 3:3 + T], ps[:, :]), [tp], [txe]))
        fm(C_Q, 64, lambda ps, tp: P.op("dve", lambda e: e.tensor_scalar(
            self.QT[0:64, i * T:(i + 1) * T], ps[0:64, :], 0.125, None, ALU.mult), [tp], [self.tQT[i]]))
        fm(C_K, 64, lambda ps, tp: P.op("dve", lambda e: e.tensor_copy(
            self.KT[0:64, i * T:(i + 1) * T], ps[0:64, :]), [tp], [self.tKT[i]]))

        def tm():
            bnk = bank()
            ps = P.psum[bnk]
            for j in range(4):
                for k in range(KD):
                    P.op("pe", lambda e, k=k, j=j: e.matmul(ps[:, j * 66:(j + 1) * 66], ub3[:, k, j * 128:(j + 1) * 128],
                                                            wsb3[:, k, C_V:C_V + 66], start=(k == 0), stop=(k == KD - 1)),
                         [self.twsb, tub], [P.pbank[bnk]])
            V3 = self.V.rearrange("p (n d) -> p n d", d=64)
            ps3 = ps[:, 0:264].rearrange("p (j c) -> p j c", c=66)
            P.op("dve", lambda e: e.tensor_copy(V3[:, i * 4:(i + 1) * 4, :], ps3[:, :, 0:64]), [P.pbank[bnk]], [self.tV[i]])
            P.op("dve", lambda e: e.tensor_copy(self.dtr.rearrange("p (j c) -> p j c", c=2), ps3[:, :, 64:66]),
                 [P.pbank[bnk]], [self.tdtr])
        units.append(tm)
        return units

    def mid(self, b, i):
        P = self.P
        tok0 = b * SEQ + i * T
        first = (i == 0)
        ve = self.ve
        sh = [1, 2, 4, 8]
        lo = [1, 3, 7, 15]
        prev, tprev = ve, self.tve
        for q in range(4):
            s, ts = self.s[q]
            P.op("pool", lambda e, s=s, prev=prev, q=q: e.tensor_tensor(
                s[:, lo[q]:15 + T], prev[:, lo[q]:15 + T], prev[:, lo[q] - sh[q]:15 + T - sh[q]], ALU.add),
                [tprev], [ts])
            prev, tprev = s, ts
        s0, ts0 = self.s[0]
        P.op("dve", lambda e: e.tensor_scalar(self.res, s0[:, 15:15 + T], self.mc[:, 16:17], None, ALU.mult),
             [ts0, self.tmc], [self.tres])
        for q in range(1, 4):
            s, ts = self.s[q]
            P.op("dve", lambda e, s=s, q=q: e.scalar_tensor_tensor(self.res, s[:, 15:15 + T], self.mc[:, 16 + q:17 + q],
                                                                    self.res, ALU.mult, ALU.add),
                 [ts, self.tmc, self.tres], [self.tres])
        if first:
            P.op("dve", lambda e: e.tensor_tensor(self.res, self.res, self.invc, ALU.mult), [self.tres, self.tinvc], [self.tres])
            P.op("dve", lambda e: e.tensor_tensor(self.pdiff, self.res, ve[:, 15:15 + T], ALU.subtract),
                 [self.tres, self.tve], [self.tpdiff])
        else:
            P.op("dve", lambda e: e.scalar_tensor_tensor(self.pdiff, self.res, self.mc[:, 20:21], ve[:, 15:15 + T],
                                                          ALU.mult, ALU.subtract),
                 [self.tres, self.tmc, self.tve], [self.tpdiff])
        P.op("pool", lambda e: e.tensor_copy(ve[:, 0:15], ve[:, T:T + 15]), [self.tve], [self.tve])
        bnk = 6
        ps = P.psum[bnk]
        P.op("pe", lambda e, ps=ps: e.matmul(ps[0:64, :], self.pwb, self.pdiff, start=True, stop=True),
             [self.tpwb, self.tpdiff], [P.pbank[bnk]])
        P.op("dve", lambda e, ps=ps: e.tensor_scalar(self.po[0:64, :], ps[0:64, :], self.mc[0:64, 15:16], None, ALU.mult),
             [P.pbank[bnk], self.tmc], [self.tpo])
        P.dma("act", self.out_d[0:64, tok0:tok0 + T], self.po[0:64, :], [self.tpo], [self.t_out])
        yield
        for g in range(3):
            xe, txe = self.xe[g]
            acc, tacc = self.acc[g]
            P.op("dve", lambda e, xe=xe, acc=acc, g=g: e.tensor_scalar(
                acc, xe[:, 3:3 + T], self.mc[:, 4 * g + 3:4 * g + 4], self.mc[:, 12 + g:13 + g], ALU.mult, ALU.add),
                [txe, self.tmc], [tacc])
            for kk in (2, 1, 0):
                P.op("dve", lambda e, xe=xe, acc=acc, g=g, kk=kk: e.scalar_tensor_tensor(
                    acc, xe[:, kk:kk + T], self.mc[:, 4 * g + kk:4 * g + kk + 1], acc, ALU.mult, ALU.add),
                    [txe, self.tmc, tacc], [tacc])
            P.op("pool", lambda e, xe=xe: e.tensor_copy(xe[:, 0:3], xe[:, T:T + 3]), [txe], [txe])
            yield
        P.op("act", lambda e: e.activation(self.xc, self.acc[0][0], AF.Silu), [self.acc[0][1]], [self.txc])
        P.op("act", lambda e: e.activation(self.BTb, self.acc[1][0], AF.Silu), [self.acc[1][1]], [self.tBTb])
        P.op("act", lambda e: e.activation(self.CTf, self.acc[2][0], AF.Silu), [self.acc[2][1]], [self.tCTf])
        P.op("pool", lambda e: e.tensor_copy(self.CTb, self.CTf), [self.tCTf], [self.tCTb])
        P.op("dve", lambda e: e.tensor_tensor(self.dx, self.dtr, self.mc[:, 22:30], ALU.add), [self.tdtr, self.tmc], [self.tdx])
        P.op("dve", lambda e: e.scalar_tensor_tensor(self.dax, self.dx, -1.0, self.dx, ALU.mult, ALU.max), [self.tdx], [self.tdax])
        P.op("act", lambda e: e.activation(self.dax, self.dax, AF.Exp, scale=-1.0), [self.tdax], [self.tdax])
        P.op("act", lambda e: e.activation(self.dax, self.dax, AF.Ln, bias=1.0), [self.tdax], [self.tdax])
        P.op("dve", lambda e: e.scalar_tensor_tensor(self.dt, self.dx, 0.0, self.dax, ALU.max, ALU.add),
             [self.tdx, self.tdax], [self.tdt])
        P.op("dve", lambda e: e.tensor_tensor(self.aa, self.dt, self.Abc, ALU.mult), [self.tdt, self.tAbc], [self.taa])
        b4, b5, b6 = P.psum[4], P.psum[5], P.psum[6]
        t4, t5, t6 = P.pbank[4], P.pbank[5], P.pbank[6]
        yield
        for ci in range(4):
            for st in range(6):
                self.ssd_stage(st, ci)
                yield
            self.ssd_rec(ci)
            yield
        self.post(b, i, tok0)

    NMID = 34

    def mid_units(self, b, i):
        gen = self.mid(b, i)
        return [(lambda: next(gen, None)) for _ in range(self.NMID + 2)]

    def ssd_stage(self, st, ci):
        P = self.P
        sl = ci % 2
        bA, bB = (4, 5)
        b4, b5 = P.psum[bA], P.psum[bB]
        t4, t5 = P.pbank[bA], P.pbank[bB]
        c0 = ci * 128
        abc = self.abc[sl]
        E, Dm, M, Cs, xdtp = self.E[sl], self.Dm[sl], self.M[sl], self.Cs[sl], self.xdtp[sl]
        nacs, tnacs = self.nacs[sl]
        d2, td2 = self.d2[sl]
        w2, tw2 = self.w2[sl]
        dtw, tdtw = self.dtw[sl]
        xdtw, txdtw = self.xdtw[sl]
        Btok, tBtok = self.Btok[sl]
        btp = b5[:, 392:456].bitcast(BF16)
        if st == 0:
            for h in range(2):
                a_, ta_ = abc[h]
                P.op("pool", lambda e, a_=a_, h=h: e.tensor_scalar(
                    a_, self.onesf, self.aa[:, ci * 2 + h:ci * 2 + h + 1], None, ALU.mult), [self.tcf, self.taa], [ta_])
        elif st == 1:
            for h in range(2):
                a_, ta_ = abc[h]
                P.op("pe", lambda e, a_=a_, h=h: e.matmul(b4[:, h * 128:(h + 1) * 128], a_, self.triu, start=True, stop=True),
                     [ta_, self.tcf], [t4])
            for h in range(2):
                a_, ta_ = abc[h]
                P.op("pe", lambda e, a_=a_, h=h: e.matmul(b4[:, 256 + h * 128:256 + (h + 1) * 128], a_, self.triu,
                                                          start=True, stop=False), [ta_, self.tcf], [t4])
                P.op("pe", lambda e, h=h: e.matmul(b4[:, 256 + h * 128:256 + (h + 1) * 128], self.identf, self.smask,
                                                   start=False, stop=True), [self.tcf], [t4])
            P.op("pe", lambda e: e.matmul(b5[:, 256:258], self.triu, self.aa[:, ci * 2:ci * 2 + 2], start=True, stop=True),
                 [self.tcf, self.taa], [t5])
            P.op("pe", lambda e: e.matmul(b5[:, 0:128], self.BTb[:, c0:c0 + 128], self.CTb[:, c0:c0 + 128], start=True, stop=True),
                 [self.tBTb, self.tCTb], [t5])
            P.op("pe", lambda e: e.transpose(b5[:, 128:256], self.xc[:, c0:c0 + 128], self.identf), [self.txc, self.tcf], [t5])
            P.op("pe", lambda e: e.transpose(btp, self.BTb[:, c0:c0 + 128], self.identb), [self.tBTb, self.tcb], [t5])
        elif st == 2:
            P.op("dve", lambda e: e.tensor_scalar(nacs, b5[:, 256:258], -1.0, None, ALU.mult), [t5], [tnacs])
            P.op("act", lambda e: e.copy(Btok, btp), [t5], [tBtok])
        elif st == 3:
            for h in range(2):
                E_, tE = E[h]
                Dm_, tDm = Dm[h]
                P.op("act", lambda e, E_=E_, h=h: e.activation(E_, b4[:, h * 128:(h + 1) * 128], AF.Exp), [t4], [tE])
                P.op("act", lambda e, Dm_=Dm_, h=h: e.activation(Dm_, b4[:, 256 + h * 128:256 + (h + 1) * 128], AF.Exp,
                                                                bias=nacs[:, h:h + 1]), [t4, tnacs], [tDm])
                P.op("dve", lambda e, h=h: e.tensor_tensor(d2[:, h:h + 1], b4[:, h * 128 + 127:h * 128 + 128],
                                                          nacs[:, h:h + 1], ALU.add), [t4, tnacs], [td2])
        elif st == 4:
            P.op("act", lambda e: e.activation(w2, d2, AF.Exp), [td2], [tw2])
            P.op("dve", lambda e: e.tensor_tensor(dtw, self.dt[:, ci * 2:ci * 2 + 2], w2, ALU.mult), [self.tdt, tw2], [tdtw])
        elif st == 5:
            for h in range(2):
                M_, tM = M[h]
                Dm_, tDm = Dm[h]
                E_, tE = E[h]
                Cs_, tCs = Cs[h]
                xp, txp = xdtp[h]
                P.op("dve", lambda e, M_=M_, Dm_=Dm_: e.tensor_tensor(M_, b5[:, 0:128], Dm_, ALU.mult), [t5, tDm], [tM])
                P.op("pool", lambda e, Cs_=Cs_, E_=E_: e.tensor_tensor(Cs_, self.CTf[:, c0:c0 + 128], E_, ALU.mult),
                     [self.tCTf, tE], [tCs])
                P.op("dve", lambda e, xp=xp, h=h: e.tensor_scalar(
                    xp[:, h * 64:(h + 1) * 64], b5[:, 128 + h * 64:128 + (h + 1) * 64],
                    self.dt[:, ci * 2 + h:ci * 2 + h + 1], None, ALU.mult), [t5, self.tdt], [txp])
                P.op("dve", lambda e, h=h: e.tensor_scalar(
                    xdtw[:, h * 64:(h + 1) * 64], b5[:, 128 + h * 64:128 + (h + 1) * 64],
                    dtw[:, h:h + 1], None, ALU.mult), [t5, tdtw], [txdtw])

    def ssd_rec(self, ci):
        P = self.P
        sl = ci % 2
        bB = 5
        b5, t5 = P.psum[bB], P.pbank[bB]
        b6, t6 = P.psum[6], P.pbank[6]
        c0 = ci * 128
        E, M, Cs, xdtp = self.E[sl], self.M[sl], self.Cs[sl], self.xdtp[sl]
        xdtw, txdtw = self.xdtw[sl]
        Btok, tBtok = self.Btok[sl]
        seqm = [(xdtp[0], M[0]), (xdtp[1], M[1]), (self.Sbp[0], Cs[0]), (self.Sbp[1], Cs[1])]
        for n, ((l, tl), (r, tr)) in enumerate(seqm):
            P.op("pe", lambda e, l=l, r=r, n=n: e.matmul(b6[:, c0:c0 + 128], l, r, start=(n == 0), stop=(n == 3)),
                 [tl, tr], [t6])
        P.op("pe", lambda e: e.matmul(b5[:, 264:392], Btok, xdtw, start=True, stop=True), [tBtok, txdtw], [t5])
        for h in range(2):
            E_, tE = E[h]
            sb, tsb = self.Sbp[h]
            P.op("dve", lambda e, E_=E_, h=h: e.scalar_tensor_tensor(
                self.S[:, h * 64:(h + 1) * 64], self.S[:, h * 64:(h + 1) * 64], E_[:, 127:128],
                b5[:, 264 + h * 64:264 + (h + 1) * 64], ALU.mult, ALU.add), [self.tS, tE, t5], [self.tS])
            P.op("pool", lambda e, sb=sb, h=h: e.tensor_copy(sb[:, h * 64:(h + 1) * 64], self.S[:, h * 64:(h + 1) * 64]),
                 [self.tS], [tsb])

    def post(self, b, i, tok0):
        P = self.P
        b6, t6 = P.psum[6], P.pbank[6]
        P.op("dve", lambda e: e.scalar_tensor_tensor(self.yt, self.xc, self.mc[:, 21:22], b6[:, :], ALU.mult, ALU.add),
             [self.txc, self.tmc, t6], [self.tyt])
        if b == 0 and i == 0:
            P.dump("yt", self.yt, self.tyt); P.dump("S", self.S, self.tS)
        P.op("pool", lambda e: e.tensor_tensor(self.yg, self.yt, self.sz, ALU.mult), [self.tyt, self.tsz], [self.tyg])
        P.dma("act", self.out_d[64:192, tok0:tok0 + T], self.yg, [self.tyg], [self.t_out])

    def attention(self, b, i, filler):
        P = self.P
        tok0 = b * SEQ + i * T
        b7, t7 = P.psum[7], P.pbank[7]
        nblk = 4 * i + 4
        qs = self.QT[0:64, i * T:(i + 1) * T]
        V3 = self.V.rearrange("p (n d) -> p n d", d=64)
        abanks = [0, 1, 2, 3]

        def st_z(n):
            kb = nblk - 1 - n
            j = kb - 4 * i
            diag = j >= 0
            ks = self.KT[0:64, kb * 128:(kb + 1) * 128]
            ab = abanks[n % 4]
            pa, ta = P.psum[ab], P.pbank[ab]
            ez, tez = self.ez[n % 2]
            L, tL = self.L[n % 4]
            P.op("pe", lambda e: e.matmul(pa[:, :], ks, qs, start=True, stop=False), [self.tKT[kb // 4], self.tQT[i]], [ta])
            if diag:
                P.op("pe", lambda e: e.matmul(pa[:, :], self.identb, self.amask[j], start=False, stop=False), [self.tcb], [ta])
            P.op("act", lambda e: e.activation(ez, pa[:, :], AF.Exp), [ta], [tez])
            P.op("act", lambda e: e.activation(L, ez, AF.Ln, bias=1.0), [tez], [tL])

        def st_a(n):
            kb = nblk - 1 - n
            ab = abanks[n % 4]
            pa, ta = P.psum[ab], P.pbank[ab]
            L, tL = self.L[n % 4]
            W, tW = self.W[n % 4]
            P.op("pe", lambda e: e.matmul(pa[:, :], self.ntril, L, start=False, stop=(n == 0)), [self.tcb, tL], [ta])
            ls, tls = self.Lsum[n % 3]
            ln_, tln = self.Lsum[(n + 1) % 3]
            if n > 0:
                P.op("pe", lambda e: e.matmul(pa[:, :], self.nones, ls, start=False, stop=True), [self.tcb, tls], [ta])
            P.op("act", lambda e: e.activation(W, pa[:, :], AF.Exp), [ta], [tW])
            if kb > 0:
                if n == 0:
                    P.op("dve", lambda e: e.tensor_copy(ln_, L), [tL], [tln])
                else:
                    P.op("dve", lambda e: e.tensor_tensor(ln_, ls, L, ALU.add), [tL, tls], [tln])

        def st_v(n):
            kb = nblk - 1 - n
            W, tW = self.W[n % 4]
            P.op("pe", lambda e: e.matmul(b7[0:64, :], V3[:, kb, :], W, start=(n == 0), stop=(n == nblk - 1)),
                 [self.tV[kb // 4], tW], [t7])

        units = list(filler)
        per = -(-len(units) // nblk) if units else 0
        SK = 2
        for sidx in range(nblk + 2 * SK):
            if sidx < nblk:
                st_z(sidx)
            if SK <= sidx < nblk + SK:
                st_a(sidx - SK)
            if sidx >= 2 * SK:
                st_v(sidx - 2 * SK)
            for _ in range(per):
                if units:
                    units.pop(0)()
        while units:
            units.pop(0)()
        P.op("act", lambda e: e.copy(self.ob[0:64, :], b7[0:64, :]), [t7], [self.tob])
        P.dma("act", self.out_d[192:256, tok0:tok0 + T], self.ob[0:64, :], [self.tob], [self.t_out])

    def emit(self):
        self.setup()
        for b in range(self.nseq):
            for u in self.proj_units(b, 0) + self.mid_units(b, 0):
                u()
            for i in range(self.ntile):
                nxt = (self.proj_units(b, i + 1) + self.mid_units(b, i + 1)) if i + 1 < self.ntile else []
                self.attention(b, i, nxt)


def colsT(v):
    return np.ascontiguousarray(np.asarray(v, np.float32).reshape(-1, 128).T)


_CONST = {}


def mixer_consts():
    if "cf" not in _CONST:
        k = np.arange(128)
        triu = (k[:, None] <= k[None, :]).astype(np.float32)
        ident = np.eye(128, dtype=np.float32)
        smask = np.where(k[:, None] > k[None, :], NEG, 0.0).astype(np.float32)
        ones = np.ones((128, 128), np.float32)
        _CONST["cf"] = np.concatenate([triu, ident, smask, ones], 1)
        ntril = -(k[:, None] >= k[None, :]).astype(np.float32)
        t = np.arange(T)
        am = [np.where(128 * j + k[:, None] >= t[None, :], NEG, 0.0).astype(np.float32) for j in range(4)]
        _CONST["cb"] = np.concatenate([ident, ntril, -ones] + am, 1).astype(ml_dtypes.bfloat16)
    return _CONST["cf"], _CONST["cb"]


def mixer_inputs(c, w_in, pool_w, pool_scale, conv_w, conv_b, dt_bias, a_log, d_skip):
    g = c // 2
    bc = c // 4
    XB = 1536
    colsel = np.concatenate([
        np.arange(128 * g, 128 * g + 128),
        np.arange(512 + 128 * c, 512 + 128 * c + 128),
        np.arange(XB + 128 * c, XB + 128 * c + 128),
        np.arange(XB + 1024 + 128 * bc, XB + 1024 + 128 * bc + 128),
        np.arange(XB + 1280 + 128 * bc, XB + 1280 + 128 * bc + 128),
        np.arange(3088 + 64 * c, 3088 + 64 * c + 64),
        np.arange(3600 + 64 * c, 3600 + 64 * c + 64),
        np.arange(4112 + 64 * c, 4112 + 64 * c + 64),
        np.arange(3072 + 2 * c, 3072 + 2 * c + 2),
    ])
    wsel = np.ascontiguousarray(w_in[:, colsel])
    poolw = np.ascontiguousarray(pool_w[g][:, 64 * (c % 2):64 * (c % 2) + 64])
    mc = np.zeros((128, NMC), np.float32)
    chx = np.arange(128 * c, 128 * c + 128)
    chB = np.arange(1024 + 128 * bc, 1024 + 128 * bc + 128)
    chC = np.arange(1280 + 128 * bc, 1280 + 128 * bc + 128)
    for gi, ch in enumerate((chx, chB, chC)):
        for kk in range(4):
            mc[:, 4 * gi + kk] = conv_w[kk, ch]
        mc[:, 12 + gi] = conv_b[ch]
    mc[0:64, 15] = pool_scale[128 * g + 64 * (c % 2):128 * g + 64 * (c % 2) + 64]
    mc[:, 16 + g] = 1.0
    w = 2 ** (g + 1)
    mc[:, 20] = 1.0 / w
    mc[0:64, 21] = d_skip[2 * c]
    mc[64:128, 21] = d_skip[2 * c + 1]
    for j in range(4):
        for h in range(2):
            mc[:, 22 + 2 * j + h] = dt_bias[2 * c + h]
            mc[:, 30 + 2 * j + h] = a_log[2 * c + h]
    invc = np.broadcast_to(1.0 / np.minimum(np.arange(1, T + 1), w).astype(np.float32), (128, T)).copy()
    cf, cb = mixer_consts()
    return {"wsel": wsel, "poolw": poolw, "mc": mc, "invc": invc, "cf": cf, "cb": cb}


_PROGS = {}


def get_chain(n_ffn, has_mix, epi):
    key = ("chain", n_ffn, has_mix, epi)
    if key not in _PROGS:
        P = Prog()
        Chain(P, n_ffn, has_mix, epi).emit()
        _PROGS[key] = P.finish()
    return _PROGS[key]


def get_mixer():
    key = ("mixer",)
    if key not in _PROGS:
        P = Prog()
        Mixer(P).emit()
        _PROGS[key] = P.finish()
    return _PROGS[key]


NCORE = 8


def kernel(x, ffn1_norm, ffn1_w_gate, ffn1_w_up, ffn1_w_down, mix_norm, w_in, pool_w, pool_scale,
           conv_w, conv_b, dt_bias, a_log, d_skip, ssd_norm, w_out, ffn2_norm, ffn2_w_gate,
           ffn2_w_up, ffn2_w_down, final_norm):
    f = lambda a: np.asarray(a, dtype=np.float32)
    x = f(x)
    depth = w_in.shape[0]
    xt = x.reshape(-1, D)
    cores = list(range(NCORE))
    z8 = np.zeros((128, 8), np.float32)
    nc = get_chain(1, False, "u")
    cv = np.concatenate([colsT(f(ffn1_norm[0])), colsT(f(mix_norm[0])), z8], 1)
    maps = []
    for c in cores:
        maps.append({"h_in": np.ascontiguousarray(xt[c * NT:(c + 1) * NT].T), "cvec": cv,
                     "wg0": f(ffn1_w_gate[0]), "wu0": f(ffn1_w_up[0]), "wd0": f(ffn1_w_down[0])})
    res = run_bass_kernel_spmd(nc, maps, core_ids=cores)
    h = [res.results[c]["h_out"] for c in cores]
    u = [res.results[c]["u_out"] for c in cores]
    out = None
    for l in range(depth):
        uT = np.ascontiguousarray(np.concatenate([np.asarray(a) for a in u], axis=1))
        nc = get_mixer()
        maps = []
        for c in cores:
            m = mixer_inputs(c, f(w_in[l]), f(pool_w[l]), f(pool_scale[l]), f(conv_w[l]), f(conv_b[l]),
                             f(dt_bias[l]), f(a_log[l]), f(d_skip[l]))
            m["uT"] = uT
            maps.append(m)
        res = run_bass_kernel_spmd(nc, maps, core_ids=cores)
        mixT = np.empty((D, NCORE * NT), dtype=ml_dtypes.bfloat16)
        for c in cores:
            mo = np.asarray(res.results[c]["mixo"])
            mixT[64 * c:64 * c + 64] = mo[0:64]
            mixT[512 + 128 * c:512 + 128 * c + 128] = mo[64:192]
            mixT[1536 + 64 * c:1536 + 64 * c + 64] = mo[192:256]
        last = (l == depth - 1)
        if not last:
            nc = get_chain(2, True, "u")
            cv = np.concatenate([colsT(f(ffn2_norm[l])), colsT(f(ffn1_norm[l + 1])), colsT(f(mix_norm[l + 1])),
                                 colsT(f(ssd_norm[l]))], 1)
        else:
            nc = get_chain(1, True, "final")
            cv = np.concatenate([colsT(f(ffn2_norm[l])), colsT(f(final_norm)), colsT(f(ssd_norm[l]))], 1)
        maps = []
        for c in cores:
            m = {"h_in": h[c], "cvec": cv, "mixT": np.ascontiguousarray(mixT[:, c * NT:(c + 1) * NT]),
                 "wout": f(w_out[l]),
                 "wg0": f(ffn2_w_gate[l]), "wu0": f(ffn2_w_up[l]), "wd0": f(ffn2_w_down[l])}
            if not last:
                m.update({"wg1": f(ffn1_w_gate[l + 1]), "wu1": f(ffn1_w_up[l + 1]), "wd1": f(ffn1_w_down[l + 1])})
            maps.append(m)
        res = run_bass_kernel_spmd(nc, maps, core_ids=cores)
        if not last:
            h = [res.results[c]["h_out"] for c in cores]
            u = [res.results[c]["u_out"] for c in cores]
        else:
            out = np.concatenate([np.asarray(res.results[c]["o_out"]).T for c in cores], axis=0)
    return np.ascontiguousarray(out.reshape(x.shape).astype(np.float32))
```

```python
import numpy as np
import ml_dtypes
from contextlib import ExitStack
import concourse.bass as bass
import concourse.mybir as mybir
from concourse.bass_utils import run_bass_kernel_spmd

F32 = mybir.dt.float32
BF16 = mybir.dt.bfloat16
AF = mybir.ActivationFunctionType
ALU = mybir.AluOpType
AX = mybir.AxisListType

ENG = ("pe", "act", "dve", "pool", "sp")
NDQ = 8
SAME_ENGINE_SYNC = True


class Tk:
    __slots__ = ("name", "w", "r", "excl")

    def __init__(self, name="", excl=False):
        self.name = name
        self.w = None
        self.r = {}
        self.excl = excl


class Prog:
    def __init__(self, arena_f32=49152):
        self.nc = bass.Bass("TRN2", target_bir_lowering=False)
        self.es = ExitStack()
        self.ops = {e: [] for e in ENG}
        self.cnt = {e: 0 for e in ENG}
        self.dcnt = {}
        self.dnext = {q: 0 for q in ("sp", "act", "pool")}
        self.seen = {e: {} for e in ENG}
        self.sems = {}
        nc = self.nc
        for e in ENG:
            self.sems[e] = self.es.enter_context(nc.semaphore("s_" + e))
        for q in ("sp", "act", "pool"):
            for j in range(NDQ):
                k = "d_%s_%d" % (q, j)
                self.sems[k] = self.es.enter_context(nc.semaphore(k))
                self.dcnt[k] = 0
        self.arena = self.es.enter_context(nc.sbuf_tensor("arena", [128, arena_f32], F32))
        self.arena_n = arena_f32
        self.aoff = 0
        self.psum = []
        self.pbank = []
        for i in range(8):
            t = self.es.enter_context(nc.psum_tensor("ps%d" % i, [128, 512], F32))
            self.psum.append(t)
            self.pbank.append(Tk("ps%d" % i, excl=True))
        self.n_inst = 0

    def reset_arena(self, keep=0):
        self.aoff = keep

    def alloc(self, name, cols, dtype=F32):
        nf = cols if dtype == F32 else (cols + 1) // 2
        nf = (nf + 7) // 8 * 8
        assert self.aoff + nf <= self.arena_n, ("arena overflow", name, self.aoff, nf)
        ap = self.arena[:, self.aoff:self.aoff + nf]
        self.aoff += nf
        if dtype != F32:
            ap = ap.bitcast(dtype)[:, 0:cols]
        else:
            ap = ap[:, 0:cols]
        return ap, Tk(name)

    def dram(self, name, shape, dtype, kind="Internal"):
        return self.nc.dram_tensor(name, list(shape), dtype, kind=kind).ap()

    def _waits(self, e, reads, writes):
        waits = {}

        def need(dep):
            if dep is None:
                return
            k, v = dep
            if k == e and (e == "pe" or not SAME_ENGINE_SYNC):
                return
            if waits.get(k, 0) < v:
                waits[k] = v

        for t in reads:
            need(t.w)
            if t.excl:
                for k, v in t.r.items():
                    need((k, v))
        for t in writes:
            need(t.w)
            for k, v in t.r.items():
                need((k, v))
        wl = []
        for k, v in waits.items():
            if self.seen[e].get(k, 0) < v:
                self.seen[e][k] = v
                wl.append((k, v))
        return wl

    def op(self, e, fn, reads=(), writes=()):
        wl = self._waits(e, reads, writes)
        self.cnt[e] += 1
        c = self.cnt[e]
        sems = self.sems
        semE = sems[e]

        def emit(eng):
            for k, v in wl:
                eng.wait_ge(sems[k], v)
            fn(eng).then_inc(semE, 1)

        self.ops[e].append(emit)
        self.n_inst += 1 + len(wl)
        for t in reads:
            if t.excl:
                t.w = (e, c)
                t.r = {}
            else:
                t.r[e] = c
        for t in writes:
            t.w = (e, c)
            t.r = {}

    def dma(self, q, out_ap, in_ap, reads=(), writes=()):
        wl = self._waits(q, reads, writes)
        j = self.dnext[q]
        self.dnext[q] = (j + 1) % NDQ
        key = "d_%s_%d" % (q, j)
        prev = self.dcnt[key]
        if prev > 0 and self.seen[q].get(key, 0) < prev:
            self.seen[q][key] = prev
            wl.append((key, prev))
        self.dcnt[key] = prev + 16
        v = prev + 16
        sems = self.sems

        def emit(eng):
            for k, vv in wl:
                eng.wait_ge(sems[k], vv)
            eng.dma_start(out=out_ap, in_=in_ap).then_inc(sems[key], 16)

        self.ops[q].append(emit)
        self.n_inst += 1 + len(wl)
        for t in reads:
            t.r[key] = v
        for t in writes:
            t.w = (key, v)
            t.r = {}

    def dump(self, name, ap, tk, dtype=F32):
        if not getattr(self, "debug", False):
            return
        d = self.dram("dbg_" + name, [ap.shape[0], ap.shape[1]], dtype, "ExternalOutput")
        self.dma("sp", d, ap, [tk], [Tk()])

    def barrier(self):
        cur = dict(self.cnt)
        cur.update(self.dcnt)
        sems = self.sems
        for e in ENG:
            wl = []
            for k, v in cur.items():
                if k != e and v > self.seen[e].get(k, 0):
                    self.seen[e][k] = v
                    wl.append((k, v))

            def emit(eng, wl=wl):
                for k, v in wl:
                    eng.wait_ge(sems[k], v)

            self.ops[e].append(emit)
            self.n_inst += len(wl)

    def finish(self):
        self.barrier()
        nc = self.nc
        ops = self.ops
        with nc.Block() as block:
            @block.tensor
            def _(eng):
                for f in ops["pe"]:
                    f(eng)

            @block.scalar
            def _(eng):
                for f in ops["act"]:
                    f(eng)

            @block.vector
            def _(eng):
                for f in ops["dve"]:
                    f(eng)

            @block.gpsimd
            def _(eng):
                for f in ops["pool"]:
                    f(eng)

            @block.sync
            def _(eng):
                for f in ops["sp"]:
                    f(eng)
        self.es.close()
        return nc


D = 2048
DFF = 5632
NT = 2048
T = 512
KD = D // 128
KF = DFF // 128
EPS = 1e-6
WB = 8192
NWB = 5


def v3(ap, k):
    return ap.rearrange("p (k t) -> p k t", k=k)


class Chain:
    def __init__(self, P, n_ffn, has_mix, epilogue):
        self.P = P
        nc = P.nc
        self.n_ffn, self.has_mix, self.epi = n_ffn, has_mix, epilogue
        self.h_in = P.dram("h_in", [D, NT], F32, "ExternalInput")
        self.t_hin = Tk("h_in")
        ncv = 16 * (n_ffn + 1) + 8
        self.ncv = ncv
        self.cv_d = P.dram("cvec", [128, ncv], F32, "ExternalInput")
        self.w32 = []
        self.wbf = []
        self.twb = []
        for i in range(n_ffn):
            for nm, shp in (("wg", [D, DFF]), ("wu", [D, DFF]), ("wd", [DFF, D])):
                self.w32.append(P.dram("%s%d" % (nm, i), shp, F32, "ExternalInput"))
                self.wbf.append(P.dram("%s%d_bf" % (nm, i), shp, BF16))
                self.twb.append([Tk() for _ in range(44)])
        if has_mix:
            self.mix_d = P.dram("mixT", [D, NT], BF16, "ExternalInput")
            self.wo32 = P.dram("wout", [D, D], F32, "ExternalInput")
            self.wobf = P.dram("wout_bf", [D, D], BF16)
            self.two = [Tk() for _ in range(KD)]
        if epilogue == "u":
            self.h_out = P.dram("h_out", [D, NT], F32, "ExternalOutput")
            self.u_out = P.dram("u_out", [D, NT], BF16, "ExternalOutput")
        else:
            self.o_out = P.dram("o_out", [D, NT], F32, "ExternalOutput")
        self.t_out = Tk("out")
        self.cv, self.tcv = P.alloc("cv", ncv)
        self.ones, self.tones = P.alloc("ones", 128, BF16)
        self.h, self.th = P.alloc("h", KD * T)
        self.u, self.tu = P.alloc("u", KD * T, BF16)
        self.act, self.tact = P.alloc("act", KF * T, BF16)
        self.sq = [P.alloc("sq%d" % i, T, BF16) for i in range(2)]
        self.sg = [P.alloc("sg%d" % i, T) for i in range(2)]
        self.rs, self.trs = P.alloc("rs", T)
        self.wb = [P.alloc("wb%d" % i, WB, BF16) for i in range(NWB)]
        self.wbi = 0
        self.tk_h = [Tk("h%d" % k) for k in range(KD)]
        self.tk_u = [Tk("u%d" % k) for k in range(KD)]
        self.tk_a = [Tk("a%d" % k) for k in range(KF)]
        self.alt = 0

    def nextwb(self):
        w = self.wb[self.wbi]
        self.wbi = (self.wbi + 1) % NWB
        return w

    def ew(self):
        self.alt ^= 1
        return "dve" if self.alt else "pool"

    def cast_weights(self):
        P = self.P
        P.op("pool", lambda e: e.memset(self.ones, 1.0), [], [self.tones])
        P.dma("sp", self.cv, self.cv_d, [], [self.tcv])
        if self.has_mix:
            for k in range(KD):
                P.dma("pool", self.wobf[k * 128:(k + 1) * 128, :], self.wo32[k * 128:(k + 1) * 128, :],
                      [], [self.two[k]])
        for fi in range(self.n_ffn):
            for fg in range(KF // 4):
                for mi in (3 * fi, 3 * fi + 1):
                    for rq in range(4):
                        P.dma("pool", self.wbf[mi][rq * 512:(rq + 1) * 512, fg * 512:(fg + 1) * 512],
                              self.w32[mi][rq * 512:(rq + 1) * 512, fg * 512:(fg + 1) * 512], [], [self.twb[mi][fg * 4 + rq]])
                mi = 3 * fi + 2
                for k in range(fg * 4, fg * 4 + 4):
                    P.dma("pool", self.wbf[mi][k * 128:(k + 1) * 128, :], self.w32[mi][k * 128:(k + 1) * 128, :],
                          [], [self.twb[mi][k]])

    def norm_stats(self, src3, tks, idxs, nfeat, bank):
        P = self.P
        ps, tps = P.psum[bank], P.pbank[bank]
        n = len(idxs)
        for i, k in enumerate(idxs):
            sq, tsq = self.sq[i % 2]
            P.op("act", lambda e, k=k, sq=sq: e.activation(sq, src3[:, k, :], AF.Square), [tks[k]], [tsq])
            P.op("pe", lambda e, i=i, sq=sq: e.matmul(ps[:, :], self.ones, sq, start=(i == 0), stop=(i == n - 1)),
                 [self.tones, tsq], [tps])
        P.op("act", lambda e: e.activation(self.rs, ps[:, :], AF.Sqrt, bias=EPS, scale=1.0 / nfeat), [tps], [self.trs])
        P.op("dve", lambda e: e.reciprocal(self.rs, self.rs), [self.trs], [self.trs])

    def rmsnorm(self, gcol0, dst3, tdst):
        P = self.P
        h3 = v3(self.h, KD)
        self.norm_stats(h3, self.tk_h, list(range(KD)), D, 4)
        for k in range(KD):
            P.op("dve", lambda e, k=k: e.scalar_tensor_tensor(
                dst3[:, k, :], h3[:, k, :], self.cv[:, gcol0 + k:gcol0 + k + 1], self.rs, ALU.mult, ALU.mult),
                [self.tk_h[k], self.tcv, self.trs], tdst[k] if isinstance(tdst[k], list) else [tdst[k]])

    def ffn(self, i):
        P = self.P
        h3 = v3(self.h, KD)
        u3 = v3(self.u, KD)
        a3 = v3(self.act, KF)
        wg, wu, wd = self.wbf[3 * i], self.wbf[3 * i + 1], self.wbf[3 * i + 2]
        twg, twu, twd = self.twb[3 * i], self.twb[3 * i + 1], self.twb[3 * i + 2]
        self.rmsnorm(16 * i, u3, self.tk_u)
        wg3 = wg.rearrange("(k p) f -> p k f", p=128)
        wu3 = wu.rearrange("(k p) f -> p k f", p=128)
        wd3 = wd.rearrange("(k p) f -> p k f", p=128)
        gi = 0
        for fg in range(KF // 4):
            (wa, twa), (wb_, twb_) = self.nextwb(), self.nextwb()
            wa3, wb3 = v3(wa, KD), v3(wb_, KD)
            P.dma("sp", wa3, wg3[:, :, fg * 512:(fg + 1) * 512], twg[fg * 4:fg * 4 + 4], [twa])
            P.dma("sp", wb3, wu3[:, :, fg * 512:(fg + 1) * 512], twu[fg * 4:fg * 4 + 4], [twb_])
            for f4 in range(4):
                f = fg * 4 + f4
                bg, bu = (0, 1) if gi % 2 == 0 else (2, 3)
                gi += 1
                pg, pu = P.psum[bg], P.psum[bu]
                for k in range(KD):
                    P.op("pe", lambda e, k=k, f4=f4, pg=pg, wa3=wa3: e.matmul(
                        pg[:, :], wa3[:, k, f4 * 128:(f4 + 1) * 128], u3[:, k, :], start=(k == 0), stop=(k == KD - 1)),
                        [twa, self.tk_u[k]], [P.pbank[bg]])
                for k in range(KD):
                    P.op("pe", lambda e, k=k, f4=f4, pu=pu, wb3=wb3: e.matmul(
                        pu[:, :], wb3[:, k, f4 * 128:(f4 + 1) * 128], u3[:, k, :], start=(k == 0), stop=(k == KD - 1)),
                        [twb_, self.tk_u[k]], [P.pbank[bu]])
                sg, tsg = self.sg[f % 2]
                P.op("act", lambda e, sg=sg, pg=pg: e.activation(sg, pg[:, :], AF.Silu), [P.pbank[bg]], [tsg])
                P.op("dve", lambda e, sg=sg, pu=pu, f=f: e.tensor_tensor(a3[:, f, :], sg, pu[:, :], ALU.mult),
                     [tsg, P.pbank[bu]], [self.tk_a[f]])
        FD = 11
        for dg in range(4):
            banks = [4, 5, 6, 7] if dg % 2 == 0 else [0, 1, 2, 3]
            for fgd in range(KF // FD):
                w, tw = self.nextwb()
                w3 = w[:, 0:FD * 512].rearrange("p (k t) -> p k t", k=FD)
                P.dma("sp", w3, wd3[:, fgd * FD:(fgd + 1) * FD, dg * 512:(dg + 1) * 512],
                      twd[fgd * FD:(fgd + 1) * FD], [tw])
                for j in range(4):
                    pb = P.psum[banks[j]]
                    for f in range(FD):
                        ff = fgd * FD + f
                        P.op("pe", lambda e, j=j, f=f, ff=ff, pb=pb, w3=w3: e.matmul(
                            pb[:, :], w3[:, f, j * 128:(j + 1) * 128], a3[:, ff, :],
                            start=(ff == 0), stop=(ff == KF - 1)),
                            [tw, self.tk_a[ff]], [P.pbank[banks[j]]])
            for j in range(4):
                c = dg * 4 + j
                pb = P.psum[banks[j]]
                P.op("dve", lambda e, c=c, pb=pb: e.scalar_tensor_tensor(
                    h3[:, c, :], pb[:, :], 0.5, h3[:, c, :], ALU.mult, ALU.add),
                    [P.pbank[banks[j]], self.tk_h[c]], [self.tk_h[c]])

    def mix_stage(self, t0):
        P = self.P
        h3 = v3(self.h, KD)
        m3 = v3(self.u, KD)
        P.dma("sp", m3, self.mix_d.rearrange("(k p) t -> p k t", p=128)[:, :, t0:t0 + T], [], self.tk_u)
        gc0 = 16 * (self.n_ffn + 1)
        for grp in range(2):
            idxs = [4 + grp * 4 + c for c in range(4)]
            self.norm_stats(m3, self.tk_u, idxs, 512, 4)
            for c in idxs:
                P.op("dve", lambda e, c=c: e.scalar_tensor_tensor(
                    m3[:, c, :], m3[:, c, :], self.cv[:, gc0 + c - 4:gc0 + c - 3], self.rs, ALU.mult, ALU.mult),
                    [self.tk_u[c], self.tcv, self.trs], [self.tk_u[c]])
        wo3 = self.wobf.rearrange("(k p) f -> p k f", p=128)
        for dg in range(4):
            banks = [0, 1, 2, 3] if dg % 2 == 0 else [4, 5, 6, 7]
            w, tw = self.nextwb()
            w3 = v3(w, KD)
            P.dma("sp", w3, wo3[:, :, dg * 512:(dg + 1) * 512], self.two, [tw])
            for j in range(4):
                pb = P.psum[banks[j]]
                for k in range(KD):
                    P.op("pe", lambda e, j=j, k=k, pb=pb, w3=w3: e.matmul(
                        pb[:, :], w3[:, k, j * 128:(j + 1) * 128], m3[:, k, :], start=(k == 0), stop=(k == KD - 1)),
                        [tw, self.tk_u[k]], [P.pbank[banks[j]]])
            for j in range(4):
                c = dg * 4 + j
                pb = P.psum[banks[j]]
                P.op("dve", lambda e, c=c, pb=pb: e.tensor_tensor(h3[:, c, :], pb[:, :], h3[:, c, :], ALU.add),
                     [P.pbank[banks[j]], self.tk_h[c]], [self.tk_h[c]])

    def emit(self):
        P = self.P
        self.cast_weights()
        h3 = v3(self.h, KD)
        hin3 = self.h_in.rearrange("(k p) t -> p k t", p=128)
        for it in range(NT // T):
            t0 = it * T
            P.dma("sp", h3, hin3[:, :, t0:t0 + T], [self.t_hin], self.tk_h)
            if self.has_mix:
                self.mix_stage(t0)
            for i in range(self.n_ffn):
                self.ffn(i)
            gc = 16 * self.n_ffn
            if self.epi == "u":
                P.dma("act", self.h_out.rearrange("(k p) t -> p k t", p=128)[:, :, t0:t0 + T], h3, self.tk_h, [self.t_out])
                u3 = v3(self.u, KD)
                self.rmsnorm(gc, u3, self.tk_u)
                P.dma("act", self.u_out.rearrange("(k p) t -> p k t", p=128)[:, :, t0:t0 + T], u3, self.tk_u, [self.t_out])
            else:
                o3 = v3(self.act.bitcast(F32)[:, 0:KD * T], KD)
                self.rmsnorm(gc, o3, [[self.tk_a[2 * k], self.tk_a[2 * k + 1]] for k in range(KD)])
                P.dma("act", self.o_out.rearrange("(k p) t -> p k t", p=128)[:, :, t0:t0 + T], o3, self.tk_a[0:2 * KD], [self.t_out])


SEQ = 8192
NSEQ = 2
NTILE = SEQ // T
WSEL = 834
C_POOL, C_Z, C_X, C_B, C_C, C_Q, C_K, C_V, C_DT = 0, 128, 256, 384, 512, 640, 704, 768, 832
NMC = 38
NEG = -30000.0


class Mixer:
    def __init__(self, P, nseq=NSEQ, ntile=NTILE):
        self.P = P
        self.nseq, self.ntile = nseq, ntile
        ntok = nseq * SEQ
        self.u_d = P.dram("uT", [D, ntok], BF16, "ExternalInput")
        self.w32 = P.dram("wsel", [D, WSEL], F32, "ExternalInput")
        self.wbf_d = P.dram("wsel_bf", [D, WSEL], BF16)
        self.pw_d = P.dram("poolw", [128, 64], F32, "ExternalInput")
        self.mc_d = P.dram("mc", [128, NMC], F32, "ExternalInput")
        self.invc_d = P.dram("invc", [128, T], F32, "ExternalInput")
        self.cf_d = P.dram("cf", [128, 4 * 128], F32, "ExternalInput")
        self.cb_d = P.dram("cb", [128, 3 * 128 + 4 * T], BF16, "ExternalInput")
        self.out_d = P.dram("mixo", [256, ntok], BF16, "ExternalOutput")
        self.t_out = Tk("mixo")
        self.twd = [Tk() for _ in range(KD)]
        A = P.alloc
        self.wsb, self.twsb = A("wsb", KD * WSEL, BF16)
        self.ub = [A("ub%d" % i, KD * T, BF16) for i in range(2)]
        self.QT, _ = A("QT", SEQ, BF16)
        self.KT, _ = A("KT", SEQ, BF16)
        self.V, _ = A("V", 64 * 64, BF16)
        self.tQT = [Tk() for _ in range(NTILE)]
        self.tKT = [Tk() for _ in range(NTILE)]
        self.tV = [Tk() for _ in range(NTILE)]
        self.pbk = 0
        self.mc, self.tmc = A("mc", NMC)
        self.invc, self.tinvc = A("invc", T)
        self.cf, self.tcf = A("cf", 4 * 128)
        self.cb, self.tcb = A("cb", 3 * 128 + 4 * T, BF16)
        self.pw32, self.tpw32 = A("pw32", 64)
        self.pwb, self.tpwb = A("pwb", 64, BF16)
        self.Abc, self.tAbc = A("Abc", 8)
        self.ve2 = [A("ve%d" % p, 15 + T) for p in range(2)]
        self.s = [A("s%d" % i, 15 + T) for i in range(4)]
        self.res, self.tres = A("res", T)
        self.pdiff, self.tpdiff = A("pdiff", T, BF16)
        self.po, self.tpo = A("po", T, BF16)
        self.xe2 = [[A("xe%d%d" % (p, i), 3 + T) for i in range(3)] for p in range(2)]
        self.acc = [A("acc%d" % i, T) for i in range(3)]
        self.xc, self.txc = A("xc", T)
        self.BTb, self.tBTb = A("BTb", T, BF16)
        self.CTf, self.tCTf = A("CTf", T)
        self.CTb, self.tCTb = A("CTb", T, BF16)
        self.sz2 = [A("sz%d" % p, T) for p in range(2)]
        self.dtr2 = [A("dtr%d" % p, 8) for p in range(2)]
        self.dx, self.tdx = A("dx", 8)
        self.dax, self.tdax = A("dax", 8)
        self.dt, self.tdt = A("dt", 8)
        self.aa, self.taa = A("aa", 8)
        self.abc = [[A("abc%d%d" % (sl, i), 128) for i in range(2)] for sl in range(2)]
        self.nacs = [A("nacs%d" % sl, 2) for sl in range(2)]
        self.d2 = [A("d2%d" % sl, 2) for sl in range(2)]
        self.w2 = [A("w2%d" % sl, 2) for sl in range(2)]
        self.dtw = [A("dtw%d" % sl, 2) for sl in range(2)]
        self.E = [[A("E%d%d" % (sl, i), 128) for i in range(2)] for sl in range(2)]
        self.Dm = [[A("Dm%d%d" % (sl, i), 128) for i in range(2)] for sl in range(2)]
        self.M = [[A("M%d%d" % (sl, i), 128, BF16) for i in range(2)] for sl in range(2)]
        self.Cs = [[A("Cs%d%d" % (sl, i), 128, BF16) for i in range(2)] for sl in range(2)]
        self.xdtp = [[A("xdtp%d%d" % (sl, i), 128, BF16) for i in range(2)] for sl in range(2)]
        self.xdtw = [A("xdtw%d" % sl, 128, BF16) for sl in range(2)]
        self.Btok = [A("Btok%d" % sl, 128, BF16) for sl in range(2)]
        self.S, self.tS = A("S", 128)
        self.Sbp = [A("Sbp%d" % i, 128, BF16) for i in range(2)]
        self.yt, self.tyt = A("yt", T)
        self.yg, self.tyg = A("yg", T, BF16)
        self.ez = [A("ez%d" % i, T) for i in range(2)]
        self.L = [A("L%d" % i, T, BF16) for i in range(4)]
        self.W = [A("W%d" % i, T, BF16) for i in range(4)]
        self.Lsum = [A("Lsum%d" % i, T, BF16) for i in range(3)]
        self.ob, self.tob = A("ob", T, BF16)
        self.blk = 0

    def setup(self):
        P = self.P
        for k in range(KD):
            P.dma("pool", self.wbf_d[k * 128:(k + 1) * 128, :], self.w32[k * 128:(k + 1) * 128, :], [], [self.twd[k]])
        P.dma("sp", v3(self.wsb, KD), self.wbf_d.rearrange("(k p) f -> p k f", p=128), self.twd, [self.twsb])
        P.dma("sp", self.mc, self.mc_d, [], [self.tmc])
        P.dma("sp", self.invc, self.invc_d, [], [self.tinvc])
        P.dma("sp", self.cf, self.cf_d, [], [self.tcf])
        P.dma("sp", self.cb, self.cb_d, [], [self.tcb])
        P.dma("sp", self.pw32, self.pw_d, [], [self.tpw32])
        P.op("act", lambda e: e.copy(self.pwb, self.pw32), [self.tpw32], [self.tpwb])
        P.op("act", lambda e: e.activation(self.Abc, self.mc[:, 30:38], AF.Exp), [self.tmc], [self.tAbc])
        P.op("dve", lambda e: e.tensor_scalar(self.Abc, self.Abc, -1.0, None, ALU.mult), [self.tAbc], [self.tAbc])
        for sl in range(2):
            for i in range(2):
                x, t = self.xdtp[sl][i]
                P.op("pool", lambda e, x=x: e.memset(x, 0.0), [], [t])
        self.triu = self.cf[:, 0:128]
        self.identf = self.cf[:, 128:256]
        self.smask = self.cf[:, 256:384]
        self.onesf = self.cf[:, 384:512]
        self.identb = self.cb[:, 0:128]
        self.ntril = self.cb[:, 128:256]
        self.nones = self.cb[:, 256:384]
        self.amask = [self.cb[:, 384 + j * T:384 + (j + 1) * T] for j in range(4)]

    def proj_units(self, b, i):
        P = self.P
        it = b * self.ntile + i
        tok0 = b * SEQ + i * T
        ub, tub = self.ub[it % 2]
        ub3 = v3(ub, KD)
        wsb3 = v3(self.wsb, KD)
        first = (i == 0)
        units = []
        pp = i % 2
        ve, tve = self.ve2[pp]
        xes = self.xe2[pp]
        sz, tsz = self.sz2[pp]
        dtr, tdtr = self.dtr2[pp]

        def u_dma():
            P.dma("sp", ub3, self.u_d.rearrange("(k p) t -> p k t", p=128)[:, :, tok0:tok0 + T], [], [tub])
            if first:
                P.op("pool", lambda e: e.memset(ve[:, 0:15], 0.0), [], [tve])
                for g in range(3):
                    xe, txe = xes[g]
                    P.op("pool", lambda e, xe=xe: e.memset(xe[:, 0:3], 0.0), [], [txe])
                P.op("pool", lambda e: e.memset(self.S, 0.0), [], [self.tS])
                for h in range(2):
                    sb, tsb = self.Sbp[h]
                    P.op("pool", lambda e, sb=sb: e.memset(sb, 0.0), [], [tsb])
        units.append(u_dma)

        def bank():
            bnk = 3
            self.pbk += 1
            return bnk

        def fm(c0, ncols, evac):
            def unit():
                bnk = bank()
                ps = P.psum[bnk]
                for k in range(KD):
                    P.op("pe", lambda e, k=k: e.matmul(ps[0:ncols, :], wsb3[:, k, c0:c0 + ncols], ub3[:, k, :],
                                                       start=(k == 0), stop=(k == KD - 1)),
                         [self.twsb, tub], [P.pbank[bnk]])
                evac(ps, P.pbank[bnk])
            units.append(unit)

        fm(C_POOL, 128, lambda ps, tp: P.op("dve", lambda e: e.tensor_copy(ve[:, 15:15 + T], ps[:, :]), [tp], [tve]))
        fm(C_Z, 128, lambda ps, tp: P.op("act", lambda e: e.activation(sz, ps[:, :], AF.Silu), [tp], [tsz]))
        for g, c0 in enumerate((C_X, C_B, C_C)):
            xe, txe = xes[g]
            fm(c0, 128, lambda ps, tp, xe=xe, txe=txe: P.op(
                "dve", lambda e: e.tensor_copy(xe[:, 3:3 + T], ps[:, :]), [tp], [txe]))
        fm(C_Q, 64, lambda ps, tp: P.op("dve", lambda e: e.tensor_scalar(
            self.QT[0:64, i * T:(i + 1) * T], ps[0:64, :], 0.125, None, ALU.mult), [tp], [self.tQT[i]]))
        fm(C_K, 64, lambda ps, tp: P.op("dve", lambda e: e.tensor_copy(
            self.KT[0:64, i * T:(i + 1) * T], ps[0:64, :]), [tp], [self.tKT[i]]))

        def tm():
            bnk = bank()
            ps = P.psum[bnk]
            for j in range(4):
                for k in range(KD):
                    P.op("pe", lambda e, k=k, j=j: e.matmul(ps[:, j * 66:(j + 1) * 66], ub3[:, k, j * 128:(j + 1) * 128],
                                                            wsb3[:, k, C_V:C_V + 66], start=(k == 0), stop=(k == KD - 1)),
                         [self.twsb, tub], [P.pbank[bnk]])
            V3 = self.V.rearrange("p (n d) -> p n d", d=64)
            ps3 = ps[:, 0:264].rearrange("p (j c) -> p j c", c=66)
            P.op("dve", lambda e: e.tensor_copy(V3[:, i * 4:(i + 1) * 4, :], ps3[:, :, 0:64]), [P.pbank[bnk]], [self.tV[i]])
            P.op("dve", lambda e: e.tensor_copy(dtr.rearrange("p (j c) -> p j c", c=2), ps3[:, :, 64:66]),
                 [P.pbank[bnk]], [tdtr])
        units.append(tm)
        return units

    def mid(self, b, i):
        P = self.P
        tok0 = b * SEQ + i * T
        first = (i == 0)
        pp = i % 2
        ve, tve = self.ve2[pp]
        ven, tven = self.ve2[1 - pp]
        xes, xesn = self.xe2[pp], self.xe2[1 - pp]
        self.cur_sz = self.sz2[pp]
        dtr, tdtr = self.dtr2[pp]
        sh = [1, 2, 4, 8]
        lo = [1, 3, 7, 15]
        prev, tprev = ve, tve
        for q in range(4):
            s, ts = self.s[q]
            P.op("pool", lambda e, s=s, prev=prev, q=q: e.tensor_tensor(
                s[:, lo[q]:15 + T], prev[:, lo[q]:15 + T], prev[:, lo[q] - sh[q]:15 + T - sh[q]], ALU.add),
                [tprev], [ts])
            prev, tprev = s, ts
        s0, ts0 = self.s[0]
        P.op("dve", lambda e: e.tensor_scalar(self.res, s0[:, 15:15 + T], self.mc[:, 16:17], None, ALU.mult),
             [ts0, self.tmc], [self.tres])
        for q in range(1, 4):
            s, ts = self.s[q]
            P.op("dve", lambda e, s=s, q=q: e.scalar_tensor_tensor(self.res, s[:, 15:15 + T], self.mc[:, 16 + q:17 + q],
                                                                    self.res, ALU.mult, ALU.add),
                 [ts, self.tmc, self.tres], [self.tres])
        if first:
            P.op("dve", lambda e: e.tensor_tensor(self.res, self.res, self.invc, ALU.mult), [self.tres, self.tinvc], [self.tres])
            P.op("dve", lambda e: e.tensor_tensor(self.pdiff, self.res, ve[:, 15:15 + T], ALU.subtract),
                 [self.tres, tve], [self.tpdiff])
        else:
            P.op("dve", lambda e: e.scalar_tensor_tensor(self.pdiff, self.res, self.mc[:, 20:21], ve[:, 15:15 + T],
                                                          ALU.mult, ALU.subtract),
                 [self.tres, self.tmc, tve], [self.tpdiff])
        P.op("pool", lambda e: e.tensor_copy(ven[:, 0:15], ve[:, T:T + 15]), [tve], [tven])
        bnk = 6
        ps = P.psum[bnk]
        P.op("pe", lambda e, ps=ps: e.matmul(ps[0:64, :], self.pwb, self.pdiff, start=True, stop=True),
             [self.tpwb, self.tpdiff], [P.pbank[bnk]])
        P.op("dve", lambda e, ps=ps: e.tensor_scalar(self.po[0:64, :], ps[0:64, :], self.mc[0:64, 15:16], None, ALU.mult),
             [P.pbank[bnk], self.tmc], [self.tpo])
        P.dma("act", self.out_d[0:64, tok0:tok0 + T], self.po[0:64, :], [self.tpo], [self.t_out])
        yield
        for g in range(3):
            xe, txe = xes[g]
            xen, txen = xesn[g]
            acc, tacc = self.acc[g]
            P.op("dve", lambda e, xe=xe, acc=acc, g=g: e.tensor_scalar(
                acc, xe[:, 3:3 + T], self.mc[:, 4 * g + 3:4 * g + 4], self.mc[:, 12 + g:13 + g], ALU.mult, ALU.add),
                [txe, self.tmc], [tacc])
            for kk in (2, 1, 0):
                P.op("dve", lambda e, xe=xe, acc=acc, g=g, kk=kk: e.scalar_tensor_tensor(
                    acc, xe[:, kk:kk + T], self.mc[:, 4 * g + kk:4 * g + kk + 1], acc, ALU.mult, ALU.add),
                    [txe, self.tmc, tacc], [tacc])
            P.op("pool", lambda e, xe=xe, xen=xen: e.tensor_copy(xen[:, 0:3], xe[:, T:T + 3]), [txe], [txen])
            yield
        P.op("act", lambda e: e.activation(self.xc, self.acc[0][0], AF.Silu), [self.acc[0][1]], [self.txc])
        P.op("act", lambda e: e.activation(self.BTb, self.acc[1][0], AF.Silu), [self.acc[1][1]], [self.tBTb])
        P.op("act", lambda e: e.activation(self.CTf, self.acc[2][0], AF.Silu), [self.acc[2][1]], [self.tCTf])
        P.op("pool", lambda e: e.tensor_copy(self.CTb, self.CTf), [self.tCTf], [self.tCTb])
        P.op("dve", lambda e: e.tensor_tensor(self.dx, dtr, self.mc[:, 22:30], ALU.add), [tdtr, self.tmc], [self.tdx])
        P.op("dve", lambda e: e.scalar_tensor_tensor(self.dax, self.dx, -1.0, self.dx, ALU.mult, ALU.max), [self.tdx], [self.tdax])
        P.op("act", lambda e: e.activation(self.dax, self.dax, AF.Exp, scale=-1.0), [self.tdax], [self.tdax])
        P.op("act", lambda e: e.activation(self.dax, self.dax, AF.Ln, bias=1.0), [self.tdax], [self.tdax])
        P.op("dve", lambda e: e.scalar_tensor_tensor(self.dt, self.dx, 0.0, self.dax, ALU.max, ALU.add),
             [self.tdx, self.tdax], [self.tdt])
        P.op("dve", lambda e: e.tensor_tensor(self.aa, self.dt, self.Abc, ALU.mult), [self.tdt, self.tAbc], [self.taa])
        b4, b5, b6 = P.psum[4], P.psum[5], P.psum[6]
        t4, t5, t6 = P.pbank[4], P.pbank[5], P.pbank[6]
        yield
        self.two_slot = False
        if self.two_slot:
            for pair in ((0, 1), (2, 3)):
                for st in range(6):
                    for ci in pair:
                        self.ssd_stage(st, ci)
                    yield
                for ci in pair:
                    self.ssd_rec(ci)
                    yield
        else:
            for ci in range(4):
                for st in range(6):
                    self.ssd_stage(st, ci)
                    yield
                self.ssd_rec(ci)
                yield
        self.post(b, i, tok0)

    NMID = 34

    def mid_units(self, b, i):
        gen = self.mid(b, i)
        return [(lambda: next(gen, None)) for _ in range(self.NMID + 2)]

    def ssd_stage(self, st, ci):
        P = self.P
        sl = ci % 2
        bA, bB = (4, 5) if (sl == 0 or not self.two_slot) else (2, 3)
        b4, b5 = P.psum[bA], P.psum[bB]
        t4, t5 = P.pbank[bA], P.pbank[bB]
        c0 = ci * 128
        abc = self.abc[sl]
        E, Dm, M, Cs, xdtp = self.E[sl], self.Dm[sl], self.M[sl], self.Cs[sl], self.xdtp[sl]
        nacs, tnacs = self.nacs[sl]
        d2, td2 = self.d2[sl]
        w2, tw2 = self.w2[sl]
        dtw, tdtw = self.dtw[sl]
        xdtw, txdtw = self.xdtw[sl]
        Btok, tBtok = self.Btok[sl]
        btp = b5[:, 392:456].bitcast(BF16)
        if st == 0:
            for h in range(2):
                a_, ta_ = abc[h]
                P.op("pool", lambda e, a_=a_, h=h: e.tensor_scalar(
                    a_, self.onesf, self.aa[:, ci * 2 + h:ci * 2 + h + 1], None, ALU.mult), [self.tcf, self.taa], [ta_])
        elif st == 1:
            for h in range(2):
                a_, ta_ = abc[h]
                P.op("pe", lambda e, a_=a_, h=h: e.matmul(b4[:, h * 128:(h + 1) * 128], a_, self.triu, start=True, stop=True),
                     [ta_, self.tcf], [t4])
            for h in range(2):
                a_, ta_ = abc[h]
                P.op("pe", lambda e, a_=a_, h=h: e.matmul(b4[:, 256 + h * 128:256 + (h + 1) * 128], a_, self.triu,
                                                          start=True, stop=False), [ta_, self.tcf], [t4])
                P.op("pe", lambda e, h=h: e.matmul(b4[:, 256 + h * 128:256 + (h + 1) * 128], self.identf, self.smask,
                                                   start=False, stop=True), [self.tcf], [t4])
            P.op("pe", lambda e: e.matmul(b5[:, 256:258], self.triu, self.aa[:, ci * 2:ci * 2 + 2], start=True, stop=True),
                 [self.tcf, self.taa], [t5])
            P.op("pe", lambda e: e.matmul(b5[:, 0:128], self.BTb[:, c0:c0 + 128], self.CTb[:, c0:c0 + 128], start=True, stop=True),
                 [self.tBTb, self.tCTb], [t5])
            P.op("pe", lambda e: e.transpose(b5[:, 128:256], self.xc[:, c0:c0 + 128], self.identf), [self.txc, self.tcf], [t5])
            P.op("pe", lambda e: e.transpose(btp, self.BTb[:, c0:c0 + 128], self.identb), [self.tBTb, self.tcb], [t5])
        elif st == 2:
            P.op("dve", lambda e: e.tensor_scalar(nacs, b5[:, 256:258], -1.0, None, ALU.mult), [t5], [tnacs])
            P.op("act", lambda e: e.copy(Btok, btp), [t5], [tBtok])
        elif st == 3:
            for h in range(2):
                E_, tE = E[h]
                Dm_, tDm = Dm[h]
                P.op("act", lambda e, E_=E_, h=h: e.activation(E_, b4[:, h * 128:(h + 1) * 128], AF.Exp), [t4], [tE])
                P.op("act", lambda e, Dm_=Dm_, h=h: e.activation(Dm_, b4[:, 256 + h * 128:256 + (h + 1) * 128], AF.Exp,
                                                                bias=nacs[:, h:h + 1]), [t4, tnacs], [tDm])
                P.op("dve", lambda e, h=h: e.tensor_tensor(d2[:, h:h + 1], b4[:, h * 128 + 127:h * 128 + 128],
                                                          nacs[:, h:h + 1], ALU.add), [t4, tnacs], [td2])
        elif st == 4:
            P.op("act", lambda e: e.activation(w2, d2, AF.Exp), [td2], [tw2])
            P.op("dve", lambda e: e.tensor_tensor(dtw, self.dt[:, ci * 2:ci * 2 + 2], w2, ALU.mult), [self.tdt, tw2], [tdtw])
        elif st == 5:
            for h in range(2):
                M_, tM = M[h]
                Dm_, tDm = Dm[h]
                E_, tE = E[h]
                Cs_, tCs = Cs[h]
                xp, txp = xdtp[h]
                P.op("dve", lambda e, M_=M_, Dm_=Dm_: e.tensor_tensor(M_, b5[:, 0:128], Dm_, ALU.mult), [t5, tDm], [tM])
                P.op("pool", lambda e, Cs_=Cs_, E_=E_: e.tensor_tensor(Cs_, self.CTf[:, c0:c0 + 128], E_, ALU.mult),
                     [self.tCTf, tE], [tCs])
                P.op("dve", lambda e, xp=xp, h=h: e.tensor_scalar(
                    xp[:, h * 64:(h + 1) * 64], b5[:, 128 + h * 64:128 + (h + 1) * 64],
                    self.dt[:, ci * 2 + h:ci * 2 + h + 1], None, ALU.mult), [t5, self.tdt], [txp])
                P.op("dve", lambda e, h=h: e.tensor_scalar(
                    xdtw[:, h * 64:(h + 1) * 64], b5[:, 128 + h * 64:128 + (h + 1) * 64],
                    dtw[:, h:h + 1], None, ALU.mult), [t5, tdtw], [txdtw])

    def ssd_rec(self, ci):
        P = self.P
        sl = ci % 2
        bB = 5 if (sl == 0 or not self.two_slot) else 3
        b5, t5 = P.psum[bB], P.pbank[bB]
        b6, t6 = P.psum[6], P.pbank[6]
        c0 = ci * 128
        E, M, Cs, xdtp = self.E[sl], self.M[sl], self.Cs[sl], self.xdtp[sl]
        xdtw, txdtw = self.xdtw[sl]
        Btok, tBtok = self.Btok[sl]
        seqm = [(xdtp[0], M[0]), (xdtp[1], M[1]), (self.Sbp[0], Cs[0]), (self.Sbp[1], Cs[1])]
        for n, ((l, tl), (r, tr)) in enumerate(seqm):
            P.op("pe", lambda e, l=l, r=r, n=n: e.matmul(b6[:, c0:c0 + 128], l, r, start=(n == 0), stop=(n == 3)),
                 [tl, tr], [t6])
        P.op("pe", lambda e: e.matmul(b5[:, 264:392], Btok, xdtw, start=True, stop=True), [tBtok, txdtw], [t5])
        for h in range(2):
            E_, tE = E[h]
            sb, tsb = self.Sbp[h]
            P.op("dve", lambda e, E_=E_, h=h: e.scalar_tensor_tensor(
                self.S[:, h * 64:(h + 1) * 64], self.S[:, h * 64:(h + 1) * 64], E_[:, 127:128],
                b5[:, 264 + h * 64:264 + (h + 1) * 64], ALU.mult, ALU.add), [self.tS, tE, t5], [self.tS])
            P.op("pool", lambda e, sb=sb, h=h: e.tensor_copy(sb[:, h * 64:(h + 1) * 64], self.S[:, h * 64:(h + 1) * 64]),
                 [self.tS], [tsb])

    def post(self, b, i, tok0):
        P = self.P
        b6, t6 = P.psum[6], P.pbank[6]
        P.op("dve", lambda e: e.scalar_tensor_tensor(self.yt, self.xc, self.mc[:, 21:22], b6[:, :], ALU.mult, ALU.add),
             [self.txc, self.tmc, t6], [self.tyt])
        if b == 0 and i == 0:
            P.dump("yt", self.yt, self.tyt); P.dump("S", self.S, self.tS)
        sz, tsz = self.cur_sz
        P.op("pool", lambda e: e.tensor_tensor(self.yg, self.yt, sz, ALU.mult), [self.tyt, tsz], [self.tyg])
        P.dma("act", self.out_d[64:192, tok0:tok0 + T], self.yg, [self.tyg], [self.t_out])

    def attention(self, b, i, filler):
        P = self.P
        tok0 = b * SEQ + i * T
        b7, t7 = P.psum[7], P.pbank[7]
        nblk = 4 * i + 4
        qs = self.QT[0:64, i * T:(i + 1) * T]
        V3 = self.V.rearrange("p (n d) -> p n d", d=64)
        abanks = [0, 1, 2]

        def st_z(n):
            kb = nblk - 1 - n
            j = kb - 4 * i
            diag = j >= 0
            ks = self.KT[0:64, kb * 128:(kb + 1) * 128]
            ab = abanks[n % len(abanks)]
            pa, ta = P.psum[ab], P.pbank[ab]
            ez, tez = self.ez[n % 2]
            L, tL = self.L[n % 4]
            P.op("pe", lambda e: e.matmul(pa[:, :], ks, qs, start=True, stop=False), [self.tKT[kb // 4], self.tQT[i]], [ta])
            if diag:
                P.op("pe", lambda e: e.matmul(pa[:, :], self.identb, self.amask[j], start=False, stop=False), [self.tcb], [ta])
            P.op("act", lambda e: e.activation(ez, pa[:, :], AF.Exp), [ta], [tez])
            P.op("act", lambda e: e.activation(L, ez, AF.Ln, bias=1.0), [tez], [tL])

        def st_a(n):
            kb = nblk - 1 - n
            ab = abanks[n % len(abanks)]
            pa, ta = P.psum[ab], P.pbank[ab]
            L, tL = self.L[n % 4]
            W, tW = self.W[n % 4]
            P.op("pe", lambda e: e.matmul(pa[:, :], self.ntril, L, start=False, stop=(n == 0)), [self.tcb, tL], [ta])
            ls, tls = self.Lsum[n % 3]
            ln_, tln = self.Lsum[(n + 1) % 3]
            if n > 0:
                P.op("pe", lambda e: e.matmul(pa[:, :], self.nones, ls, start=False, stop=True), [self.tcb, tls], [ta])
            P.op("act", lambda e: e.activation(W, pa[:, :], AF.Exp), [ta], [tW])
            if kb > 0:
                if n == 0:
                    P.op("dve", lambda e: e.tensor_copy(ln_, L), [tL], [tln])
                else:
                    P.op("dve", lambda e: e.tensor_tensor(ln_, ls, L, ALU.add), [tL, tls], [tln])

        def st_v(n):
            kb = nblk - 1 - n
            W, tW = self.W[n % 4]
            P.op("pe", lambda e: e.matmul(b7[0:64, :], V3[:, kb, :], W, start=(n == 0), stop=(n == nblk - 1)),
                 [self.tV[kb // 4], tW], [t7])

        units = list(filler)
        per = -(-len(units) // nblk) if units else 0
        SK = 2
        for sidx in range(nblk + 2 * SK):
            if sidx < nblk:
                st_z(sidx)
            if SK <= sidx < nblk + SK:
                st_a(sidx - SK)
            if sidx >= 2 * SK:
                st_v(sidx - 2 * SK)
            for _ in range(per):
                if units:
                    units.pop(0)()
        while units:
            units.pop(0)()
        P.op("act", lambda e: e.copy(self.ob[0:64, :], b7[0:64, :]), [t7], [self.tob])
        P.dma("act", self.out_d[192:256, tok0:tok0 + T], self.ob[0:64, :], [self.tob], [self.t_out])

    @staticmethod
    def merge(mid_u, proj_u):
        out = []
        proj_u = list(proj_u)
        for k, m in enumerate(mid_u):
            out.append(m)
            if k % 3 == 2 and proj_u:
                out.append(proj_u.pop(0))
        return out + proj_u

    def emit(self):
        self.setup()
        nt = self.ntile
        for b in range(self.nseq):
            for u in self.proj_units(b, 0):
                u()
            for u in self.merge(self.mid_units(b, 0), self.proj_units(b, 1) if nt > 1 else []):
                u()
            for i in range(nt):
                mu = self.mid_units(b, i + 1) if i + 1 < nt else []
                pu = self.proj_units(b, i + 2) if i + 2 < nt else []
                self.attention(b, i, self.merge(mu, pu))


def colsT(v):
    return np.ascontiguousarray(np.asarray(v, np.float32).reshape(-1, 128).T)


_CONST = {}


def mixer_consts():
    if "cf" not in _CONST:
        k = np.arange(128)
        triu = (k[:, None] <= k[None, :]).astype(np.float32)
        ident = np.eye(128, dtype=np.float32)
        smask = np.where(k[:, None] > k[None, :], NEG, 0.0).astype(np.float32)
        ones = np.ones((128, 128), np.float32)
        _CONST["cf"] = np.concatenate([triu, ident, smask, ones], 1)
        ntril = -(k[:, None] >= k[None, :]).astype(np.float32)
        t = np.arange(T)
        am = [np.where(128 * j + k[:, None] >= t[None, :], NEG, 0.0).astype(np.float32) for j in range(4)]
        _CONST["cb"] = np.concatenate([ident, ntril, -ones] + am, 1).astype(ml_dtypes.bfloat16)
    return _CONST["cf"], _CONST["cb"]


def mixer_inputs(c, w_in, pool_w, pool_scale, conv_w, conv_b, dt_bias, a_log, d_skip):
    g = c // 2
    bc = c // 4
    XB = 1536
    colsel = np.concatenate([
        np.arange(128 * g, 128 * g + 128),
        np.arange(512 + 128 * c, 512 + 128 * c + 128),
        np.arange(XB + 128 * c, XB + 128 * c + 128),
        np.arange(XB + 1024 + 128 * bc, XB + 1024 + 128 * bc + 128),
        np.arange(XB + 1280 + 128 * bc, XB + 1280 + 128 * bc + 128),
        np.arange(3088 + 64 * c, 3088 + 64 * c + 64),
        np.arange(3600 + 64 * c, 3600 + 64 * c + 64),
        np.arange(4112 + 64 * c, 4112 + 64 * c + 64),
        np.arange(3072 + 2 * c, 3072 + 2 * c + 2),
    ])
    wsel = np.ascontiguousarray(w_in[:, colsel])
    poolw = np.ascontiguousarray(pool_w[g][:, 64 * (c % 2):64 * (c % 2) + 64])
    mc = np.zeros((128, NMC), np.float32)
    chx = np.arange(128 * c, 128 * c + 128)
    chB = np.arange(1024 + 128 * bc, 1024 + 128 * bc + 128)
    chC = np.arange(1280 + 128 * bc, 1280 + 128 * bc + 128)
    for gi, ch in enumerate((chx, chB, chC)):
        for kk in range(4):
            mc[:, 4 * gi + kk] = conv_w[kk, ch]
        mc[:, 12 + gi] = conv_b[ch]
    mc[0:64, 15] = pool_scale[128 * g + 64 * (c % 2):128 * g + 64 * (c % 2) + 64]
    mc[:, 16 + g] = 1.0
    w = 2 ** (g + 1)
    mc[:, 20] = 1.0 / w
    mc[0:64, 21] = d_skip[2 * c]
    mc[64:128, 21] = d_skip[2 * c + 1]
    for j in range(4):
        for h in range(2):
            mc[:, 22 + 2 * j + h] = dt_bias[2 * c + h]
            mc[:, 30 + 2 * j + h] = a_log[2 * c + h]
    invc = np.broadcast_to(1.0 / np.minimum(np.arange(1, T + 1), w).astype(np.float32), (128, T)).copy()
    cf, cb = mixer_consts()
    return {"wsel": wsel, "poolw": poolw, "mc": mc, "invc": invc, "cf": cf, "cb": cb}


_PROGS = {}


def get_chain(n_ffn, has_mix, epi):
    key = ("chain", n_ffn, has_mix, epi)
    if key not in _PROGS:
        P = Prog()
        Chain(P, n_ffn, has_mix, epi).emit()
        _PROGS[key] = P.finish()
    return _PROGS[key]


def get_mixer():
    key = ("mixer",)
    if key not in _PROGS:
        P = Prog()
        Mixer(P).emit()
        _PROGS[key] = P.finish()
    return _PROGS[key]


NCORE = 8


def kernel(x, ffn1_norm, ffn1_w_gate, ffn1_w_up, ffn1_w_down, mix_norm, w_in, pool_w, pool_scale,
           conv_w, conv_b, dt_bias, a_log, d_skip, ssd_norm, w_out, ffn2_norm, ffn2_w_gate,
           ffn2_w_up, ffn2_w_down, final_norm):
    f = lambda a: np.asarray(a, dtype=np.float32)
    x = f(x)
    depth = w_in.shape[0]
    xt = x.reshape(-1, D)
    cores = list(range(NCORE))
    z8 = np.zeros((128, 8), np.float32)
    nc = get_chain(1, False, "u")
    cv = np.concatenate([colsT(f(ffn1_norm[0])), colsT(f(mix_norm[0])), z8], 1)
    maps = []
    for c in cores:
        maps.append({"h_in": np.ascontiguousarray(xt[c * NT:(c + 1) * NT].T), "cvec": cv,
                     "wg0": f(ffn1_w_gate[0]), "wu0": f(ffn1_w_up[0]), "wd0": f(ffn1_w_down[0])})
    res = run_bass_kernel_spmd(nc, maps, core_ids=cores)
    h = [res.results[c]["h_out"] for c in cores]
    u = [res.results[c]["u_out"] for c in cores]
    out = None
    for l in range(depth):
        uT = np.ascontiguousarray(np.concatenate([np.asarray(a) for a in u], axis=1))
        nc = get_mixer()
        maps = []
        for c in cores:
            m = mixer_inputs(c, f(w_in[l]), f(pool_w[l]), f(pool_scale[l]), f(conv_w[l]), f(conv_b[l]),
                             f(dt_bias[l]), f(a_log[l]), f(d_skip[l]))
            m["uT"] = uT
            maps.append(m)
        res = run_bass_kernel_spmd(nc, maps, core_ids=cores)
        mixT = np.empty((D, NCORE * NT), dtype=ml_dtypes.bfloat16)
        for c in cores:
            mo = np.asarray(res.results[c]["mixo"])
            mixT[64 * c:64 * c + 64] = mo[0:64]
            mixT[512 + 128 * c:512 + 128 * c + 128] = mo[64:192]
            mixT[1536 + 64 * c:1536 + 64 * c + 64] = mo[192:256]
        last = (l == depth - 1)
        if not last:
            nc = get_chain(2, True, "u")
            cv = np.concatenate([colsT(f(ffn2_norm[l])), colsT(f(ffn1_norm[l + 1])), colsT(f(mix_norm[l + 1])),
                                 colsT(f(ssd_norm[l]))], 1)
        else:
            nc = get_chain(1, True, "final")
            cv = np.concatenate([colsT(f(ffn2_norm[l])), colsT(f(final_norm)), colsT(f(ssd_norm[l]))], 1)
        maps = []
        for c in cores:
            m = {"h_in": h[c], "cvec": cv, "mixT": np.ascontiguousarray(mixT[:, c * NT:(c + 1) * NT]),
                 "wout": f(w_out[l]),
                 "wg0": f(ffn2_w_gate[l]), "wu0": f(ffn2_w_up[l]), "wd0": f(ffn2_w_down[l])}
            if not last:
                m.update({"wg1": f(ffn1_w_gate[l + 1]), "wu1": f(ffn1_w_up[l + 1]), "wd1": f(ffn1_w_down[l + 1])})
            maps.append(m)
        res = run_bass_kernel_spmd(nc, maps, core_ids=cores)
        if not last:
            h = [res.results[c]["h_out"] for c in cores]
            u = [res.results[c]["u_out"] for c in cores]
        else:
            out = np.concatenate([np.asarray(res.results[c]["o_out"]).T for c in cores], axis=0)
    return np.ascontiguousarray(out.reshape(x.shape).astype(np.float32))
```

```python
import numpy as np
import ml_dtypes
from contextlib import ExitStack
import concourse.bass as bass
import concourse.mybir as mybir
from concourse.bass_utils import run_bass_kernel_spmd

F32 = mybir.dt.float32
BF16 = mybir.dt.bfloat16
AF = mybir.ActivationFunctionType
ALU = mybir.AluOpType
AX = mybir.AxisListType

ENG = ("pe", "act", "dve", "pool", "sp")
NDQ = 8
SAME_ENGINE_SYNC = True


class Tk:
    __slots__ = ("name", "w", "r", "excl")

    def __init__(self, name="", excl=False):
        self.name = name
        self.w = None
        self.r = {}
        self.excl = excl


class Prog:
    def __init__(self, arena_f32=49152):
        self.nc = bass.Bass("TRN2", target_bir_lowering=False)
        self.es = ExitStack()
        self.ops = {e: [] for e in ENG}
        self.cnt = {e: 0 for e in ENG}
        self.dcnt = {}
        self.dnext = {q: 0 for q in ("sp", "act", "pool")}
        self.seen = {e: {} for e in ENG}
        self.sems = {}
        nc = self.nc
        for e in ENG:
            self.sems[e] = self.es.enter_context(nc.semaphore("s_" + e))
        for q in ("sp", "act", "pool"):
            for j in range(NDQ):
                k = "d_%s_%d" % (q, j)
                self.sems[k] = self.es.enter_context(nc.semaphore(k))
                self.dcnt[k] = 0
        self.arena = self.es.enter_context(nc.sbuf_tensor("arena", [128, arena_f32], F32))
        self.arena_n = arena_f32
        self.aoff = 0
        self.psum = []
        self.pbank = []
        for i in range(8):
            t = self.es.enter_context(nc.psum_tensor("ps%d" % i, [128, 512], F32))
            self.psum.append(t)
            self.pbank.append(Tk("ps%d" % i, excl=True))
        self.n_inst = 0

    def reset_arena(self, keep=0):
        self.aoff = keep

    def alloc(self, name, cols, dtype=F32):
        nf = cols if dtype == F32 else (cols + 1) // 2
        nf = (nf + 7) // 8 * 8
        assert self.aoff + nf <= self.arena_n, ("arena overflow", name, self.aoff, nf)
        ap = self.arena[:, self.aoff:self.aoff + nf]
        self.aoff += nf
        if dtype != F32:
            ap = ap.bitcast(dtype)[:, 0:cols]
        else:
            ap = ap[:, 0:cols]
        return ap, Tk(name)

    def dram(self, name, shape, dtype, kind="Internal"):
        return self.nc.dram_tensor(name, list(shape), dtype, kind=kind).ap()

    def _waits(self, e, reads, writes):
        waits = {}

        def need(dep):
            if dep is None:
                return
            k, v = dep
            if k == e and (e == "pe" or not SAME_ENGINE_SYNC):
                return
            if waits.get(k, 0) < v:
                waits[k] = v

        for t in reads:
            need(t.w)
            if t.excl:
                for k, v in t.r.items():
                    need((k, v))
        for t in writes:
            need(t.w)
            for k, v in t.r.items():
                need((k, v))
        wl = []
        for k, v in waits.items():
            if self.seen[e].get(k, 0) < v:
                self.seen[e][k] = v
                wl.append((k, v))
        return wl

    def op(self, e, fn, reads=(), writes=()):
        wl = self._waits(e, reads, writes)
        self.cnt[e] += 1
        c = self.cnt[e]
        sems = self.sems
        semE = sems[e]

        def emit(eng):
            for k, v in wl:
                eng.wait_ge(sems[k], v)
            fn(eng).then_inc(semE, 1)

        self.ops[e].append(emit)
        self.n_inst += 1 + len(wl)
        for t in reads:
            if t.excl:
                t.w = (e, c)
                t.r = {}
            else:
                t.r[e] = c
        for t in writes:
            t.w = (e, c)
            t.r = {}

    def dma(self, q, out_ap, in_ap, reads=(), writes=()):
        wl = self._waits(q, reads, writes)
        j = self.dnext[q]
        self.dnext[q] = (j + 1) % NDQ
        key = "d_%s_%d" % (q, j)
        prev = self.dcnt[key]
        if prev > 0 and self.seen[q].get(key, 0) < prev:
            self.seen[q][key] = prev
            wl.append((key, prev))
        self.dcnt[key] = prev + 16
        v = prev + 16
        sems = self.sems

        def emit(eng):
            for k, vv in wl:
                eng.wait_ge(sems[k], vv)
            eng.dma_start(out=out_ap, in_=in_ap).then_inc(sems[key], 16)

        self.ops[q].append(emit)
        self.n_inst += 1 + len(wl)
        for t in reads:
            t.r[key] = v
        for t in writes:
            t.w = (key, v)
            t.r = {}

    def dump(self, name, ap, tk, dtype=F32):
        if not getattr(self, "debug", False):
            return
        d = self.dram("dbg_" + name, [ap.shape[0], ap.shape[1]], dtype, "ExternalOutput")
        self.dma("sp", d, ap, [tk], [Tk()])

    def barrier(self):
        cur = dict(self.cnt)
        cur.update(self.dcnt)
        sems = self.sems
        for e in ENG:
            wl = []
            for k, v in cur.items():
                if k != e and v > self.seen[e].get(k, 0):
                    self.seen[e][k] = v
                    wl.append((k, v))

            def emit(eng, wl=wl):
                for k, v in wl:
                    eng.wait_ge(sems[k], v)

            self.ops[e].append(emit)
            self.n_inst += len(wl)

    def finish(self):
        self.barrier()
        nc = self.nc
        ops = self.ops
        with nc.Block() as block:
            @block.tensor
            def _(eng):
                for f in ops["pe"]:
                    f(eng)

            @block.scalar
            def _(eng):
                for f in ops["act"]:
                    f(eng)

            @block.vector
            def _(eng):
                for f in ops["dve"]:
                    f(eng)

            @block.gpsimd
            def _(eng):
                for f in ops["pool"]:
                    f(eng)

            @block.sync
            def _(eng):
                for f in ops["sp"]:
                    f(eng)
        self.es.close()
        return nc


D = 2048
DFF = 5632
NT = 2048
T = 512
KD = D // 128
KF = DFF // 128
EPS = 1e-6
WB = 8192
NWB = 5


def v3(ap, k):
    return ap.rearrange("p (k t) -> p k t", k=k)


class Chain:
    def __init__(self, P, n_ffn, has_mix, epilogue):
        self.P = P
        nc = P.nc
        self.n_ffn, self.has_mix, self.epi = n_ffn, has_mix, epilogue
        self.h_in = P.dram("h_in", [D, NT], F32, "ExternalInput")
        self.t_hin = Tk("h_in")
        ncv = 16 * (n_ffn + 1) + 8
        self.ncv = ncv
        self.cv_d = P.dram("cvec", [128, ncv], F32, "ExternalInput")
        self.w32 = []
        self.wbf = []
        self.twb = []
        for i in range(n_ffn):
            for nm, shp in (("wg", [D, DFF]), ("wu", [D, DFF]), ("wd", [DFF, D])):
                self.w32.append(P.dram("%s%d" % (nm, i), shp, F32, "ExternalInput"))
                self.wbf.append(P.dram("%s%d_bf" % (nm, i), shp, BF16))
                self.twb.append([Tk() for _ in range(44)])
        if has_mix:
            self.mix_d = P.dram("mixT", [D, NT], BF16, "ExternalInput")
            self.wo32 = P.dram("wout", [D, D], F32, "ExternalInput")
            self.wobf = P.dram("wout_bf", [D, D], BF16)
            self.two = [Tk() for _ in range(KD)]
        if epilogue == "u":
            self.h_out = P.dram("h_out", [D, NT], F32, "ExternalOutput")
            self.u_out = P.dram("u_out", [D, NT], BF16, "ExternalOutput")
        else:
            self.o_out = P.dram("o_out", [D, NT], F32, "ExternalOutput")
        self.t_out = Tk("out")
        self.cv, self.tcv = P.alloc("cv", ncv)
        self.ones, self.tones = P.alloc("ones", 128, BF16)
        self.h, self.th = P.alloc("h", KD * T)
        self.u, self.tu = P.alloc("u", KD * T, BF16)
        self.act, self.tact = P.alloc("act", KF * T, BF16)
        self.sq = [P.alloc("sq%d" % i, T, BF16) for i in range(2)]
        self.sg = [P.alloc("sg%d" % i, T) for i in range(2)]
        self.rs, self.trs = P.alloc("rs", T)
        self.wb = [P.alloc("wb%d" % i, WB, BF16) for i in range(NWB)]
        self.wbi = 0
        self.tk_h = [Tk("h%d" % k) for k in range(KD)]
        self.tk_u = [Tk("u%d" % k) for k in range(KD)]
        self.tk_a = [Tk("a%d" % k) for k in range(KF)]
        self.alt = 0

    def nextwb(self):
        w = self.wb[self.wbi]
        self.wbi = (self.wbi + 1) % NWB
        return w

    def ew(self):
        self.alt ^= 1
        return "dve" if self.alt else "pool"

    def cast_weights(self):
        P = self.P
        P.op("pool", lambda e: e.memset(self.ones, 1.0), [], [self.tones])
        P.dma("sp", self.cv, self.cv_d, [], [self.tcv])
        if self.has_mix:
            for k in range(KD):
                P.dma("pool", self.wobf[k * 128:(k + 1) * 128, :], self.wo32[k * 128:(k + 1) * 128, :],
                      [], [self.two[k]])
        for fi in range(self.n_ffn):
            for fg in range(KF // 4):
                for mi in (3 * fi, 3 * fi + 1):
                    for rq in range(4):
                        P.dma("pool", self.wbf[mi][rq * 512:(rq + 1) * 512, fg * 512:(fg + 1) * 512],
                              self.w32[mi][rq * 512:(rq + 1) * 512, fg * 512:(fg + 1) * 512], [], [self.twb[mi][fg * 4 + rq]])
                mi = 3 * fi + 2
                for k in range(fg * 4, fg * 4 + 4):
                    P.dma("pool", self.wbf[mi][k * 128:(k + 1) * 128, :], self.w32[mi][k * 128:(k + 1) * 128, :],
                          [], [self.twb[mi][k]])

    def norm_stats(self, src3, tks, idxs, nfeat, bank):
        P = self.P
        ps, tps = P.psum[bank], P.pbank[bank]
        n = len(idxs)
        for i, k in enumerate(idxs):
            sq, tsq = self.sq[i % 2]
            P.op("act", lambda e, k=k, sq=sq: e.activation(sq, src3[:, k, :], AF.Square), [tks[k]], [tsq])
            P.op("pe", lambda e, i=i, sq=sq: e.matmul(ps[:, :], self.ones, sq, start=(i == 0), stop=(i == n - 1)),
                 [self.tones, tsq], [tps])
        P.op("act", lambda e: e.activation(self.rs, ps[:, :], AF.Sqrt, bias=EPS, scale=1.0 / nfeat), [tps], [self.trs])
        P.op("dve", lambda e: e.reciprocal(self.rs, self.rs), [self.trs], [self.trs])

    def rmsnorm(self, gcol0, dst3, tdst):
        P = self.P
        h3 = v3(self.h, KD)
        self.norm_stats(h3, self.tk_h, list(range(KD)), D, 4)
        for k in range(KD):
            P.op("dve", lambda e, k=k: e.scalar_tensor_tensor(
                dst3[:, k, :], h3[:, k, :], self.cv[:, gcol0 + k:gcol0 + k + 1], self.rs, ALU.mult, ALU.mult),
                [self.tk_h[k], self.tcv, self.trs], tdst[k] if isinstance(tdst[k], list) else [tdst[k]])

    def ffn(self, i):
        P = self.P
        h3 = v3(self.h, KD)
        u3 = v3(self.u, KD)
        a3 = v3(self.act, KF)
        wg, wu, wd = self.wbf[3 * i], self.wbf[3 * i + 1], self.wbf[3 * i + 2]
        twg, twu, twd = self.twb[3 * i], self.twb[3 * i + 1], self.twb[3 * i + 2]
        self.rmsnorm(16 * i, u3, self.tk_u)
        wg3 = wg.rearrange("(k p) f -> p k f", p=128)
        wu3 = wu.rearrange("(k p) f -> p k f", p=128)
        wd3 = wd.rearrange("(k p) f -> p k f", p=128)
        gi = 0
        for fg in range(KF // 4):
            (wa, twa), (wb_, twb_) = self.nextwb(), self.nextwb()
            wa3, wb3 = v3(wa, KD), v3(wb_, KD)
            P.dma("sp", wa3, wg3[:, :, fg * 512:(fg + 1) * 512], twg[fg * 4:fg * 4 + 4], [twa])
            P.dma("sp", wb3, wu3[:, :, fg * 512:(fg + 1) * 512], twu[fg * 4:fg * 4 + 4], [twb_])
            for f4 in range(4):
                f = fg * 4 + f4
                bg, bu = (0, 1) if gi % 2 == 0 else (2, 3)
                gi += 1
                pg, pu = P.psum[bg], P.psum[bu]
                for k in range(KD):
                    P.op("pe", lambda e, k=k, f4=f4, pg=pg, wa3=wa3: e.matmul(
                        pg[:, :], wa3[:, k, f4 * 128:(f4 + 1) * 128], u3[:, k, :], start=(k == 0), stop=(k == KD - 1)),
                        [twa, self.tk_u[k]], [P.pbank[bg]])
                for k in range(KD):
                    P.op("pe", lambda e, k=k, f4=f4, pu=pu, wb3=wb3: e.matmul(
                        pu[:, :], wb3[:, k, f4 * 128:(f4 + 1) * 128], u3[:, k, :], start=(k == 0), stop=(k == KD - 1)),
                        [twb_, self.tk_u[k]], [P.pbank[bu]])
                sg, tsg = self.sg[f % 2]
                P.op("act", lambda e, sg=sg, pg=pg: e.activation(sg, pg[:, :], AF.Silu), [P.pbank[bg]], [tsg])
                P.op("dve", lambda e, sg=sg, pu=pu, f=f: e.tensor_tensor(a3[:, f, :], sg, pu[:, :], ALU.mult),
                     [tsg, P.pbank[bu]], [self.tk_a[f]])
        FD = 11
        for dg in range(4):
            banks = [4, 5, 6, 7] if dg % 2 == 0 else [0, 1, 2, 3]
            for fgd in range(KF // FD):
                w, tw = self.nextwb()
                w3 = w[:, 0:FD * 512].rearrange("p (k t) -> p k t", k=FD)
                P.dma("sp", w3, wd3[:, fgd * FD:(fgd + 1) * FD, dg * 512:(dg + 1) * 512],
                      twd[fgd * FD:(fgd + 1) * FD], [tw])
                for j in range(4):
                    pb = P.psum[banks[j]]
                    for f in range(FD):
                        ff = fgd * FD + f
                        P.op("pe", lambda e, j=j, f=f, ff=ff, pb=pb, w3=w3: e.matmul(
                            pb[:, :], w3[:, f, j * 128:(j + 1) * 128], a3[:, ff, :],
                            start=(ff == 0), stop=(ff == KF - 1)),
                            [tw, self.tk_a[ff]], [P.pbank[banks[j]]])
            for j in range(4):
                c = dg * 4 + j
                pb = P.psum[banks[j]]
                P.op("dve", lambda e, c=c, pb=pb: e.scalar_tensor_tensor(
                    h3[:, c, :], pb[:, :], 0.5, h3[:, c, :], ALU.mult, ALU.add),
                    [P.pbank[banks[j]], self.tk_h[c]], [self.tk_h[c]])

    def mix_stage(self, t0):
        P = self.P
        h3 = v3(self.h, KD)
        m3 = v3(self.u, KD)
        P.dma("sp", m3, self.mix_d.rearrange("(k p) t -> p k t", p=128)[:, :, t0:t0 + T], [], self.tk_u)
        gc0 = 16 * (self.n_ffn + 1)
        for grp in range(2):
            idxs = [4 + grp * 4 + c for c in range(4)]
            self.norm_stats(m3, self.tk_u, idxs, 512, 4)
            for c in idxs:
                P.op("dve", lambda e, c=c: e.scalar_tensor_tensor(
                    m3[:, c, :], m3[:, c, :], self.cv[:, gc0 + c - 4:gc0 + c - 3], self.rs, ALU.mult, ALU.mult),
                    [self.tk_u[c], self.tcv, self.trs], [self.tk_u[c]])
        wo3 = self.wobf.rearrange("(k p) f -> p k f", p=128)
        for dg in range(4):
            banks = [0, 1, 2, 3] if dg % 2 == 0 else [4, 5, 6, 7]
            w, tw = self.nextwb()
            w3 = v3(w, KD)
            P.dma("sp", w3, wo3[:, :, dg * 512:(dg + 1) * 512], self.two, [tw])
            for j in range(4):
                pb = P.psum[banks[j]]
                for k in range(KD):
                    P.op("pe", lambda e, j=j, k=k, pb=pb, w3=w3: e.matmul(
                        pb[:, :], w3[:, k, j * 128:(j + 1) * 128], m3[:, k, :], start=(k == 0), stop=(k == KD - 1)),
                        [tw, self.tk_u[k]], [P.pbank[banks[j]]])
            for j in range(4):
                c = dg * 4 + j
                pb = P.psum[banks[j]]
                P.op("dve", lambda e, c=c, pb=pb: e.tensor_tensor(h3[:, c, :], pb[:, :], h3[:, c, :], ALU.add),
                     [P.pbank[banks[j]], self.tk_h[c]], [self.tk_h[c]])

    def emit(self):
        P = self.P
        self.cast_weights()
        h3 = v3(self.h, KD)
        hin3 = self.h_in.rearrange("(k p) t -> p k t", p=128)
        for it in range(NT // T):
            t0 = it * T
            P.dma("sp", h3, hin3[:, :, t0:t0 + T], [self.t_hin], self.tk_h)
            if self.has_mix:
                self.mix_stage(t0)
            for i in range(self.n_ffn):
                self.ffn(i)
            gc = 16 * self.n_ffn
            if self.epi == "u":
                P.dma("act", self.h_out.rearrange("(k p) t -> p k t", p=128)[:, :, t0:t0 + T], h3, self.tk_h, [self.t_out])
                u3 = v3(self.u, KD)
                self.rmsnorm(gc, u3, self.tk_u)
                P.dma("act", self.u_out.rearrange("(k p) t -> p k t", p=128)[:, :, t0:t0 + T], u3, self.tk_u, [self.t_out])
            else:
                o3 = v3(self.act.bitcast(F32)[:, 0:KD * T], KD)
                self.rmsnorm(gc, o3, [[self.tk_a[2 * k], self.tk_a[2 * k + 1]] for k in range(KD)])
                P.dma("act", self.o_out.rearrange("(k p) t -> p k t", p=128)[:, :, t0:t0 + T], o3, self.tk_a[0:2 * KD], [self.t_out])


SEQ = 8192
NSEQ = 2
NTILE = SEQ // T
WSEL = 834
C_POOL, C_Z, C_X, C_B, C_C, C_Q, C_K, C_V, C_DT = 0, 128, 256, 384, 512, 640, 704, 768, 832
NMC = 38
NEG = -30000.0


class Mixer:
    def __init__(self, P, nseq=NSEQ, ntile=NTILE):
        self.P = P
        self.nseq, self.ntile = nseq, ntile
        ntok = nseq * SEQ
        self.u_d = P.dram("uT", [D, ntok], BF16, "ExternalInput")
        self.w32 = P.dram("wsel", [D, WSEL], F32, "ExternalInput")
        self.wbf_d = P.dram("wsel_bf", [D, WSEL], BF16)
        self.pw_d = P.dram("poolw", [128, 64], F32, "ExternalInput")
        self.mc_d = P.dram("mc", [128, NMC], F32, "ExternalInput")
        self.invc_d = P.dram("invc", [128, T], F32, "ExternalInput")
        self.cf_d = P.dram("cf", [128, 4 * 128], F32, "ExternalInput")
        self.cb_d = P.dram("cb", [128, 3 * 128 + 4 * T], BF16, "ExternalInput")
        self.out_d = P.dram("mixo", [256, ntok], BF16, "ExternalOutput")
        self.t_out = Tk("mixo")
        self.twd = [Tk() for _ in range(KD)]
        A = P.alloc
        self.wsb, self.twsb = A("wsb", KD * WSEL, BF16)
        self.ub = [A("ub%d" % i, KD * T, BF16) for i in range(2)]
        self.QT, _ = A("QT", SEQ, BF16)
        self.KT, _ = A("KT", SEQ, BF16)
        self.V, _ = A("V", 64 * 64, BF16)
        self.tQT = [Tk() for _ in range(NTILE)]
        self.tKT = [Tk() for _ in range(NTILE)]
        self.tV = [Tk() for _ in range(NTILE)]
        self.pbk = 0
        self.mc, self.tmc = A("mc", NMC)
        self.invc, self.tinvc = A("invc", T)
        self.cf, self.tcf = A("cf", 4 * 128)
        self.cb, self.tcb = A("cb", 3 * 128 + 4 * T, BF16)
        self.pw32, self.tpw32 = A("pw32", 64)
        self.pwb, self.tpwb = A("pwb", 64, BF16)
        self.Abc, self.tAbc = A("Abc", 8)
        self.ve2 = [A("ve%d" % p, 15 + T) for p in range(2)]
        self.s = [A("s%d" % i, 15 + T) for i in range(4)]
        self.res, self.tres = A("res", T)
        self.pdiff, self.tpdiff = A("pdiff", T, BF16)
        self.po, self.tpo = A("po", T, BF16)
        self.xe2 = [[A("xe%d%d" % (p, i), 3 + T) for i in range(3)] for p in range(2)]
        self.acc = [A("acc%d" % i, T) for i in range(3)]
        self.xc, self.txc = A("xc", T)
        self.BTb, self.tBTb = A("BTb", T, BF16)
        self.CTf, self.tCTf = A("CTf", T)
        self.CTb, self.tCTb = A("CTb", T, BF16)
        self.sz2 = [A("sz%d" % p, T) for p in range(2)]
        self.dtr2 = [A("dtr%d" % p, 8) for p in range(2)]
        self.dx, self.tdx = A("dx", 8)
        self.dax, self.tdax = A("dax", 8)
        self.dt, self.tdt = A("dt", 8)
        self.aa, self.taa = A("aa", 8)
        self.abc = [[A("abc%d%d" % (sl, i), 128) for i in range(2)] for sl in range(2)]
        self.nacs = [A("nacs%d" % sl, 2) for sl in range(2)]
        self.d2 = [A("d2%d" % sl, 2) for sl in range(2)]
        self.w2 = [A("w2%d" % sl, 2) for sl in range(2)]
        self.dtw = [A("dtw%d" % sl, 2) for sl in range(2)]
        self.E = [[A("E%d%d" % (sl, i), 128) for i in range(2)] for sl in range(2)]
        self.Dm = [[A("Dm%d%d" % (sl, i), 128) for i in range(2)] for sl in range(2)]
        self.M = [[A("M%d%d" % (sl, i), 128, BF16) for i in range(2)] for sl in range(2)]
        self.Cs = [[A("Cs%d%d" % (sl, i), 128, BF16) for i in range(2)] for sl in range(2)]
        self.xdtp = [[A("xdtp%d%d" % (sl, i), 128, BF16) for i in range(2)] for sl in range(2)]
        self.xdtw = [A("xdtw%d" % sl, 128, BF16) for sl in range(2)]
        self.Btok = [A("Btok%d" % sl, 128, BF16) for sl in range(2)]
        self.S, self.tS = A("S", 128)
        self.Sbp = [A("Sbp%d" % i, 128, BF16) for i in range(2)]
        self.yt, self.tyt = A("yt", T)
        self.yg, self.tyg = A("yg", T, BF16)
        self.ez = [A("ez%d" % i, T) for i in range(2)]
        self.L = [A("L%d" % i, T, BF16) for i in range(4)]
        self.W = [A("W%d" % i, T, BF16) for i in range(4)]
        self.Lsum = [A("Lsum%d" % i, T, BF16) for i in range(3)]
        self.ob, self.tob = A("ob", T, BF16)
        self.blk = 0

    def setup(self):
        P = self.P
        for k in range(KD):
            P.dma("pool", self.wbf_d[k * 128:(k + 1) * 128, :], self.w32[k * 128:(k + 1) * 128, :], [], [self.twd[k]])
        P.dma("sp", v3(self.wsb, KD), self.wbf_d.rearrange("(k p) f -> p k f", p=128), self.twd, [self.twsb])
        P.dma("sp", self.mc, self.mc_d, [], [self.tmc])
        P.dma("sp", self.invc, self.invc_d, [], [self.tinvc])
        P.dma("sp", self.cf, self.cf_d, [], [self.tcf])
        P.dma("sp", self.cb, self.cb_d, [], [self.tcb])
        P.dma("sp", self.pw32, self.pw_d, [], [self.tpw32])
        P.op("act", lambda e: e.copy(self.pwb, self.pw32), [self.tpw32], [self.tpwb])
        P.op("act", lambda e: e.activation(self.Abc, self.mc[:, 30:38], AF.Exp), [self.tmc], [self.tAbc])
        P.op("dve", lambda e: e.tensor_scalar(self.Abc, self.Abc, -1.0, None, ALU.mult), [self.tAbc], [self.tAbc])
        for sl in range(2):
            for i in range(2):
                x, t = self.xdtp[sl][i]
                P.op("pool", lambda e, x=x: e.memset(x, 0.0), [], [t])
        self.triu = self.cf[:, 0:128]
        self.identf = self.cf[:, 128:256]
        self.smask = self.cf[:, 256:384]
        self.onesf = self.cf[:, 384:512]
        self.identb = self.cb[:, 0:128]
        self.ntril = self.cb[:, 128:256]
        self.nones = self.cb[:, 256:384]
        self.amask = [self.cb[:, 384 + j * T:384 + (j + 1) * T] for j in range(4)]

    def proj_units(self, b, i):
        P = self.P
        it = b * self.ntile + i
        tok0 = b * SEQ + i * T
        ub, tub = self.ub[it % 2]
        ub3 = v3(ub, KD)
        wsb3 = v3(self.wsb, KD)
        first = (i == 0)
        units = []
        pp = i % 2
        ve, tve = self.ve2[pp]
        xes = self.xe2[pp]
        sz, tsz = self.sz2[pp]
        dtr, tdtr = self.dtr2[pp]

        def u_dma():
            P.dma("sp", ub3, self.u_d.rearrange("(k p) t -> p k t", p=128)[:, :, tok0:tok0 + T], [], [tub])
            if first:
                P.op("pool", lambda e: e.memset(ve[:, 0:15], 0.0), [], [tve])
                for g in range(3):
                    xe, txe = xes[g]
                    P.op("pool", lambda e, xe=xe: e.memset(xe[:, 0:3], 0.0), [], [txe])
                P.op("pool", lambda e: e.memset(self.S, 0.0), [], [self.tS])
                for h in range(2):
                    sb, tsb = self.Sbp[h]
                    P.op("pool", lambda e, sb=sb: e.memset(sb, 0.0), [], [tsb])
        units.append(u_dma)

        def bank():
            bnk = 3
            self.pbk += 1
            return bnk

        def fm(c0, ncols, evac):
            def unit():
                bnk = bank()
                ps = P.psum[bnk]
                for k in range(KD):
                    P.op("pe", lambda e, k=k: e.matmul(ps[0:ncols, :], wsb3[:, k, c0:c0 + ncols], ub3[:, k, :],
                                                       start=(k == 0), stop=(k == KD - 1)),
                         [self.twsb, tub], [P.pbank[bnk]])
                evac(ps, P.pbank[bnk])
            units.append(unit)

        fm(C_POOL, 128, lambda ps, tp: P.op("dve", lambda e: e.tensor_copy(ve[:, 15:15 + T], ps[:, :]), [tp], [tve]))
        fm(C_Z, 128, lambda ps, tp: P.op("act", lambda e: e.activation(sz, ps[:, :], AF.Silu), [tp], [tsz]))
        for g, c0 in enumerate((C_X, C_B, C_C)):
            xe, txe = xes[g]
            fm(c0, 128, lambda ps, tp, xe=xe, txe=txe: P.op(
                "dve", lambda e: e.tensor_copy(xe[:, 3:3 + T], ps[:, :]), [tp], [txe]))
        fm(C_Q, 64, lambda ps, tp: P.op("dve", lambda e: e.tensor_scalar(
            self.QT[0:64, i * T:(i + 1) * T], ps[0:64, :], 0.125, None, ALU.mult), [tp], [self.tQT[i]]))
        fm(C_K, 64, lambda ps, tp: P.op("dve", lambda e: e.tensor_copy(
            self.KT[0:64, i * T:(i + 1) * T], ps[0:64, :]), [tp], [self.tKT[i]]))

        def tm():
            bnk = bank()
            ps = P.psum[bnk]
            for j in range(4):
                for k in range(KD):
                    P.op("pe", lambda e, k=k, j=j: e.matmul(ps[:, j * 66:(j + 1) * 66], ub3[:, k, j * 128:(j + 1) * 128],
                                                            wsb3[:, k, C_V:C_V + 66], start=(k == 0), stop=(k == KD - 1)),
                         [self.twsb, tub], [P.pbank[bnk]])
            V3 = self.V.rearrange("p (n d) -> p n d", d=64)
            ps3 = ps[:, 0:264].rearrange("p (j c) -> p j c", c=66)
            P.op("dve", lambda e: e.tensor_copy(V3[:, i * 4:(i + 1) * 4, :], ps3[:, :, 0:64]), [P.pbank[bnk]], [self.tV[i]])
            P.op("dve", lambda e: e.tensor_copy(dtr.rearrange("p (j c) -> p j c", c=2), ps3[:, :, 64:66]),
                 [P.pbank[bnk]], [tdtr])
        units.append(tm)
        return units

    def mid(self, b, i):
        P = self.P
        tok0 = b * SEQ + i * T
        first = (i == 0)
        pp = i % 2
        ve, tve = self.ve2[pp]
        ven, tven = self.ve2[1 - pp]
        xes, xesn = self.xe2[pp], self.xe2[1 - pp]
        self.cur_sz = self.sz2[pp]
        dtr, tdtr = self.dtr2[pp]
        sh = [1, 2, 4, 8]
        lo = [1, 3, 7, 15]
        prev, tprev = ve, tve
        for q in range(4):
            s, ts = self.s[q]
            P.op("pool", lambda e, s=s, prev=prev, q=q: e.tensor_tensor(
                s[:, lo[q]:15 + T], prev[:, lo[q]:15 + T], prev[:, lo[q] - sh[q]:15 + T - sh[q]], ALU.add),
                [tprev], [ts])
            prev, tprev = s, ts
        s0, ts0 = self.s[0]
        P.op("dve", lambda e: e.tensor_scalar(self.res, s0[:, 15:15 + T], self.mc[:, 16:17], None, ALU.mult),
             [ts0, self.tmc], [self.tres])
        for q in range(1, 4):
            s, ts = self.s[q]
            P.op("dve", lambda e, s=s, q=q: e.scalar_tensor_tensor(self.res, s[:, 15:15 + T], self.mc[:, 16 + q:17 + q],
                                                                    self.res, ALU.mult, ALU.add),
                 [ts, self.tmc, self.tres], [self.tres])
        if first:
            P.op("dve", lambda e: e.tensor_tensor(self.res, self.res, self.invc, ALU.mult), [self.tres, self.tinvc], [self.tres])
            P.op("dve", lambda e: e.tensor_tensor(self.pdiff, self.res, ve[:, 15:15 + T], ALU.subtract),
                 [self.tres, tve], [self.tpdiff])
        else:
            P.op("dve", lambda e: e.scalar_tensor_tensor(self.pdiff, self.res, self.mc[:, 20:21], ve[:, 15:15 + T],
                                                          ALU.mult, ALU.subtract),
                 [self.tres, self.tmc, tve], [self.tpdiff])
        P.op("pool", lambda e: e.tensor_copy(ven[:, 0:15], ve[:, T:T + 15]), [tve], [tven])
        bnk = 6
        ps = P.psum[bnk]
        P.op("pe", lambda e, ps=ps: e.matmul(ps[0:64, :], self.pwb, self.pdiff, start=True, stop=True),
             [self.tpwb, self.tpdiff], [P.pbank[bnk]])
        P.op("dve", lambda e, ps=ps: e.tensor_scalar(self.po[0:64, :], ps[0:64, :], self.mc[0:64, 15:16], None, ALU.mult),
             [P.pbank[bnk], self.tmc], [self.tpo])
        P.dma("act", self.out_d[0:64, tok0:tok0 + T], self.po[0:64, :], [self.tpo], [self.t_out])
        yield
        for g in range(3):
            xe, txe = xes[g]
            xen, txen = xesn[g]
            acc, tacc = self.acc[g]
            P.op("dve", lambda e, xe=xe, acc=acc, g=g: e.tensor_scalar(
                acc, xe[:, 3:3 + T], self.mc[:, 4 * g + 3:4 * g + 4], self.mc[:, 12 + g:13 + g], ALU.mult, ALU.add),
                [txe, self.tmc], [tacc])
            for kk in (2, 1, 0):
                P.op("dve", lambda e, xe=xe, acc=acc, g=g, kk=kk: e.scalar_tensor_tensor(
                    acc, xe[:, kk:kk + T], self.mc[:, 4 * g + kk:4 * g + kk + 1], acc, ALU.mult, ALU.add),
                    [txe, self.tmc, tacc], [tacc])
            P.op("pool", lambda e, xe=xe, xen=xen: e.tensor_copy(xen[:, 0:3], xe[:, T:T + 3]), [txe], [txen])
            yield
        P.op("act", lambda e: e.activation(self.xc, self.acc[0][0], AF.Silu), [self.acc[0][1]], [self.txc])
        P.op("act", lambda e: e.activation(self.BTb, self.acc[1][0], AF.Silu), [self.acc[1][1]], [self.tBTb])
        P.op("act", lambda e: e.activation(self.CTf, self.acc[2][0], AF.Silu), [self.acc[2][1]], [self.tCTf])
        P.op("dve", lambda e: e.tensor_copy(self.CTb, self.CTf), [self.tCTf], [self.tCTb])
        P.op("dve", lambda e: e.tensor_tensor(self.dx, dtr, self.mc[:, 22:30], ALU.add), [tdtr, self.tmc], [self.tdx])
        P.op("dve", lambda e: e.scalar_tensor_tensor(self.dax, self.dx, -1.0, self.dx, ALU.mult, ALU.max), [self.tdx], [self.tdax])
        P.op("act", lambda e: e.activation(self.dax, self.dax, AF.Exp, scale=-1.0), [self.tdax], [self.tdax])
        P.op("act", lambda e: e.activation(self.dax, self.dax, AF.Ln, bias=1.0), [self.tdax], [self.tdax])
        P.op("dve", lambda e: e.scalar_tensor_tensor(self.dt, self.dx, 0.0, self.dax, ALU.max, ALU.add),
             [self.tdx, self.tdax], [self.tdt])
        P.op("dve", lambda e: e.tensor_tensor(self.aa, self.dt, self.Abc, ALU.mult), [self.tdt, self.tAbc], [self.taa])
        b4, b5, b6 = P.psum[4], P.psum[5], P.psum[6]
        t4, t5, t6 = P.pbank[4], P.pbank[5], P.pbank[6]
        yield
        self.two_slot = False
        if self.two_slot:
            for pair in ((0, 1), (2, 3)):
                for st in range(6):
                    for ci in pair:
                        self.ssd_stage(st, ci)
                    yield
                for ci in pair:
                    self.ssd_rec(ci)
                    yield
        else:
            for ci in range(4):
                for st in range(6):
                    self.ssd_stage(st, ci)
                    yield
                self.ssd_rec(ci)
                yield
        self.post(b, i, tok0)

    NMID = 34

    def mid_units(self, b, i):
        gen = self.mid(b, i)
        return [(lambda: next(gen, None)) for _ in range(self.NMID + 2)]

    def ssd_stage(self, st, ci):
        P = self.P
        sl = ci % 2
        bA, bB = (4, 5) if (sl == 0 or not self.two_slot) else (2, 3)
        b4, b5 = P.psum[bA], P.psum[bB]
        t4, t5 = P.pbank[bA], P.pbank[bB]
        c0 = ci * 128
        abc = self.abc[sl]
        E, Dm, M, Cs, xdtp = self.E[sl], self.Dm[sl], self.M[sl], self.Cs[sl], self.xdtp[sl]
        nacs, tnacs = self.nacs[sl]
        d2, td2 = self.d2[sl]
        w2, tw2 = self.w2[sl]
        dtw, tdtw = self.dtw[sl]
        xdtw, txdtw = self.xdtw[sl]
        Btok, tBtok = self.Btok[sl]
        btp = b5[:, 392:456].bitcast(BF16)
        if st == 0:
            for h in range(2):
                a_, ta_ = abc[h]
                P.op("dve", lambda e, a_=a_, h=h: e.tensor_scalar(
                    a_, self.onesf, self.aa[:, ci * 2 + h:ci * 2 + h + 1], None, ALU.mult), [self.tcf, self.taa], [ta_])
        elif st == 1:
            for h in range(2):
                a_, ta_ = abc[h]
                P.op("pe", lambda e, a_=a_, h=h: e.matmul(b4[:, h * 128:(h + 1) * 128], a_, self.triu, start=True, stop=True),
                     [ta_, self.tcf], [t4])
            for h in range(2):
                a_, ta_ = abc[h]
                P.op("pe", lambda e, a_=a_, h=h: e.matmul(b4[:, 256 + h * 128:256 + (h + 1) * 128], a_, self.triu,
                                                          start=True, stop=False), [ta_, self.tcf], [t4])
                P.op("pe", lambda e, h=h: e.matmul(b4[:, 256 + h * 128:256 + (h + 1) * 128], self.identf, self.smask,
                                                   start=False, stop=True), [self.tcf], [t4])
            P.op("pe", lambda e: e.matmul(b5[:, 256:258], self.triu, self.aa[:, ci * 2:ci * 2 + 2], start=True, stop=True),
                 [self.tcf, self.taa], [t5])
            P.op("pe", lambda e: e.matmul(b5[:, 0:128], self.BTb[:, c0:c0 + 128], self.CTb[:, c0:c0 + 128], start=True, stop=True),
                 [self.tBTb, self.tCTb], [t5])
            P.op("pe", lambda e: e.transpose(b5[:, 128:256], self.xc[:, c0:c0 + 128], self.identf), [self.txc, self.tcf], [t5])
            P.op("pe", lambda e: e.transpose(btp, self.BTb[:, c0:c0 + 128], self.identb), [self.tBTb, self.tcb], [t5])
        elif st == 2:
            P.op("dve", lambda e: e.tensor_scalar(nacs, b5[:, 256:258], -1.0, None, ALU.mult), [t5], [tnacs])
            P.op("act", lambda e: e.copy(Btok, btp), [t5], [tBtok])
        elif st == 3:
            for h in range(2):
                E_, tE = E[h]
                Dm_, tDm = Dm[h]
                P.op("act", lambda e, E_=E_, h=h: e.activation(E_, b4[:, h * 128:(h + 1) * 128], AF.Exp), [t4], [tE])
                P.op("act", lambda e, Dm_=Dm_, h=h: e.activation(Dm_, b4[:, 256 + h * 128:256 + (h + 1) * 128], AF.Exp,
                                                                bias=nacs[:, h:h + 1]), [t4, tnacs], [tDm])
                P.op("dve", lambda e, h=h: e.tensor_tensor(d2[:, h:h + 1], b4[:, h * 128 + 127:h * 128 + 128],
                                                          nacs[:, h:h + 1], ALU.add), [t4, tnacs], [td2])
        elif st == 4:
            P.op("act", lambda e: e.activation(w2, d2, AF.Exp), [td2], [tw2])
            P.op("dve", lambda e: e.tensor_tensor(dtw, self.dt[:, ci * 2:ci * 2 + 2], w2, ALU.mult), [self.tdt, tw2], [tdtw])
        elif st == 5:
            for h in range(2):
                M_, tM = M[h]
                Dm_, tDm = Dm[h]
                E_, tE = E[h]
                Cs_, tCs = Cs[h]
                xp, txp = xdtp[h]
                P.op("dve", lambda e, M_=M_, Dm_=Dm_: e.tensor_tensor(M_, b5[:, 0:128], Dm_, ALU.mult), [t5, tDm], [tM])
                P.op("dve", lambda e, Cs_=Cs_, E_=E_: e.tensor_tensor(Cs_, self.CTf[:, c0:c0 + 128], E_, ALU.mult),
                     [self.tCTf, tE], [tCs])
                P.op("dve", lambda e, xp=xp, h=h: e.tensor_scalar(
                    xp[:, h * 64:(h + 1) * 64], b5[:, 128 + h * 64:128 + (h + 1) * 64],
                    self.dt[:, ci * 2 + h:ci * 2 + h + 1], None, ALU.mult), [t5, self.tdt], [txp])
                P.op("dve", lambda e, h=h: e.tensor_scalar(
                    xdtw[:, h * 64:(h + 1) * 64], b5[:, 128 + h * 64:128 + (h + 1) * 64],
                    dtw[:, h:h + 1], None, ALU.mult), [t5, tdtw], [txdtw])

    def ssd_rec(self, ci):
        P = self.P
        sl = ci % 2
        bB = 5 if (sl == 0 or not self.two_slot) else 3
        b5, t5 = P.psum[bB], P.pbank[bB]
        b6, t6 = P.psum[6], P.pbank[6]
        c0 = ci * 128
        E, M, Cs, xdtp = self.E[sl], self.M[sl], self.Cs[sl], self.xdtp[sl]
        xdtw, txdtw = self.xdtw[sl]
        Btok, tBtok = self.Btok[sl]
        seqm = [(xdtp[0], M[0]), (xdtp[1], M[1]), (self.Sbp[0], Cs[0]), (self.Sbp[1], Cs[1])]
        for n, ((l, tl), (r, tr)) in enumerate(seqm):
            P.op("pe", lambda e, l=l, r=r, n=n: e.matmul(b6[:, c0:c0 + 128], l, r, start=(n == 0), stop=(n == 3)),
                 [tl, tr], [t6])
        P.op("pe", lambda e: e.matmul(b5[:, 264:392], Btok, xdtw, start=True, stop=True), [tBtok, txdtw], [t5])
        for h in range(2):
            E_, tE = E[h]
            sb, tsb = self.Sbp[h]
            P.op("dve", lambda e, E_=E_, h=h: e.scalar_tensor_tensor(
                self.S[:, h * 64:(h + 1) * 64], self.S[:, h * 64:(h + 1) * 64], E_[:, 127:128],
                b5[:, 264 + h * 64:264 + (h + 1) * 64], ALU.mult, ALU.add), [self.tS, tE, t5], [self.tS])
            P.op("dve", lambda e, sb=sb, h=h: e.tensor_copy(sb[:, h * 64:(h + 1) * 64], self.S[:, h * 64:(h + 1) * 64]),
                 [self.tS], [tsb])

    def post(self, b, i, tok0):
        P = self.P
        b6, t6 = P.psum[6], P.pbank[6]
        P.op("dve", lambda e: e.scalar_tensor_tensor(self.yt, self.xc, self.mc[:, 21:22], b6[:, :], ALU.mult, ALU.add),
             [self.txc, self.tmc, t6], [self.tyt])
        if b == 0 and i == 0:
            P.dump("yt", self.yt, self.tyt); P.dump("S", self.S, self.tS)
        sz, tsz = self.cur_sz
        P.op("dve", lambda e: e.tensor_tensor(self.yg, self.yt, sz, ALU.mult), [self.tyt, tsz], [self.tyg])
        P.dma("act", self.out_d[64:192, tok0:tok0 + T], self.yg, [self.tyg], [self.t_out])

    def attention(self, b, i, filler):
        P = self.P
        tok0 = b * SEQ + i * T
        b7, t7 = P.psum[7], P.pbank[7]
        nblk = 4 * i + 4
        qs = self.QT[0:64, i * T:(i + 1) * T]
        V3 = self.V.rearrange("p (n d) -> p n d", d=64)
        abanks = [0, 1, 2]

        def st_z(n):
            kb = nblk - 1 - n
            j = kb - 4 * i
            diag = j >= 0
            ks = self.KT[0:64, kb * 128:(kb + 1) * 128]
            ab = abanks[n % len(abanks)]
            pa, ta = P.psum[ab], P.pbank[ab]
            ez, tez = self.ez[n % 2]
            L, tL = self.L[n % 4]
            P.op("pe", lambda e: e.matmul(pa[:, :], ks, qs, start=True, stop=False), [self.tKT[kb // 4], self.tQT[i]], [ta])
            if diag:
                P.op("pe", lambda e: e.matmul(pa[:, :], self.identb, self.amask[j], start=False, stop=False), [self.tcb], [ta])
            P.op("act", lambda e: e.activation(ez, pa[:, :], AF.Exp), [ta], [tez])
            P.op("act", lambda e: e.activation(L, ez, AF.Ln, bias=1.0), [tez], [tL])

        def st_a(n):
            kb = nblk - 1 - n
            ab = abanks[n % len(abanks)]
            pa, ta = P.psum[ab], P.pbank[ab]
            L, tL = self.L[n % 4]
            W, tW = self.W[n % 4]
            P.op("pe", lambda e: e.matmul(pa[:, :], self.ntril, L, start=False, stop=(n == 0)), [self.tcb, tL], [ta])
            ls, tls = self.Lsum[n % 3]
            ln_, tln = self.Lsum[(n + 1) % 3]
            if n > 0:
                P.op("pe", lambda e: e.matmul(pa[:, :], self.nones, ls, start=False, stop=True), [self.tcb, tls], [ta])
            P.op("act", lambda e: e.activation(W, pa[:, :], AF.Exp), [ta], [tW])
            if kb > 0:
                if n == 0:
                    P.op("dve", lambda e: e.tensor_copy(ln_, L), [tL], [tln])
                else:
                    P.op("dve", lambda e: e.tensor_tensor(ln_, ls, L, ALU.add), [tL, tls], [tln])

        def st_v(n):
            kb = nblk - 1 - n
            W, tW = self.W[n % 4]
            P.op("pe", lambda e: e.matmul(b7[0:64, :], V3[:, kb, :], W, start=(n == 0), stop=(n == nblk - 1)),
                 [self.tV[kb // 4], tW], [t7])

        units = list(filler)
        per = -(-len(units) // nblk) if units else 0
        SK = 2
        for sidx in range(nblk + 2 * SK):
            if sidx < nblk:
                st_z(sidx)
            if SK <= sidx < nblk + SK:
                st_a(sidx - SK)
            if sidx >= 2 * SK:
                st_v(sidx - 2 * SK)
            for _ in range(per):
                if units:
                    units.pop(0)()
        while units:
            units.pop(0)()
        P.op("act", lambda e: e.copy(self.ob[0:64, :], b7[0:64, :]), [t7], [self.tob])
        P.dma("act", self.out_d[192:256, tok0:tok0 + T], self.ob[0:64, :], [self.tob], [self.t_out])

    @staticmethod
    def merge(mid_u, proj_u):
        out = []
        proj_u = list(proj_u)
        for k, m in enumerate(mid_u):
            out.append(m)
            if k % 3 == 2 and proj_u:
                out.append(proj_u.pop(0))
        return out + proj_u

    def emit(self):
        self.setup()
        nt = self.ntile
        for b in range(self.nseq):
            for u in self.proj_units(b, 0):
                u()
            for u in self.merge(self.mid_units(b, 0), self.proj_units(b, 1) if nt > 1 else []):
                u()
            for i in range(nt):
                mu = self.mid_units(b, i + 1) if i + 1 < nt else []
                pu = self.proj_units(b, i + 2) if i + 2 < nt else []
                self.attention(b, i, self.merge(mu, pu))


def colsT(v):
    return np.ascontiguousarray(np.asarray(v, np.float32).reshape(-1, 128).T)


_CONST = {}


def mixer_consts():
    if "cf" not in _CONST:
        k = np.arange(128)
        triu = (k[:, None] <= k[None, :]).astype(np.float32)
        ident = np.eye(128, dtype=np.float32)
        smask = np.where(k[:, None] > k[None, :], NEG, 0.0).astype(np.float32)
        ones = np.ones((128, 128), np.float32)
        _CONST["cf"] = np.concatenate([triu, ident, smask, ones], 1)
        ntril = -(k[:, None] >= k[None, :]).astype(np.float32)
        t = np.arange(T)
        am = [np.where(128 * j + k[:, None] >= t[None, :], NEG, 0.0).astype(np.float32) for j in range(4)]
        _CONST["cb"] = np.concatenate([ident, ntril, -ones] + am, 1).astype(ml_dtypes.bfloat16)
    return _CONST["cf"], _CONST["cb"]


def mixer_inputs(c, w_in, pool_w, pool_scale, conv_w, conv_b, dt_bias, a_log, d_skip):
    g = c // 2
    bc = c // 4
    XB = 1536
    colsel = np.concatenate([
        np.arange(128 * g, 128 * g + 128),
        np.arange(512 + 128 * c, 512 + 128 * c + 128),
        np.arange(XB + 128 * c, XB + 128 * c + 128),
        np.arange(XB + 1024 + 128 * bc, XB + 1024 + 128 * bc + 128),
        np.arange(XB + 1280 + 128 * bc, XB + 1280 + 128 * bc + 128),
        np.arange(3088 + 64 * c, 3088 + 64 * c + 64),
        np.arange(3600 + 64 * c, 3600 + 64 * c + 64),
        np.arange(4112 + 64 * c, 4112 + 64 * c + 64),
        np.arange(3072 + 2 * c, 3072 + 2 * c + 2),
    ])
    wsel = np.ascontiguousarray(w_in[:, colsel])
    poolw = np.ascontiguousarray(pool_w[g][:, 64 * (c % 2):64 * (c % 2) + 64])
    mc = np.zeros((128, NMC), np.float32)
    chx = np.arange(128 * c, 128 * c + 128)
    chB = np.arange(1024 + 128 * bc, 1024 + 128 * bc + 128)
    chC = np.arange(1280 + 128 * bc, 1280 + 128 * bc + 128)
    for gi, ch in enumerate((chx, chB, chC)):
        for kk in range(4):
            mc[:, 4 * gi + kk] = conv_w[kk, ch]
        mc[:, 12 + gi] = conv_b[ch]
    mc[0:64, 15] = pool_scale[128 * g + 64 * (c % 2):128 * g + 64 * (c % 2) + 64]
    mc[:, 16 + g] = 1.0
    w = 2 ** (g + 1)
    mc[:, 20] = 1.0 / w
    mc[0:64, 21] = d_skip[2 * c]
    mc[64:128, 21] = d_skip[2 * c + 1]
    for j in range(4):
        for h in range(2):
            mc[:, 22 + 2 * j + h] = dt_bias[2 * c + h]
            mc[:, 30 + 2 * j + h] = a_log[2 * c + h]
    invc = np.broadcast_to(1.0 / np.minimum(np.arange(1, T + 1), w).astype(np.float32), (128, T)).copy()
    cf, cb = mixer_consts()
    return {"wsel": wsel, "poolw": poolw, "mc": mc, "invc": invc, "cf": cf, "cb": cb}


_PROGS = {}


def get_chain(n_ffn, has_mix, epi):
    key = ("chain", n_ffn, has_mix, epi)
    if key not in _PROGS:
        P = Prog()
        Chain(P, n_ffn, has_mix, epi).emit()
        _PROGS[key] = P.finish()
    return _PROGS[key]


def get_mixer():
    key = ("mixer",)
    if key not in _PROGS:
        P = Prog()
        Mixer(P).emit()
        _PROGS[key] = P.finish()
    return _PROGS[key]


NCORE = 8


def kernel(x, ffn1_norm, ffn1_w_gate, ffn1_w_up, ffn1_w_down, mix_norm, w_in, pool_w, pool_scale,
           conv_w, conv_b, dt_bias, a_log, d_skip, ssd_norm, w_out, ffn2_norm, ffn2_w_gate,
           ffn2_w_up, ffn2_w_down, final_norm):
    f = lambda a: np.asarray(a, dtype=np.float32)
    x = f(x)
    depth = w_in.shape[0]
    xt = x.reshape(-1, D)
    cores = list(range(NCORE))
    z8 = np.zeros((128, 8), np.float32)
    nc = get_chain(1, False, "u")
    cv = np.concatenate([colsT(f(ffn1_norm[0])), colsT(f(mix_norm[0])), z8], 1)
    maps = []
    for c in cores:
        maps.append({"h_in": np.ascontiguousarray(xt[c * NT:(c + 1) * NT].T), "cvec": cv,
                     "wg0": f(ffn1_w_gate[0]), "wu0": f(ffn1_w_up[0]), "wd0": f(ffn1_w_down[0])})
    res = run_bass_kernel_spmd(nc, maps, core_ids=cores)
    h = [res.results[c]["h_out"] for c in cores]
    u = [res.results[c]["u_out"] for c in cores]
    out = None
    for l in range(depth):
        uT = np.ascontiguousarray(np.concatenate([np.asarray(a) for a in u], axis=1))
        nc = get_mixer()
        maps = []
        for c in cores:
            m = mixer_inputs(c, f(w_in[l]), f(pool_w[l]), f(pool_scale[l]), f(conv_w[l]), f(conv_b[l]),
                             f(dt_bias[l]), f(a_log[l]), f(d_skip[l]))
            m["uT"] = uT
            maps.append(m)
        res = run_bass_kernel_spmd(nc, maps, core_ids=cores)
        mixT = np.empty((D, NCORE * NT), dtype=ml_dtypes.bfloat16)
        for c in cores:
            mo = np.asarray(res.results[c]["mixo"])
            mixT[64 * c:64 * c + 64] = mo[0:64]
            mixT[512 + 128 * c:512 + 128 * c + 128] = mo[64:192]
            mixT[1536 + 64 * c:1536 + 64 * c + 64] = mo[192:256]
        last = (l == depth - 1)
        if not last:
            nc = get_chain(2, True, "u")
            cv = np.concatenate([colsT(f(ffn2_norm[l])), colsT(f(ffn1_norm[l + 1])), colsT(f(mix_norm[l + 1])),
                                 colsT(f(ssd_norm[l]))], 1)
        else:
            nc = get_chain(1, True, "final")
            cv = np.concatenate([colsT(f(ffn2_norm[l])), colsT(f(final_norm)), colsT(f(ssd_norm[l]))], 1)
        maps = []
        for c in cores:
            m = {"h_in": h[c], "cvec": cv, "mixT": np.ascontiguousarray(mixT[:, c * NT:(c + 1) * NT]),
                 "wout": f(w_out[l]),
                 "wg0": f(ffn2_w_gate[l]), "wu0": f(ffn2_w_up[l]), "wd0": f(ffn2_w_down[l])}
            if not last:
                m.update({"wg1": f(ffn1_w_gate[l + 1]), "wu1": f(ffn1_w_up[l + 1]), "wd1": f(ffn1_w_down[l + 1])})
            maps.append(m)
        res = run_bass_kernel_spmd(nc, maps, core_ids=cores)
        if not last:
            h = [res.results[c]["h_out"] for c in cores]
            u = [res.results[c]["u_out"] for c in cores]
        else:
            out = np.concatenate([np.asarray(res.results[c]["o_out"]).T for c in cores], axis=0)
    return np.ascontiguousarray(out.reshape(x.shape).astype(np.float32))
```

```python
import numpy as np
import ml_dtypes
from contextlib import ExitStack
import concourse.bass as bass
import concourse.mybir as mybir
from concourse.bass_utils import run_bass_kernel_spmd

F32 = mybir.dt.float32
BF16 = mybir.dt.bfloat16
AF = mybir.ActivationFunctionType
ALU = mybir.AluOpType
AX = mybir.AxisListType

ENG = ("pe", "act", "dve", "pool", "sp")
NDQ = 8
SAME_ENGINE_SYNC = True


class Tk:
    __slots__ = ("name", "w", "r", "excl")

    def __init__(self, name="", excl=False):
        self.name = name
        self.w = None
        self.r = {}
        self.excl = excl


class Prog:
    def __init__(self, arena_f32=49152):
        self.nc = bass.Bass("TRN2", target_bir_lowering=False)
        self.es = ExitStack()
        self.ops = {e: [] for e in ENG}
        self.cnt = {e: 0 for e in ENG}
        self.dcnt = {}
        self.dnext = {q: 0 for q in ("sp", "act", "pool")}
        self.seen = {e: {} for e in ENG}
        self.sems = {}
        nc = self.nc
        for e in ENG:
            self.sems[e] = self.es.enter_context(nc.semaphore("s_" + e))
        for q in ("sp", "act", "pool"):
            for j in range(NDQ):
                k = "d_%s_%d" % (q, j)
                self.sems[k] = self.es.enter_context(nc.semaphore(k))
                self.dcnt[k] = 0
        self.arena = self.es.enter_context(nc.sbuf_tensor("arena", [128, arena_f32], F32))
        self.arena_n = arena_f32
        self.aoff = 0
        self.psum = []
        self.pbank = []
        for i in range(8):
            t = self.es.enter_context(nc.psum_tensor("ps%d" % i, [128, 512], F32))
            self.psum.append(t)
            self.pbank.append(Tk("ps%d" % i, excl=True))
        self.n_inst = 0

    def reset_arena(self, keep=0):
        self.aoff = keep

    def alloc(self, name, cols, dtype=F32):
        nf = cols if dtype == F32 else (cols + 1) // 2
        nf = (nf + 7) // 8 * 8
        assert self.aoff + nf <= self.arena_n, ("arena overflow", name, self.aoff, nf)
        ap = self.arena[:, self.aoff:self.aoff + nf]
        self.aoff += nf
        if dtype != F32:
            ap = ap.bitcast(dtype)[:, 0:cols]
        else:
            ap = ap[:, 0:cols]
        return ap, Tk(name)

    def dram(self, name, shape, dtype, kind="Internal"):
        return self.nc.dram_tensor(name, list(shape), dtype, kind=kind).ap()

    def _waits(self, e, reads, writes):
        waits = {}

        def need(dep):
            if dep is None:
                return
            k, v = dep
            if k == e and (e == "pe" or not SAME_ENGINE_SYNC):
                return
            if waits.get(k, 0) < v:
                waits[k] = v

        for t in reads:
            need(t.w)
            if t.excl:
                for k, v in t.r.items():
                    need((k, v))
        for t in writes:
            need(t.w)
            for k, v in t.r.items():
                need((k, v))
        wl = []
        for k, v in waits.items():
            if self.seen[e].get(k, 0) < v:
                self.seen[e][k] = v
                wl.append((k, v))
        return wl

    def op(self, e, fn, reads=(), writes=()):
        wl = self._waits(e, reads, writes)
        self.cnt[e] += 1
        c = self.cnt[e]
        sems = self.sems
        semE = sems[e]

        def emit(eng):
            for k, v in wl:
                eng.wait_ge(sems[k], v)
            fn(eng).then_inc(semE, 1)

        self.ops[e].append(emit)
        self.n_inst += 1 + len(wl)
        for t in reads:
            if t.excl:
                t.w = (e, c)
                t.r = {}
            else:
                t.r[e] = c
        for t in writes:
            t.w = (e, c)
            t.r = {}

    def dma(self, q, out_ap, in_ap, reads=(), writes=()):
        wl = self._waits(q, reads, writes)
        j = self.dnext[q]
        self.dnext[q] = (j + 1) % NDQ
        key = "d_%s_%d" % (q, j)
        prev = self.dcnt[key]
        if prev > 0 and self.seen[q].get(key, 0) < prev:
            self.seen[q][key] = prev
            wl.append((key, prev))
        self.dcnt[key] = prev + 16
        v = prev + 16
        sems = self.sems

        def emit(eng):
            for k, vv in wl:
                eng.wait_ge(sems[k], vv)
            eng.dma_start(out=out_ap, in_=in_ap).then_inc(sems[key], 16)

        self.ops[q].append(emit)
        self.n_inst += 1 + len(wl)
        for t in reads:
            t.r[key] = v
        for t in writes:
            t.w = (key, v)
            t.r = {}

    def dump(self, name, ap, tk, dtype=F32):
        if not getattr(self, "debug", False):
            return
        d = self.dram("dbg_" + name, [ap.shape[0], ap.shape[1]], dtype, "ExternalOutput")
        self.dma("sp", d, ap, [tk], [Tk()])

    def barrier(self):
        cur = dict(self.cnt)
        cur.update(self.dcnt)
        sems = self.sems
        for e in ENG:
            wl = []
            for k, v in cur.items():
                if k != e and v > self.seen[e].get(k, 0):
                    self.seen[e][k] = v
                    wl.append((k, v))

            def emit(eng, wl=wl):
                for k, v in wl:
                    eng.wait_ge(sems[k], v)

            self.ops[e].append(emit)
            self.n_inst += len(wl)

    def finish(self):
        self.barrier()
        nc = self.nc
        ops = self.ops
        with nc.Block() as block:
            @block.tensor
            def _(eng):
                for f in ops["pe"]:
                    f(eng)

            @block.scalar
            def _(eng):
                for f in ops["act"]:
                    f(eng)

            @block.vector
            def _(eng):
                for f in ops["dve"]:
                    f(eng)

            @block.gpsimd
            def _(eng):
                for f in ops["pool"]:
                    f(eng)

            @block.sync
            def _(eng):
                for f in ops["sp"]:
                    f(eng)
        self.es.close()
        return nc


D = 2048
DFF = 5632
NT = 2048
T = 512
KD = D // 128
KF = DFF // 128
EPS = 1e-6
WB = 8192
NWB = 5


def v3(ap, k):
    return ap.rearrange("p (k t) -> p k t", k=k)


class Chain:
    def __init__(self, P, n_ffn, has_mix, epilogue):
        self.P = P
        nc = P.nc
        self.n_ffn, self.has_mix, self.epi = n_ffn, has_mix, epilogue
        self.h_in = P.dram("h_in", [D, NT], F32, "ExternalInput")
        self.t_hin = Tk("h_in")
        ncv = 16 * (n_ffn + 1) + 8
        self.ncv = ncv
        self.cv_d = P.dram("cvec", [128, ncv], F32, "ExternalInput")
        self.w32 = []
        self.wbf = []
        self.twb = []
        for i in range(n_ffn):
            for nm, shp in (("wg", [D, DFF]), ("wu", [D, DFF]), ("wd", [DFF, D])):
                self.w32.append(P.dram("%s%d" % (nm, i), shp, F32, "ExternalInput"))
                self.wbf.append(P.dram("%s%d_bf" % (nm, i), shp, BF16))
                self.twb.append([Tk() for _ in range(44)])
        self.first_tile = True
        if has_mix:
            self.mix_d = P.dram("mixT", [D, NT], BF16, "ExternalInput")
            self.wo32 = P.dram("wout", [D, D], F32, "ExternalInput")
            self.wobf = P.dram("wout_bf", [D, D], BF16)
            self.two = [Tk() for _ in range(KD)]
        if epilogue == "u":
            self.h_out = P.dram("h_out", [D, NT], F32, "ExternalOutput")
            self.u_out = P.dram("u_out", [D, NT], BF16, "ExternalOutput")
        else:
            self.o_out = P.dram("o_out", [D, NT], F32, "ExternalOutput")
        self.t_out = Tk("out")
        self.cv, self.tcv = P.alloc("cv", ncv)
        self.ones, self.tones = P.alloc("ones", 128, BF16)
        self.h, self.th = P.alloc("h", KD * T)
        self.u, self.tu = P.alloc("u", KD * T, BF16)
        self.act, self.tact = P.alloc("act", KF * T, BF16)
        self.sq = [P.alloc("sq%d" % i, T, BF16) for i in range(2)]
        self.sg = [P.alloc("sg%d" % i, T) for i in range(2)]
        self.rs, self.trs = P.alloc("rs", T)
        self.wb = [P.alloc("wb%d" % i, WB, BF16) for i in range(NWB)]
        self.wbi = 0
        self.tk_h = [Tk("h%d" % k) for k in range(KD)]
        self.tk_u = [Tk("u%d" % k) for k in range(KD)]
        self.tk_a = [Tk("a%d" % k) for k in range(KF)]
        self.alt = 0

    def nextwb(self):
        w = self.wb[self.wbi]
        self.wbi = (self.wbi + 1) % NWB
        return w

    def ew(self):
        self.alt ^= 1
        return "dve" if self.alt else "pool"

    def cast_weights(self):
        P = self.P
        P.op("pool", lambda e: e.memset(self.ones, 1.0), [], [self.tones])
        P.dma("sp", self.cv, self.cv_d, [], [self.tcv])
        if self.has_mix:
            for k in range(KD):
                P.dma("pool", self.wobf[k * 128:(k + 1) * 128, :], self.wo32[k * 128:(k + 1) * 128, :],
                      [], [self.two[k]])

    def norm_stats(self, src3, tks, idxs, nfeat, bank):
        P = self.P
        ps, tps = P.psum[bank], P.pbank[bank]
        n = len(idxs)
        for i, k in enumerate(idxs):
            sq, tsq = self.sq[i % 2]
            P.op("act", lambda e, k=k, sq=sq: e.activation(sq, src3[:, k, :], AF.Square), [tks[k]], [tsq])
            P.op("pe", lambda e, i=i, sq=sq: e.matmul(ps[:, :], self.ones, sq, start=(i == 0), stop=(i == n - 1)),
                 [self.tones, tsq], [tps])
        P.op("act", lambda e: e.activation(self.rs, ps[:, :], AF.Sqrt, bias=EPS, scale=1.0 / nfeat), [tps], [self.trs])
        P.op("dve", lambda e: e.reciprocal(self.rs, self.rs), [self.trs], [self.trs])

    def rmsnorm(self, gcol0, dst3, tdst):
        P = self.P
        h3 = v3(self.h, KD)
        self.norm_stats(h3, self.tk_h, list(range(KD)), D, 4)
        for k in range(KD):
            P.op("dve", lambda e, k=k: e.scalar_tensor_tensor(
                dst3[:, k, :], h3[:, k, :], self.cv[:, gcol0 + k:gcol0 + k + 1], self.rs, ALU.mult, ALU.mult),
                [self.tk_h[k], self.tcv, self.trs], tdst[k] if isinstance(tdst[k], list) else [tdst[k]])

    def ffn(self, i):
        P = self.P
        h3 = v3(self.h, KD)
        u3 = v3(self.u, KD)
        a3 = v3(self.act, KF)
        wg, wu, wd = self.wbf[3 * i], self.wbf[3 * i + 1], self.wbf[3 * i + 2]
        twg, twu, twd = self.twb[3 * i], self.twb[3 * i + 1], self.twb[3 * i + 2]
        self.rmsnorm(16 * i, u3, self.tk_u)
        wg3 = wg.rearrange("(k p) f -> p k f", p=128)
        wu3 = wu.rearrange("(k p) f -> p k f", p=128)
        wd3 = wd.rearrange("(k p) f -> p k f", p=128)
        wg32 = self.w32[3 * i].rearrange("(k p) f -> p k f", p=128)
        wu32 = self.w32[3 * i + 1].rearrange("(k p) f -> p k f", p=128)
        wd32 = self.w32[3 * i + 2].rearrange("(k p) f -> p k f", p=128)
        gi = 0
        for fg in range(KF // 4):
            (wa, twa), (wb_, twb_) = self.nextwb(), self.nextwb()
            wa3, wb3 = v3(wa, KD), v3(wb_, KD)
            if self.first_tile:
                P.dma("pool", wa3, wg32[:, :, fg * 512:(fg + 1) * 512], [], [twa])
                P.dma("pool", wb3, wu32[:, :, fg * 512:(fg + 1) * 512], [], [twb_])
                P.dma("sp", wg3[:, :, fg * 512:(fg + 1) * 512], wa3, [twa], twg[fg * 4:fg * 4 + 4])
                P.dma("sp", wu3[:, :, fg * 512:(fg + 1) * 512], wb3, [twb_], twu[fg * 4:fg * 4 + 4])
            else:
                P.dma("sp", wa3, wg3[:, :, fg * 512:(fg + 1) * 512], twg[fg * 4:fg * 4 + 4], [twa])
                P.dma("sp", wb3, wu3[:, :, fg * 512:(fg + 1) * 512], twu[fg * 4:fg * 4 + 4], [twb_])
            for f4 in range(4):
                f = fg * 4 + f4
                bg, bu = (0, 1) if gi % 2 == 0 else (2, 3)
                gi += 1
                pg, pu = P.psum[bg], P.psum[bu]
                for k in range(KD):
                    P.op("pe", lambda e, k=k, f4=f4, pg=pg, wa3=wa3: e.matmul(
                        pg[:, :], wa3[:, k, f4 * 128:(f4 + 1) * 128], u3[:, k, :], start=(k == 0), stop=(k == KD - 1)),
                        [twa, self.tk_u[k]], [P.pbank[bg]])
                for k in range(KD):
                    P.op("pe", lambda e, k=k, f4=f4, pu=pu, wb3=wb3: e.matmul(
                        pu[:, :], wb3[:, k, f4 * 128:(f4 + 1) * 128], u3[:, k, :], start=(k == 0), stop=(k == KD - 1)),
                        [twb_, self.tk_u[k]], [P.pbank[bu]])
                sg, tsg = self.sg[f % 2]
                P.op("act", lambda e, sg=sg, pg=pg: e.activation(sg, pg[:, :], AF.Silu), [P.pbank[bg]], [tsg])
                P.op("dve", lambda e, sg=sg, pu=pu, f=f: e.tensor_tensor(a3[:, f, :], sg, pu[:, :], ALU.mult),
                     [tsg, P.pbank[bu]], [self.tk_a[f]])
        FD = 11
        for dg in range(4):
            banks = [4, 5, 6, 7] if dg % 2 == 0 else [0, 1, 2, 3]
            for fgd in range(KF // FD):
                w, tw = self.nextwb()
                w3 = w[:, 0:FD * 512].rearrange("p (k t) -> p k t", k=FD)
                tdw = [twd[fgd * 4 + dg]]
                if self.first_tile:
                    P.dma("pool", w3, wd32[:, fgd * FD:(fgd + 1) * FD, dg * 512:(dg + 1) * 512], [], [tw])
                    P.dma("sp", wd3[:, fgd * FD:(fgd + 1) * FD, dg * 512:(dg + 1) * 512], w3, [tw], tdw)
                else:
                    P.dma("sp", w3, wd3[:, fgd * FD:(fgd + 1) * FD, dg * 512:(dg + 1) * 512], tdw, [tw])
                for j in range(4):
                    pb = P.psum[banks[j]]
                    for f in range(FD):
                        ff = fgd * FD + f
                        P.op("pe", lambda e, j=j, f=f, ff=ff, pb=pb, w3=w3: e.matmul(
                            pb[:, :], w3[:, f, j * 128:(j + 1) * 128], a3[:, ff, :],
                            start=(ff == 0), stop=(ff == KF - 1)),
                            [tw, self.tk_a[ff]], [P.pbank[banks[j]]])
            for j in range(4):
                c = dg * 4 + j
                pb = P.psum[banks[j]]
                P.op("dve", lambda e, c=c, pb=pb: e.scalar_tensor_tensor(
                    h3[:, c, :], pb[:, :], 0.5, h3[:, c, :], ALU.mult, ALU.add),
                    [P.pbank[banks[j]], self.tk_h[c]], [self.tk_h[c]])

    def mix_stage(self, t0):
        P = self.P
        h3 = v3(self.h, KD)
        m3 = v3(self.u, KD)
        P.dma("sp", m3, self.mix_d.rearrange("(k p) t -> p k t", p=128)[:, :, t0:t0 + T], [], self.tk_u)
        gc0 = 16 * (self.n_ffn + 1)
        for grp in range(2):
            idxs = [4 + grp * 4 + c for c in range(4)]
            self.norm_stats(m3, self.tk_u, idxs, 512, 4)
            for c in idxs:
                P.op("dve", lambda e, c=c: e.scalar_tensor_tensor(
                    m3[:, c, :], m3[:, c, :], self.cv[:, gc0 + c - 4:gc0 + c - 3], self.rs, ALU.mult, ALU.mult),
                    [self.tk_u[c], self.tcv, self.trs], [self.tk_u[c]])
        wo3 = self.wobf.rearrange("(k p) f -> p k f", p=128)
        for dg in range(4):
            banks = [0, 1, 2, 3] if dg % 2 == 0 else [4, 5, 6, 7]
            w, tw = self.nextwb()
            w3 = v3(w, KD)
            P.dma("sp", w3, wo3[:, :, dg * 512:(dg + 1) * 512], self.two, [tw])
            for j in range(4):
                pb = P.psum[banks[j]]
                for k in range(KD):
                    P.op("pe", lambda e, j=j, k=k, pb=pb, w3=w3: e.matmul(
                        pb[:, :], w3[:, k, j * 128:(j + 1) * 128], m3[:, k, :], start=(k == 0), stop=(k == KD - 1)),
                        [tw, self.tk_u[k]], [P.pbank[banks[j]]])
            for j in range(4):
                c = dg * 4 + j
                pb = P.psum[banks[j]]
                P.op("dve", lambda e, c=c, pb=pb: e.tensor_tensor(h3[:, c, :], pb[:, :], h3[:, c, :], ALU.add),
                     [P.pbank[banks[j]], self.tk_h[c]], [self.tk_h[c]])

    def emit(self):
        P = self.P
        self.cast_weights()
        h3 = v3(self.h, KD)
        hin3 = self.h_in.rearrange("(k p) t -> p k t", p=128)
        for it in range(NT // T):
            t0 = it * T
            self.first_tile = (it == 0)
            P.dma("sp", h3, hin3[:, :, t0:t0 + T], [self.t_hin], self.tk_h)
            if self.has_mix:
                self.mix_stage(t0)
            for i in range(self.n_ffn):
                self.ffn(i)
            gc = 16 * self.n_ffn
            if self.epi == "u":
                P.dma("act", self.h_out.rearrange("(k p) t -> p k t", p=128)[:, :, t0:t0 + T], h3, self.tk_h, [self.t_out])
                u3 = v3(self.u, KD)
                self.rmsnorm(gc, u3, self.tk_u)
                P.dma("act", self.u_out.rearrange("(k p) t -> p k t", p=128)[:, :, t0:t0 + T], u3, self.tk_u, [self.t_out])
            else:
                o3 = v3(self.act.bitcast(F32)[:, 0:KD * T], KD)
                self.rmsnorm(gc, o3, [[self.tk_a[2 * k], self.tk_a[2 * k + 1]] for k in range(KD)])
                P.dma("act", self.o_out.rearrange("(k p) t -> p k t", p=128)[:, :, t0:t0 + T], o3, self.tk_a[0:2 * KD], [self.t_out])


SEQ = 8192
NSEQ = 2
NTILE = SEQ // T
WSEL = 834
C_POOL, C_Z, C_X, C_B, C_C, C_Q, C_K, C_V, C_DT = 0, 128, 256, 384, 512, 640, 704, 768, 832
NMC = 38
NEG = -30000.0


class Mixer:
    def __init__(self, P, nseq=NSEQ, ntile=NTILE):
        self.P = P
        self.nseq, self.ntile = nseq, ntile
        ntok = nseq * SEQ
        self.u_d = P.dram("uT", [D, ntok], BF16, "ExternalInput")
        self.w32 = P.dram("wsel", [D, WSEL], F32, "ExternalInput")
        self.wbf_d = P.dram("wsel_bf", [D, WSEL], BF16)
        self.pw_d = P.dram("poolw", [128, 64], F32, "ExternalInput")
        self.mc_d = P.dram("mc", [128, NMC], F32, "ExternalInput")
        self.invc_d = P.dram("invc", [128, T], F32, "ExternalInput")
        self.cf_d = P.dram("cf", [128, 4 * 128], F32, "ExternalInput")
        self.cb_d = P.dram("cb", [128, 3 * 128 + 4 * T], BF16, "ExternalInput")
        self.out_d = P.dram("mixo", [256, ntok], BF16, "ExternalOutput")
        self.t_out = Tk("mixo")
        self.twd = [Tk() for _ in range(KD)]
        A = P.alloc
        self.wsb, self.twsb = A("wsb", KD * WSEL, BF16)
        self.ub = [A("ub%d" % i, KD * T, BF16) for i in range(2)]
        self.QT, _ = A("QT", SEQ, BF16)
        self.KT, _ = A("KT", SEQ, BF16)
        self.V, _ = A("V", 64 * 64, BF16)
        self.tQT = [Tk() for _ in range(NTILE)]
        self.tKT = [Tk() for _ in range(NTILE)]
        self.tV = [Tk() for _ in range(NTILE)]
        self.pbk = 0
        self.mc, self.tmc = A("mc", NMC)
        self.invc, self.tinvc = A("invc", T)
        self.cf, self.tcf = A("cf", 4 * 128)
        self.cb, self.tcb = A("cb", 3 * 128 + 4 * T, BF16)
        self.pw32, self.tpw32 = A("pw32", 64)
        self.pwb, self.tpwb = A("pwb", 64, BF16)
        self.Abc, self.tAbc = A("Abc", 8)
        self.ve2 = [A("ve%d" % p, 15 + T) for p in range(2)]
        self.s = [A("s%d" % i, 15 + T) for i in range(4)]
        self.res, self.tres = A("res", T)
        self.pdiff, self.tpdiff = A("pdiff", T, BF16)
        self.po, self.tpo = A("po", T, BF16)
        self.xe2 = [[A("xe%d%d" % (p, i), 3 + T) for i in range(3)] for p in range(2)]
        self.acc = [A("acc%d" % i, T) for i in range(3)]
        self.xc, self.txc = A("xc", T)
        self.BTb, self.tBTb = A("BTb", T, BF16)
        self.CTf, self.tCTf = A("CTf", T)
        self.CTb, self.tCTb = A("CTb", T, BF16)
        self.sz2 = [A("sz%d" % p, T) for p in range(2)]
        self.dtr2 = [A("dtr%d" % p, 8) for p in range(2)]
        self.dx, self.tdx = A("dx", 8)
        self.dax, self.tdax = A("dax", 8)
        self.dt, self.tdt = A("dt", 8)
        self.aa, self.taa = A("aa", 8)
        self.abc = [[A("abc%d%d" % (sl, i), 128) for i in range(2)] for sl in range(2)]
        self.nacs = [A("nacs%d" % sl, 2) for sl in range(2)]
        self.d2 = [A("d2%d" % sl, 2) for sl in range(2)]
        self.w2 = [A("w2%d" % sl, 2) for sl in range(2)]
        self.dtw = [A("dtw%d" % sl, 2) for sl in range(2)]
        self.E = [[A("E%d%d" % (sl, i), 128) for i in range(2)] for sl in range(2)]
        self.Dm = [[A("Dm%d%d" % (sl, i), 128) for i in range(2)] for sl in range(2)]
        self.M = [[A("M%d%d" % (sl, i), 128, BF16) for i in range(2)] for sl in range(2)]
        self.Cs = [[A("Cs%d%d" % (sl, i), 128, BF16) for i in range(2)] for sl in range(2)]
        self.xdtp = [[A("xdtp%d%d" % (sl, i), 128, BF16) for i in range(2)] for sl in range(2)]
        self.xdtw = [A("xdtw%d" % sl, 128, BF16) for sl in range(2)]
        self.Btok = [A("Btok%d" % sl, 128, BF16) for sl in range(2)]
        self.S, self.tS = A("S", 128)
        self.Sbp = [A("Sbp%d" % i, 128, BF16) for i in range(2)]
        self.yt, self.tyt = A("yt", T)
        self.yg, self.tyg = A("yg", T, BF16)
        self.ez = [A("ez%d" % i, T) for i in range(2)]
        self.L = [A("L%d" % i, T, BF16) for i in range(4)]
        self.W = [A("W%d" % i, T, BF16) for i in range(4)]
        self.Lsum = [A("Lsum%d" % i, T, BF16) for i in range(3)]
        self.ob, self.tob = A("ob", T, BF16)
        self.blk = 0

    def setup(self):
        P = self.P
        for k in range(KD):
            P.dma("pool", self.wbf_d[k * 128:(k + 1) * 128, :], self.w32[k * 128:(k + 1) * 128, :], [], [self.twd[k]])
        P.dma("sp", v3(self.wsb, KD), self.wbf_d.rearrange("(k p) f -> p k f", p=128), self.twd, [self.twsb])
        P.dma("sp", self.mc, self.mc_d, [], [self.tmc])
        P.dma("sp", self.invc, self.invc_d, [], [self.tinvc])
        P.dma("sp", self.cf, self.cf_d, [], [self.tcf])
        P.dma("sp", self.cb, self.cb_d, [], [self.tcb])
        P.dma("sp", self.pw32, self.pw_d, [], [self.tpw32])
        P.op("act", lambda e: e.copy(self.pwb, self.pw32), [self.tpw32], [self.tpwb])
        P.op("act", lambda e: e.activation(self.Abc, self.mc[:, 30:38], AF.Exp), [self.tmc], [self.tAbc])
        P.op("dve", lambda e: e.tensor_scalar(self.Abc, self.Abc, -1.0, None, ALU.mult), [self.tAbc], [self.tAbc])
        for sl in range(2):
            for i in range(2):
                x, t = self.xdtp[sl][i]
                P.op("pool", lambda e, x=x: e.memset(x, 0.0), [], [t])
        self.triu = self.cf[:, 0:128]
        self.identf = self.cf[:, 128:256]
        self.smask = self.cf[:, 256:384]
        self.onesf = self.cf[:, 384:512]
        self.identb = self.cb[:, 0:128]
        self.ntril = self.cb[:, 128:256]
        self.nones = self.cb[:, 256:384]
        self.amask = [self.cb[:, 384 + j * T:384 + (j + 1) * T] for j in range(4)]

    def proj_units(self, b, i):
        P = self.P
        it = b * self.ntile + i
        tok0 = b * SEQ + i * T
        ub, tub = self.ub[it % 2]
        ub3 = v3(ub, KD)
        wsb3 = v3(self.wsb, KD)
        first = (i == 0)
        units = []
        pp = i % 2
        ve, tve = self.ve2[pp]
        xes = self.xe2[pp]
        sz, tsz = self.sz2[pp]
        dtr, tdtr = self.dtr2[pp]

        def u_dma():
            P.dma("sp", ub3, self.u_d.rearrange("(k p) t -> p k t", p=128)[:, :, tok0:tok0 + T], [], [tub])
            if first:
                P.op("pool", lambda e: e.memset(ve[:, 0:15], 0.0), [], [tve])
                for g in range(3):
                    xe, txe = xes[g]
                    P.op("pool", lambda e, xe=xe: e.memset(xe[:, 0:3], 0.0), [], [txe])
                P.op("pool", lambda e: e.memset(self.S, 0.0), [], [self.tS])
                for h in range(2):
                    sb, tsb = self.Sbp[h]
                    P.op("pool", lambda e, sb=sb: e.memset(sb, 0.0), [], [tsb])
        units.append(u_dma)

        def bank():
            bnk = 3
            self.pbk += 1
            return bnk

        def fm(c0, ncols, evac):
            def unit():
                bnk = bank()
                ps = P.psum[bnk]
                for k in range(KD):
                    P.op("pe", lambda e, k=k: e.matmul(ps[0:ncols, :], wsb3[:, k, c0:c0 + ncols], ub3[:, k, :],
                                                       start=(k == 0), stop=(k == KD - 1)),
                         [self.twsb, tub], [P.pbank[bnk]])
                evac(ps, P.pbank[bnk])
            units.append(unit)

        fm(C_POOL, 128, lambda ps, tp: P.op("dve", lambda e: e.tensor_copy(ve[:, 15:15 + T], ps[:, :]), [tp], [tve]))
        fm(C_Z, 128, lambda ps, tp: P.op("act", lambda e: e.activation(sz, ps[:, :], AF.Silu), [tp], [tsz]))
        for g, c0 in enumerate((C_X, C_B, C_C)):
            xe, txe = xes[g]
            fm(c0, 128, lambda ps, tp, xe=xe, txe=txe: P.op(
                "dve", lambda e: e.tensor_copy(xe[:, 3:3 + T], ps[:, :]), [tp], [txe]))
        fm(C_Q, 64, lambda ps, tp: P.op("dve", lambda e: e.tensor_scalar(
            self.QT[0:64, i * T:(i + 1) * T], ps[0:64, :], 0.125, None, ALU.mult), [tp], [self.tQT[i]]))
        fm(C_K, 64, lambda ps, tp: P.op("dve", lambda e: e.tensor_copy(
            self.KT[0:64, i * T:(i + 1) * T], ps[0:64, :]), [tp], [self.tKT[i]]))

        def tm():
            bnk = bank()
            ps = P.psum[bnk]
            for j in range(4):
                for k in range(KD):
                    P.op("pe", lambda e, k=k, j=j: e.matmul(ps[:, j * 66:(j + 1) * 66], ub3[:, k, j * 128:(j + 1) * 128],
                                                            wsb3[:, k, C_V:C_V + 66], start=(k == 0), stop=(k == KD - 1)),
                         [self.twsb, tub], [P.pbank[bnk]])
            V3 = self.V.rearrange("p (n d) -> p n d", d=64)
            ps3 = ps[:, 0:264].rearrange("p (j c) -> p j c", c=66)
            P.op("dve", lambda e: e.tensor_copy(V3[:, i * 4:(i + 1) * 4, :], ps3[:, :, 0:64]), [P.pbank[bnk]], [self.tV[i]])
            P.op("dve", lambda e: e.tensor_copy(dtr.rearrange("p (j c) -> p j c", c=2), ps3[:, :, 64:66]),
                 [P.pbank[bnk]], [tdtr])
        units.append(tm)
        return units

    def mid(self, b, i):
        P = self.P
        tok0 = b * SEQ + i * T
        first = (i == 0)
        pp = i % 2
        ve, tve = self.ve2[pp]
        ven, tven = self.ve2[1 - pp]
        xes, xesn = self.xe2[pp], self.xe2[1 - pp]
        self.cur_sz = self.sz2[pp]
        dtr, tdtr = self.dtr2[pp]
        sh = [1, 2, 4, 8]
        lo = [1, 3, 7, 15]
        prev, tprev = ve, tve
        for q in range(4):
            s, ts = self.s[q]
            P.op("pool", lambda e, s=s, prev=prev, q=q: e.tensor_tensor(
                s[:, lo[q]:15 + T], prev[:, lo[q]:15 + T], prev[:, lo[q] - sh[q]:15 + T - sh[q]], ALU.add),
                [tprev], [ts])
            prev, tprev = s, ts
        s0, ts0 = self.s[0]
        P.op("dve", lambda e: e.tensor_scalar(self.res, s0[:, 15:15 + T], self.mc[:, 16:17], None, ALU.mult),
             [ts0, self.tmc], [self.tres])
        for q in range(1, 4):
            s, ts = self.s[q]
            P.op("dve", lambda e, s=s, q=q: e.scalar_tensor_tensor(self.res, s[:, 15:15 + T], self.mc[:, 16 + q:17 + q],
                                                                    self.res, ALU.mult, ALU.add),
                 [ts, self.tmc, self.tres], [self.tres])
        if first:
            P.op("dve", lambda e: e.tensor_tensor(self.res, self.res, self.invc, ALU.mult), [self.tres, self.tinvc], [self.tres])
            P.op("dve", lambda e: e.tensor_tensor(self.pdiff, self.res, ve[:, 15:15 + T], ALU.subtract),
                 [self.tres, tve], [self.tpdiff])
        else:
            P.op("dve", lambda e: e.scalar_tensor_tensor(self.pdiff, self.res, self.mc[:, 20:21], ve[:, 15:15 + T],
                                                          ALU.mult, ALU.subtract),
                 [self.tres, self.tmc, tve], [self.tpdiff])
        P.op("pool", lambda e: e.tensor_copy(ven[:, 0:15], ve[:, T:T + 15]), [tve], [tven])
        bnk = 6
        ps = P.psum[bnk]
        P.op("pe", lambda e, ps=ps: e.matmul(ps[0:64, :], self.pwb, self.pdiff, start=True, stop=True),
             [self.tpwb, self.tpdiff], [P.pbank[bnk]])
        P.op("dve", lambda e, ps=ps: e.tensor_scalar(self.po[0:64, :], ps[0:64, :], self.mc[0:64, 15:16], None, ALU.mult),
             [P.pbank[bnk], self.tmc], [self.tpo])
        P.dma("act", self.out_d[0:64, tok0:tok0 + T], self.po[0:64, :], [self.tpo], [self.t_out])
        yield
        for g in range(3):
            xe, txe = xes[g]
            xen, txen = xesn[g]
            acc, tacc = self.acc[g]
            P.op("dve", lambda e, xe=xe, acc=acc, g=g: e.tensor_scalar(
                acc, xe[:, 3:3 + T], self.mc[:, 4 * g + 3:4 * g + 4], self.mc[:, 12 + g:13 + g], ALU.mult, ALU.add),
                [txe, self.tmc], [tacc])
            for kk in (2, 1, 0):
                P.op("dve", lambda e, xe=xe, acc=acc, g=g, kk=kk: e.scalar_tensor_tensor(
                    acc, xe[:, kk:kk + T], self.mc[:, 4 * g + kk:4 * g + kk + 1], acc, ALU.mult, ALU.add),
                    [txe, self.tmc, tacc], [tacc])
            P.op("pool", lambda e, xe=xe, xen=xen: e.tensor_copy(xen[:, 0:3], xe[:, T:T + 3]), [txe], [txen])
            yield
        P.op("act", lambda e: e.activation(self.xc, self.acc[0][0], AF.Silu), [self.acc[0][1]], [self.txc])
        P.op("act", lambda e: e.activation(self.BTb, self.acc[1][0], AF.Silu), [self.acc[1][1]], [self.tBTb])
        P.op("act", lambda e: e.activation(self.CTf, self.acc[2][0], AF.Silu), [self.acc[2][1]], [self.tCTf])
        P.op("dve", lambda e: e.tensor_copy(self.CTb, self.CTf), [self.tCTf], [self.tCTb])
        P.op("dve", lambda e: e.tensor_tensor(self.dx, dtr, self.mc[:, 22:30], ALU.add), [tdtr, self.tmc], [self.tdx])
        P.op("dve", lambda e: e.scalar_tensor_tensor(self.dax, self.dx, -1.0, self.dx, ALU.mult, ALU.max), [self.tdx], [self.tdax])
        P.op("act", lambda e: e.activation(self.dax, self.dax, AF.Exp, scale=-1.0), [self.tdax], [self.tdax])
        P.op("act", lambda e: e.activation(self.dax, self.dax, AF.Ln, bias=1.0), [self.tdax], [self.tdax])
        P.op("dve", lambda e: e.scalar_tensor_tensor(self.dt, self.dx, 0.0, self.dax, ALU.max, ALU.add),
             [self.tdx, self.tdax], [self.tdt])
        P.op("dve", lambda e: e.tensor_tensor(self.aa, self.dt, self.Abc, ALU.mult), [self.tdt, self.tAbc], [self.taa])
        b4, b5, b6 = P.psum[4], P.psum[5], P.psum[6]
        t4, t5, t6 = P.pbank[4], P.pbank[5], P.pbank[6]
        yield
        self.two_slot = False
        if self.two_slot:
            for pair in ((0, 1), (2, 3)):
                for st in range(6):
                    for ci in pair:
                        self.ssd_stage(st, ci)
                    yield
                for ci in pair:
                    self.ssd_rec(ci)
                    yield
        else:
            for ci in range(4):
                for st in range(6):
                    self.ssd_stage(st, ci)
                    yield
                self.ssd_rec(ci)
                yield
        self.post(b, i, tok0)

    NMID = 34

    def mid_units(self, b, i):
        gen = self.mid(b, i)
        return [(lambda: next(gen, None)) for _ in range(self.NMID + 2)]

    def ssd_stage(self, st, ci):
        P = self.P
        sl = ci % 2
        bA, bB = (4, 5) if (sl == 0 or not self.two_slot) else (2, 3)
        b4, b5 = P.psum[bA], P.psum[bB]
        t4, t5 = P.pbank[bA], P.pbank[bB]
        c0 = ci * 128
        abc = self.abc[sl]
        E, Dm, M, Cs, xdtp = self.E[sl], self.Dm[sl], self.M[sl], self.Cs[sl], self.xdtp[sl]
        nacs, tnacs = self.nacs[sl]
        d2, td2 = self.d2[sl]
        w2, tw2 = self.w2[sl]
        dtw, tdtw = self.dtw[sl]
        xdtw, txdtw = self.xdtw[sl]
        Btok, tBtok = self.Btok[sl]
        btp = b5[:, 392:456].bitcast(BF16)
        if st == 0:
            for h in range(2):
                a_, ta_ = abc[h]
                P.op("dve", lambda e, a_=a_, h=h: e.tensor_scalar(
                    a_, self.onesf, self.aa[:, ci * 2 + h:ci * 2 + h + 1], None, ALU.mult), [self.tcf, self.taa], [ta_])
        elif st == 1:
            for h in range(2):
                a_, ta_ = abc[h]
                P.op("pe", lambda e, a_=a_, h=h: e.matmul(b4[:, h * 128:(h + 1) * 128], a_, self.triu, start=True, stop=True),
                     [ta_, self.tcf], [t4])
            for h in range(2):
                a_, ta_ = abc[h]
                P.op("pe", lambda e, a_=a_, h=h: e.matmul(b4[:, 256 + h * 128:256 + (h + 1) * 128], a_, self.triu,
                                                          start=True, stop=False), [ta_, self.tcf], [t4])
                P.op("pe", lambda e, h=h: e.matmul(b4[:, 256 + h * 128:256 + (h + 1) * 128], self.identf, self.smask,
                                                   start=False, stop=True), [self.tcf], [t4])
            P.op("pe", lambda e: e.matmul(b5[:, 256:258], self.triu, self.aa[:, ci * 2:ci * 2 + 2], start=True, stop=True),
                 [self.tcf, self.taa], [t5])
            P.op("pe", lambda e: e.matmul(b5[:, 0:128], self.BTb[:, c0:c0 + 128], self.CTb[:, c0:c0 + 128], start=True, stop=True),
                 [self.tBTb, self.tCTb], [t5])
            P.op("pe", lambda e: e.transpose(b5[:, 128:256], self.xc[:, c0:c0 + 128], self.identf), [self.txc, self.tcf], [t5])
            P.op("pe", lambda e: e.transpose(btp, self.BTb[:, c0:c0 + 128], self.identb), [self.tBTb, self.tcb], [t5])
        elif st == 2:
            P.op("dve", lambda e: e.tensor_scalar(nacs, b5[:, 256:258], -1.0, None, ALU.mult), [t5], [tnacs])
            P.op("act", lambda e: e.copy(Btok, btp), [t5], [tBtok])
        elif st == 3:
            for h in range(2):
                E_, tE = E[h]
                Dm_, tDm = Dm[h]
                P.op("act", lambda e, E_=E_, h=h: e.activation(E_, b4[:, h * 128:(h + 1) * 128], AF.Exp), [t4], [tE])
                P.op("act", lambda e, Dm_=Dm_, h=h: e.activation(Dm_, b4[:, 256 + h * 128:256 + (h + 1) * 128], AF.Exp,
                                                                bias=nacs[:, h:h + 1]), [t4, tnacs], [tDm])
                P.op("dve", lambda e, h=h: e.tensor_tensor(d2[:, h:h + 1], b4[:, h * 128 + 127:h * 128 + 128],
                                                          nacs[:, h:h + 1], ALU.add), [t4, tnacs], [td2])
        elif st == 4:
            P.op("act", lambda e: e.activation(w2, d2, AF.Exp), [td2], [tw2])
            P.op("dve", lambda e: e.tensor_tensor(dtw, self.dt[:, ci * 2:ci * 2 + 2], w2, ALU.mult), [self.tdt, tw2], [tdtw])
        elif st == 5:
            for h in range(2):
                M_, tM = M[h]
                Dm_, tDm = Dm[h]
                E_, tE = E[h]
                Cs_, tCs = Cs[h]
                xp, txp = xdtp[h]
                P.op("dve", lambda e, M_=M_, Dm_=Dm_: e.tensor_tensor(M_, b5[:, 0:128], Dm_, ALU.mult), [t5, tDm], [tM])
                P.op("dve", lambda e, Cs_=Cs_, E_=E_: e.tensor_tensor(Cs_, self.CTf[:, c0:c0 + 128], E_, ALU.mult),
                     [self.tCTf, tE], [tCs])
                P.op("dve", lambda e, xp=xp, h=h: e.tensor_scalar(
                    xp[:, h * 64:(h + 1) * 64], b5[:, 128 + h * 64:128 + (h + 1) * 64],
                    self.dt[:, ci * 2 + h:ci * 2 + h + 1], None, ALU.mult), [t5, self.tdt], [txp])
                P.op("dve", lambda e, h=h: e.tensor_scalar(
                    xdtw[:, h * 64:(h + 1) * 64], b5[:, 128 + h * 64:128 + (h + 1) * 64],
                    dtw[:, h:h + 1], None, ALU.mult), [t5, tdtw], [txdtw])

    def ssd_rec(self, ci):
        P = self.P
        sl = ci % 2
        bB = 5 if (sl == 0 or not self.two_slot) else 3
        b5, t5 = P.psum[bB], P.pbank[bB]
        b6, t6 = P.psum[6], P.pbank[6]
        c0 = ci * 128
        E, M, Cs, xdtp = self.E[sl], self.M[sl], self.Cs[sl], self.xdtp[sl]
        xdtw, txdtw = self.xdtw[sl]
        Btok, tBtok = self.Btok[sl]
        seqm = [(xdtp[0], M[0]), (xdtp[1], M[1]), (self.Sbp[0], Cs[0]), (self.Sbp[1], Cs[1])]
        for n, ((l, tl), (r, tr)) in enumerate(seqm):
            P.op("pe", lambda e, l=l, r=r, n=n: e.matmul(b6[:, c0:c0 + 128], l, r, start=(n == 0), stop=(n == 3)),
                 [tl, tr], [t6])
        P.op("pe", lambda e: e.matmul(b5[:, 264:392], Btok, xdtw, start=True, stop=True), [tBtok, txdtw], [t5])
        for h in range(2):
            E_, tE = E[h]
            sb, tsb = self.Sbp[h]
            P.op("dve", lambda e, E_=E_, h=h: e.scalar_tensor_tensor(
                self.S[:, h * 64:(h + 1) * 64], self.S[:, h * 64:(h + 1) * 64], E_[:, 127:128],
                b5[:, 264 + h * 64:264 + (h + 1) * 64], ALU.mult, ALU.add), [self.tS, tE, t5], [self.tS])
            P.op("dve", lambda e, sb=sb, h=h: e.tensor_copy(sb[:, h * 64:(h + 1) * 64], self.S[:, h * 64:(h + 1) * 64]),
                 [self.tS], [tsb])

    def post(self, b, i, tok0):
        P = self.P
        b6, t6 = P.psum[6], P.pbank[6]
        P.op("dve", lambda e: e.scalar_tensor_tensor(self.yt, self.xc, self.mc[:, 21:22], b6[:, :], ALU.mult, ALU.add),
             [self.txc, self.tmc, t6], [self.tyt])
        if b == 0 and i == 0:
            P.dump("yt", self.yt, self.tyt); P.dump("S", self.S, self.tS)
        sz, tsz = self.cur_sz
        P.op("dve", lambda e: e.tensor_tensor(self.yg, self.yt, sz, ALU.mult), [self.tyt, tsz], [self.tyg])
        P.dma("act", self.out_d[64:192, tok0:tok0 + T], self.yg, [self.tyg], [self.t_out])

    def attention(self, b, i, filler):
        P = self.P
        tok0 = b * SEQ + i * T
        b7, t7 = P.psum[7], P.pbank[7]
        nblk = 4 * i + 4
        qs = self.QT[0:64, i * T:(i + 1) * T]
        V3 = self.V.rearrange("p (n d) -> p n d", d=64)
        abanks = [0, 1, 2]

        def st_z(n):
            kb = nblk - 1 - n
            j = kb - 4 * i
            diag = j >= 0
            ks = self.KT[0:64, kb * 128:(kb + 1) * 128]
            ab = abanks[n % len(abanks)]
            pa, ta = P.psum[ab], P.pbank[ab]
            ez, tez = self.ez[n % 2]
            L, tL = self.L[n % 4]
            P.op("pe", lambda e: e.matmul(pa[:, :], ks, qs, start=True, stop=False), [self.tKT[kb // 4], self.tQT[i]], [ta])
            if diag:
                P.op("pe", lambda e: e.matmul(pa[:, :], self.identb, self.amask[j], start=False, stop=False), [self.tcb], [ta])
            P.op("act", lambda e: e.activation(ez, pa[:, :], AF.Exp), [ta], [tez])
            P.op("act", lambda e: e.activation(L, ez, AF.Ln, bias=1.0), [tez], [tL])

        def st_a(n):
            kb = nblk - 1 - n
            ab = abanks[n % len(abanks)]
            pa, ta = P.psum[ab], P.pbank[ab]
            L, tL = self.L[n % 4]
            W, tW = self.W[n % 4]
            P.op("pe", lambda e: e.matmul(pa[:, :], self.ntril, L, start=False, stop=(n == 0)), [self.tcb, tL], [ta])
            ls, tls = self.Lsum[n % 3]
            ln_, tln = self.Lsum[(n + 1) % 3]
            if n > 0:
                P.op("pe", lambda e: e.matmul(pa[:, :], self.nones, ls, start=False, stop=True), [self.tcb, tls], [ta])
            P.op("act", lambda e: e.activation(W, pa[:, :], AF.Exp), [ta], [tW])
            if kb > 0:
                if n == 0:
                    P.op("dve", lambda e: e.tensor_copy(ln_, L), [tL], [tln])
                else:
                    P.op("dve", lambda e: e.tensor_tensor(ln_, ls, L, ALU.add), [tL, tls], [tln])

        def st_v(n):
            kb = nblk - 1 - n
            W, tW = self.W[n % 4]
            P.op("pe", lambda e: e.matmul(b7[0:64, :], V3[:, kb, :], W, start=(n == 0), stop=(n == nblk - 1)),
                 [self.tV[kb // 4], tW], [t7])

        units = list(filler)
        per = -(-len(units) // nblk) if units else 0
        SK = 2
        for sidx in range(nblk + 2 * SK):
            if sidx < nblk:
                st_z(sidx)
            if SK <= sidx < nblk + SK:
                st_a(sidx - SK)
            if sidx >= 2 * SK:
                st_v(sidx - 2 * SK)
            for _ in range(per):
                if units:
                    units.pop(0)()
        while units:
            units.pop(0)()
        P.op("act", lambda e: e.copy(self.ob[0:64, :], b7[0:64, :]), [t7], [self.tob])
        P.dma("act", self.out_d[192:256, tok0:tok0 + T], self.ob[0:64, :], [self.tob], [self.t_out])

    @staticmethod
    def merge(mid_u, proj_u):
        out = []
        proj_u = list(proj_u)
        for k, m in enumerate(mid_u):
            out.append(m)
            if k % 3 == 2 and proj_u:
                out.append(proj_u.pop(0))
        return out + proj_u

    def emit(self):
        self.setup()
        nt = self.ntile
        for b in range(self.nseq):
            for u in self.proj_units(b, 0):
                u()
            for u in self.merge(self.mid_units(b, 0), self.proj_units(b, 1) if nt > 1 else []):
                u()
            for i in range(nt):
                mu = self.mid_units(b, i + 1) if i + 1 < nt else []
                pu = self.proj_units(b, i + 2) if i + 2 < nt else []
                self.attention(b, i, self.merge(mu, pu))


def colsT(v):
    return np.ascontiguousarray(np.asarray(v, np.float32).reshape(-1, 128).T)


_CONST = {}


def mixer_consts():
    if "cf" not in _CONST:
        k = np.arange(128)
        triu = (k[:, None] <= k[None, :]).astype(np.float32)
        ident = np.eye(128, dtype=np.float32)
        smask = np.where(k[:, None] > k[None, :], NEG, 0.0).astype(np.float32)
        ones = np.ones((128, 128), np.float32)
        _CONST["cf"] = np.concatenate([triu, ident, smask, ones], 1)
        ntril = -(k[:, None] >= k[None, :]).astype(np.float32)
        t = np.arange(T)
        am = [np.where(128 * j + k[:, None] >= t[None, :], NEG, 0.0).astype(np.float32) for j in range(4)]
        _CONST["cb"] = np.concatenate([ident, ntril, -ones] + am, 1).astype(ml_dtypes.bfloat16)
    return _CONST["cf"], _CONST["cb"]


def mixer_inputs(c, w_in, pool_w, pool_scale, conv_w, conv_b, dt_bias, a_log, d_skip):
    g = c // 2
    bc = c // 4
    XB = 1536
    colsel = np.concatenate([
        np.arange(128 * g, 128 * g + 128),
        np.arange(512 + 128 * c, 512 + 128 * c + 128),
        np.arange(XB + 128 * c, XB + 128 * c + 128),
        np.arange(XB + 1024 + 128 * bc, XB + 1024 + 128 * bc + 128),
        np.arange(XB + 1280 + 128 * bc, XB + 1280 + 128 * bc + 128),
        np.arange(3088 + 64 * c, 3088 + 64 * c + 64),
        np.arange(3600 + 64 * c, 3600 + 64 * c + 64),
        np.arange(4112 + 64 * c, 4112 + 64 * c + 64),
        np.arange(3072 + 2 * c, 3072 + 2 * c + 2),
    ])
    wsel = np.ascontiguousarray(w_in[:, colsel])
    poolw = np.ascontiguousarray(pool_w[g][:, 64 * (c % 2):64 * (c % 2) + 64])
    mc = np.zeros((128, NMC), np.float32)
    chx = np.arange(128 * c, 128 * c + 128)
    chB = np.arange(1024 + 128 * bc, 1024 + 128 * bc + 128)
    chC = np.arange(1280 + 128 * bc, 1280 + 128 * bc + 128)
    for gi, ch in enumerate((chx, chB, chC)):
        for kk in range(4):
            mc[:, 4 * gi + kk] = conv_w[kk, ch]
        mc[:, 12 + gi] = conv_b[ch]
    mc[0:64, 15] = pool_scale[128 * g + 64 * (c % 2):128 * g + 64 * (c % 2) + 64]
    mc[:, 16 + g] = 1.0
    w = 2 ** (g + 1)
    mc[:, 20] = 1.0 / w
    mc[0:64, 21] = d_skip[2 * c]
    mc[64:128, 21] = d_skip[2 * c + 1]
    for j in range(4):
        for h in range(2):
            mc[:, 22 + 2 * j + h] = dt_bias[2 * c + h]
            mc[:, 30 + 2 * j + h] = a_log[2 * c + h]
    invc = np.broadcast_to(1.0 / np.minimum(np.arange(1, T + 1), w).astype(np.float32), (128, T)).copy()
    cf, cb = mixer_consts()
    return {"wsel": wsel, "poolw": poolw, "mc": mc, "invc": invc, "cf": cf, "cb": cb}


_PROGS = {}


def get_chain(n_ffn, has_mix, epi):
    key = ("chain", n_ffn, has_mix, epi)
    if key not in _PROGS:
        P = Prog()
        Chain(P, n_ffn, has_mix, epi).emit()
        _PROGS[key] = P.finish()
    return _PROGS[key]


def get_mixer():
    key = ("mixer",)
    if key not in _PROGS:
        P = Prog()
        Mixer(P).emit()
        _PROGS[key] = P.finish()
    return _PROGS[key]


NCORE = 8


def kernel(x, ffn1_norm, ffn1_w_gate, ffn1_w_up, ffn1_w_down, mix_norm, w_in, pool_w, pool_scale,
           conv_w, conv_b, dt_bias, a_log, d_skip, ssd_norm, w_out, ffn2_norm, ffn2_w_gate,
           ffn2_w_up, ffn2_w_down, final_norm):
    f = lambda a: np.asarray(a, dtype=np.float32)
    x = f(x)
    depth = w_in.shape[0]
    xt = x.reshape(-1, D)
    cores = list(range(NCORE))
    z8 = np.zeros((128, 8), np.float32)
    nc = get_chain(1, False, "u")
    cv = np.concatenate([colsT(f(ffn1_norm[0])), colsT(f(mix_norm[0])), z8], 1)
    maps = []
    for c in cores:
        maps.append({"h_in": np.ascontiguousarray(xt[c * NT:(c + 1) * NT].T), "cvec": cv,
                     "wg0": f(ffn1_w_gate[0]), "wu0": f(ffn1_w_up[0]), "wd0": f(ffn1_w_down[0])})
    res = run_bass_kernel_spmd(nc, maps, core_ids=cores)
    h = [res.results[c]["h_out"] for c in cores]
    u = [res.results[c]["u_out"] for c in cores]
    out = None
    for l in range(depth):
        uT = np.ascontiguousarray(np.concatenate([np.asarray(a) for a in u], axis=1))
        nc = get_mixer()
        maps = []
        for c in cores:
            m = mixer_inputs(c, f(w_in[l]), f(pool_w[l]), f(pool_scale[l]), f(conv_w[l]), f(conv_b[l]),
                             f(dt_bias[l]), f(a_log[l]), f(d_skip[l]))
            m["uT"] = uT
            maps.append(m)
        res = run_bass_kernel_spmd(nc, maps, core_ids=cores)
        mixT = np.empty((D, NCORE * NT), dtype=ml_dtypes.bfloat16)
        for c in cores:
            mo = np.asarray(res.results[c]["mixo"])
            mixT[64 * c:64 * c + 64] = mo[0:64]
            mixT[512 + 128 * c:512 + 128 * c + 128] = mo[64:192]
            mixT[1536 + 64 * c:1536 + 64 * c + 64] = mo[192:256]
        last = (l == depth - 1)
        if not last:
            nc = get_chain(2, True, "u")
            cv = np.concatenate([colsT(f(ffn2_norm[l])), colsT(f(ffn1_norm[l + 1])), colsT(f(mix_norm[l + 1])),
                                 colsT(f(ssd_norm[l]))], 1)
        else:
            nc = get_chain(1, True, "final")
            cv = np.concatenate([colsT(f(ffn2_norm[l])), colsT(f(final_norm)), colsT(f(ssd_norm[l]))], 1)
        maps = []
        for c in cores:
            m = {"h_in": h[c], "cvec": cv, "mixT": np.ascontiguousarray(mixT[:, c * NT:(c + 1) * NT]),
                 "wout": f(w_out[l]),
                 "wg0": f(ffn2_w_gate[l]), "wu0": f(ffn2_w_up[l]), "wd0": f(ffn2_w_down[l])}
            if not last:
                m.update({"wg1": f(ffn1_w_gate[l + 1]), "wu1": f(ffn1_w_up[l + 1]), "wd1": f(ffn1_w_down[l + 1])})
            maps.append(m)
        res = run_bass_kernel_spmd(nc, maps, core_ids=cores)
        if not last:
            h = [res.results[c]["h_out"] for c in cores]
            u = [res.results[c]["u_out"] for c in cores]
        else:
            out = np.concatenate([np.asarray(res.results[c]["o_out"]).T for c in cores], axis=0)
    return np.ascontiguousarray(out.reshape(x.shape).astype(np.float32))
```

```python
import numpy as np
import ml_dtypes
from contextlib import ExitStack
import concourse.bass as bass
import concourse.mybir as mybir
from concourse.bass_utils import run_bass_kernel_spmd

F32 = mybir.dt.float32
BF16 = mybir.dt.bfloat16
AF = mybir.ActivationFunctionType
ALU = mybir.AluOpType
AX = mybir.AxisListType

ENG = ("pe", "act", "dve", "pool", "sp")
NDQ = 8
SAME_ENGINE_SYNC = True


class Tk:
    __slots__ = ("name", "w", "r", "excl")

    def __init__(self, name="", excl=False):
        self.name = name
        self.w = None
        self.r = {}
        self.excl = excl


class Prog:
    def __init__(self, arena_f32=49152):
        self.nc = bass.Bass("TRN2", target_bir_lowering=False)
        self.es = ExitStack()
        self.ops = {e: [] for e in ENG}
        self.cnt = {e: 0 for e in ENG}
        self.dcnt = {}
        self.dnext = {q: 0 for q in ("sp", "act", "pool")}
        self.seen = {e: {} for e in ENG}
        self.sems = {}
        nc = self.nc
        for e in ENG:
            self.sems[e] = self.es.enter_context(nc.semaphore("s_" + e))
        for q in ("sp", "act", "pool"):
            for j in range(NDQ):
                k = "d_%s_%d" % (q, j)
                self.sems[k] = self.es.enter_context(nc.semaphore(k))
                self.dcnt[k] = 0
        self.arena = self.es.enter_context(nc.sbuf_tensor("arena", [128, arena_f32], F32))
        self.arena_n = arena_f32
        self.aoff = 0
        self.psum = []
        self.pbank = []
        for i in range(8):
            t = self.es.enter_context(nc.psum_tensor("ps%d" % i, [128, 512], F32))
            self.psum.append(t)
            self.pbank.append(Tk("ps%d" % i, excl=True))
        self.n_inst = 0

    def reset_arena(self, keep=0):
        self.aoff = keep

    def alloc(self, name, cols, dtype=F32):
        nf = cols if dtype == F32 else (cols + 1) // 2
        nf = (nf + 7) // 8 * 8
        assert self.aoff + nf <= self.arena_n, ("arena overflow", name, self.aoff, nf)
        ap = self.arena[:, self.aoff:self.aoff + nf]
        self.aoff += nf
        if dtype != F32:
            ap = ap.bitcast(dtype)[:, 0:cols]
        else:
            ap = ap[:, 0:cols]
        return ap, Tk(name)

    def dram(self, name, shape, dtype, kind="Internal"):
        return self.nc.dram_tensor(name, list(shape), dtype, kind=kind).ap()

    def _waits(self, e, reads, writes):
        waits = {}

        def need(dep):
            if dep is None:
                return
            k, v = dep
            if k == e and (e == "pe" or not SAME_ENGINE_SYNC):
                return
            if waits.get(k, 0) < v:
                waits[k] = v

        for t in reads:
            need(t.w)
            if t.excl:
                for k, v in t.r.items():
                    need((k, v))
        for t in writes:
            need(t.w)
            for k, v in t.r.items():
                need((k, v))
        wl = []
        for k, v in waits.items():
            if self.seen[e].get(k, 0) < v:
                self.seen[e][k] = v
                wl.append((k, v))
        return wl

    def op(self, e, fn, reads=(), writes=()):
        wl = self._waits(e, reads, writes)
        self.cnt[e] += 1
        c = self.cnt[e]
        sems = self.sems
        semE = sems[e]

        def emit(eng):
            for k, v in wl:
                eng.wait_ge(sems[k], v)
            fn(eng).then_inc(semE, 1)

        self.ops[e].append(emit)
        self.n_inst += 1 + len(wl)
        for t in reads:
            if t.excl:
                t.w = (e, c)
                t.r = {}
            else:
                t.r[e] = c
        for t in writes:
            t.w = (e, c)
            t.r = {}

    def dma(self, q, out_ap, in_ap, reads=(), writes=()):
        wl = self._waits(q, reads, writes)
        j = self.dnext[q]
        self.dnext[q] = (j + 1) % NDQ
        key = "d_%s_%d" % (q, j)
        prev = self.dcnt[key]
        if prev > 0 and self.seen[q].get(key, 0) < prev:
            self.seen[q][key] = prev
            wl.append((key, prev))
        self.dcnt[key] = prev + 16
        v = prev + 16
        sems = self.sems

        def emit(eng):
            for k, vv in wl:
                eng.wait_ge(sems[k], vv)
            eng.dma_start(out=out_ap, in_=in_ap).then_inc(sems[key], 16)

        self.ops[q].append(emit)
        self.n_inst += 1 + len(wl)
        for t in reads:
            t.r[key] = v
        for t in writes:
            t.w = (key, v)
            t.r = {}

    def dump(self, name, ap, tk, dtype=F32):
        if not getattr(self, "debug", False):
            return
        d = self.dram("dbg_" + name, [ap.shape[0], ap.shape[1]], dtype, "ExternalOutput")
        self.dma("sp", d, ap, [tk], [Tk()])

    def barrier(self):
        cur = dict(self.cnt)
        cur.update(self.dcnt)
        sems = self.sems
        for e in ENG:
            wl = []
            for k, v in cur.items():
                if k != e and v > self.seen[e].get(k, 0):
                    self.seen[e][k] = v
                    wl.append((k, v))

            def emit(eng, wl=wl):
                for k, v in wl:
                    eng.wait_ge(sems[k], v)

            self.ops[e].append(emit)
            self.n_inst += len(wl)

    def finish(self):
        self.barrier()
        nc = self.nc
        ops = self.ops
        with nc.Block() as block:
            @block.tensor
            def _(eng):
                for f in ops["pe"]:
                    f(eng)

            @block.scalar
            def _(eng):
                for f in ops["act"]:
                    f(eng)

            @block.vector
            def _(eng):
                for f in ops["dve"]:
                    f(eng)

            @block.gpsimd
            def _(eng):
                for f in ops["pool"]:
                    f(eng)

            @block.sync
            def _(eng):
                for f in ops["sp"]:
                    f(eng)
        self.es.close()
        return nc


D = 2048
DFF = 5632
NT = 2048
T = 512
KD = D // 128
KF = DFF // 128
EPS = 1e-6
WB = 8192
NWB = 4


def v3(ap, k):
    return ap.rearrange("p (k t) -> p k t", k=k)


class Chain:
    def __init__(self, P, n_ffn, has_mix, epilogue):
        self.P = P
        nc = P.nc
        self.n_ffn, self.has_mix, self.epi = n_ffn, has_mix, epilogue
        self.h_in = P.dram("h_in", [D, NT], F32, "ExternalInput")
        self.t_hin = Tk("h_in")
        ncv = 16 * (n_ffn + 1) + 8
        self.ncv = ncv
        self.cv_d = P.dram("cvec", [128, ncv], F32, "ExternalInput")
        self.w32 = []
        self.wbf = []
        self.twb = []
        for i in range(n_ffn):
            for nm, shp in (("wg", [D, DFF]), ("wu", [D, DFF]), ("wd", [DFF, D])):
                self.w32.append(P.dram("%s%d" % (nm, i), shp, F32, "ExternalInput"))
                self.wbf.append(P.dram("%s%d_bf" % (nm, i), shp, BF16))
                self.twb.append([Tk() for _ in range(44)])
        self.first_tile = True
        if has_mix:
            self.mix_d = P.dram("mixT", [D, NT], BF16, "ExternalInput")
            self.wo32 = P.dram("wout", [D, D], F32, "ExternalInput")
            self.wobf = P.dram("wout_bf", [D, D], BF16)
            self.two = [Tk() for _ in range(KD)]
        if epilogue == "u":
            self.h_out = P.dram("h_out", [D, NT], F32, "ExternalOutput")
            self.u_out = P.dram("u_out", [D, NT], BF16, "ExternalOutput")
        else:
            self.o_out = P.dram("o_out", [D, NT], F32, "ExternalOutput")
        self.t_out = Tk("out")
        self.cv, self.tcv = P.alloc("cv", ncv)
        self.ones, self.tones = P.alloc("ones", 128, BF16)
        self.hb = [P.alloc("h%d" % i, KD * T)[0] for i in range(2)]
        self.h = self.hb[0]
        self.u, self.tu = P.alloc("u", KD * T, BF16)
        self.act, self.tact = P.alloc("act", KF * T, BF16)
        self.sq = [P.alloc("sq%d" % i, T, BF16) for i in range(2)]
        self.sg = [P.alloc("sg%d" % i, T) for i in range(2)]
        self.rs, self.trs = P.alloc("rs", T)
        self.wb = [P.alloc("wb%d" % i, WB, BF16) for i in range(NWB)]
        self.wbi = 0
        self.tk_hb = [[Tk("h%d_%d" % (i, k)) for k in range(KD)] for i in range(2)]
        self.tk_h = self.tk_hb[0]
        self.tk_u = [Tk("u%d" % k) for k in range(KD)]
        self.tk_a = [Tk("a%d" % k) for k in range(KF)]
        self.alt = 0

    def nextwb(self):
        w = self.wb[self.wbi]
        self.wbi = (self.wbi + 1) % NWB
        return w

    def ew(self):
        self.alt ^= 1
        return "dve" if self.alt else "pool"

    def cast_weights(self):
        P = self.P
        P.op("pool", lambda e: e.memset(self.ones, 1.0), [], [self.tones])
        P.dma("sp", self.cv, self.cv_d, [], [self.tcv])
        if self.has_mix:
            for k in range(KD):
                P.dma("pool", self.wobf[k * 128:(k + 1) * 128, :], self.wo32[k * 128:(k + 1) * 128, :],
                      [], [self.two[k]])

    def norm_stats(self, src3, tks, idxs, nfeat, bank):
        P = self.P
        ps, tps = P.psum[bank], P.pbank[bank]
        n = len(idxs)
        for i, k in enumerate(idxs):
            sq, tsq = self.sq[i % 2]
            P.op("act", lambda e, k=k, sq=sq: e.activation(sq, src3[:, k, :], AF.Square), [tks[k]], [tsq])
            P.op("pe", lambda e, i=i, sq=sq: e.matmul(ps[:, :], self.ones, sq, start=(i == 0), stop=(i == n - 1)),
                 [self.tones, tsq], [tps])
        P.op("act", lambda e: e.activation(self.rs, ps[:, :], AF.Sqrt, bias=EPS, scale=1.0 / nfeat), [tps], [self.trs])
        P.op("dve", lambda e: e.reciprocal(self.rs, self.rs), [self.trs], [self.trs])

    def rmsnorm(self, gcol0, dst3, tdst):
        P = self.P
        h3 = v3(self.h, KD)
        self.norm_stats(h3, self.tk_h, list(range(KD)), D, 4)
        for k in range(KD):
            P.op("dve", lambda e, k=k: e.scalar_tensor_tensor(
                dst3[:, k, :], h3[:, k, :], self.cv[:, gcol0 + k:gcol0 + k + 1], self.rs, ALU.mult, ALU.mult),
                [self.tk_h[k], self.tcv, self.trs], tdst[k] if isinstance(tdst[k], list) else [tdst[k]])

    def ffn(self, i):
        P = self.P
        h3 = v3(self.h, KD)
        u3 = v3(self.u, KD)
        a3 = v3(self.act, KF)
        wg, wu, wd = self.wbf[3 * i], self.wbf[3 * i + 1], self.wbf[3 * i + 2]
        twg, twu, twd = self.twb[3 * i], self.twb[3 * i + 1], self.twb[3 * i + 2]
        self.rmsnorm(16 * i, u3, self.tk_u)
        wg3 = wg.rearrange("(k p) f -> p k f", p=128)
        wu3 = wu.rearrange("(k p) f -> p k f", p=128)
        wd3 = wd.rearrange("(k p) f -> p k f", p=128)
        wg32 = self.w32[3 * i].rearrange("(k p) f -> p k f", p=128)
        wu32 = self.w32[3 * i + 1].rearrange("(k p) f -> p k f", p=128)
        wd32 = self.w32[3 * i + 2].rearrange("(k p) f -> p k f", p=128)
        gi = 0
        for fg in range(KF // 4):
            (wa, twa), (wb_, twb_) = self.nextwb(), self.nextwb()
            wa3, wb3 = v3(wa, KD), v3(wb_, KD)
            if self.first_tile:
                P.dma("pool", wa3, wg32[:, :, fg * 512:(fg + 1) * 512], [], [twa])
                P.dma("pool", wb3, wu32[:, :, fg * 512:(fg + 1) * 512], [], [twb_])
                P.dma("sp", wg3[:, :, fg * 512:(fg + 1) * 512], wa3, [twa], twg[fg * 4:fg * 4 + 4])
                P.dma("sp", wu3[:, :, fg * 512:(fg + 1) * 512], wb3, [twb_], twu[fg * 4:fg * 4 + 4])
            else:
                P.dma("sp", wa3, wg3[:, :, fg * 512:(fg + 1) * 512], twg[fg * 4:fg * 4 + 4], [twa])
                P.dma("sp", wb3, wu3[:, :, fg * 512:(fg + 1) * 512], twu[fg * 4:fg * 4 + 4], [twb_])
            for f4 in range(4):
                f = fg * 4 + f4
                bg, bu = (0, 1) if gi % 2 == 0 else (2, 3)
                gi += 1
                pg, pu = P.psum[bg], P.psum[bu]
                for k in range(KD):
                    P.op("pe", lambda e, k=k, f4=f4, pg=pg, wa3=wa3: e.matmul(
                        pg[:, :], wa3[:, k, f4 * 128:(f4 + 1) * 128], u3[:, k, :], start=(k == 0), stop=(k == KD - 1)),
                        [twa, self.tk_u[k]], [P.pbank[bg]])
                for k in range(KD):
                    P.op("pe", lambda e, k=k, f4=f4, pu=pu, wb3=wb3: e.matmul(
                        pu[:, :], wb3[:, k, f4 * 128:(f4 + 1) * 128], u3[:, k, :], start=(k == 0), stop=(k == KD - 1)),
                        [twb_, self.tk_u[k]], [P.pbank[bu]])
                sg, tsg = self.sg[f % 2]
                P.op("act", lambda e, sg=sg, pg=pg: e.activation(sg, pg[:, :], AF.Silu), [P.pbank[bg]], [tsg])
                P.op("dve", lambda e, sg=sg, pu=pu, f=f: e.tensor_tensor(a3[:, f, :], sg, pu[:, :], ALU.mult),
                     [tsg, P.pbank[bu]], [self.tk_a[f]])
        FD = 11
        for dg in range(4):
            banks = [4, 5, 6, 7] if dg % 2 == 0 else [0, 1, 2, 3]
            for fgd in range(KF // FD):
                w, tw = self.nextwb()
                w3 = w[:, 0:FD * 512].rearrange("p (k t) -> p k t", k=FD)
                tdw = [twd[fgd * 4 + dg]]
                if self.first_tile:
                    P.dma("pool", w3, wd32[:, fgd * FD:(fgd + 1) * FD, dg * 512:(dg + 1) * 512], [], [tw])
                    P.dma("sp", wd3[:, fgd * FD:(fgd + 1) * FD, dg * 512:(dg + 1) * 512], w3, [tw], tdw)
                else:
                    P.dma("sp", w3, wd3[:, fgd * FD:(fgd + 1) * FD, dg * 512:(dg + 1) * 512], tdw, [tw])
                for j in range(4):
                    pb = P.psum[banks[j]]
                    for f in range(FD):
                        ff = fgd * FD + f
                        P.op("pe", lambda e, j=j, f=f, ff=ff, pb=pb, w3=w3: e.matmul(
                            pb[:, :], w3[:, f, j * 128:(j + 1) * 128], a3[:, ff, :],
                            start=(ff == 0), stop=(ff == KF - 1)),
                            [tw, self.tk_a[ff]], [P.pbank[banks[j]]])
            for j in range(4):
                c = dg * 4 + j
                pb = P.psum[banks[j]]
                P.op("dve", lambda e, c=c, pb=pb: e.scalar_tensor_tensor(
                    h3[:, c, :], pb[:, :], 0.5, h3[:, c, :], ALU.mult, ALU.add),
                    [P.pbank[banks[j]], self.tk_h[c]], [self.tk_h[c]])

    def mix_stage(self, t0):
        P = self.P
        h3 = v3(self.h, KD)
        m3 = v3(self.u, KD)
        P.dma("sp", m3, self.mix_d.rearrange("(k p) t -> p k t", p=128)[:, :, t0:t0 + T], [], self.tk_u)
        gc0 = 16 * (self.n_ffn + 1)
        for grp in range(2):
            idxs = [4 + grp * 4 + c for c in range(4)]
            self.norm_stats(m3, self.tk_u, idxs, 512, 4)
            for c in idxs:
                P.op("dve", lambda e, c=c: e.scalar_tensor_tensor(
                    m3[:, c, :], m3[:, c, :], self.cv[:, gc0 + c - 4:gc0 + c - 3], self.rs, ALU.mult, ALU.mult),
                    [self.tk_u[c], self.tcv, self.trs], [self.tk_u[c]])
        wo3 = self.wobf.rearrange("(k p) f -> p k f", p=128)
        for dg in range(4):
            banks = [0, 1, 2, 3] if dg % 2 == 0 else [4, 5, 6, 7]
            w, tw = self.nextwb()
            w3 = v3(w, KD)
            P.dma("sp", w3, wo3[:, :, dg * 512:(dg + 1) * 512], self.two, [tw])
            for j in range(4):
                pb = P.psum[banks[j]]
                for k in range(KD):
                    P.op("pe", lambda e, j=j, k=k, pb=pb, w3=w3: e.matmul(
                        pb[:, :], w3[:, k, j * 128:(j + 1) * 128], m3[:, k, :], start=(k == 0), stop=(k == KD - 1)),
                        [tw, self.tk_u[k]], [P.pbank[banks[j]]])
            for j in range(4):
                c = dg * 4 + j
                pb = P.psum[banks[j]]
                P.op("dve", lambda e, c=c, pb=pb: e.tensor_tensor(h3[:, c, :], pb[:, :], h3[:, c, :], ALU.add),
                     [P.pbank[banks[j]], self.tk_h[c]], [self.tk_h[c]])

    def emit(self):
        P = self.P
        self.cast_weights()
        hin3 = self.h_in.rearrange("(k p) t -> p k t", p=128)
        for it in range(NT // T):
            t0 = it * T
            self.first_tile = (it == 0)
            self.h, self.tk_h = self.hb[it % 2], self.tk_hb[it % 2]
            h3 = v3(self.h, KD)
            P.dma("sp", h3, hin3[:, :, t0:t0 + T], [self.t_hin], self.tk_h)
            if self.has_mix:
                self.mix_stage(t0)
            for i in range(self.n_ffn):
                self.ffn(i)
            gc = 16 * self.n_ffn
            if self.epi == "u":
                P.dma("act", self.h_out.rearrange("(k p) t -> p k t", p=128)[:, :, t0:t0 + T], h3, self.tk_h, [self.t_out])
                u3 = v3(self.u, KD)
                self.rmsnorm(gc, u3, self.tk_u)
                P.dma("act", self.u_out.rearrange("(k p) t -> p k t", p=128)[:, :, t0:t0 + T], u3, self.tk_u, [self.t_out])
            else:
                o3 = v3(self.act.bitcast(F32)[:, 0:KD * T], KD)
                self.rmsnorm(gc, o3, [[self.tk_a[2 * k], self.tk_a[2 * k + 1]] for k in range(KD)])
                P.dma("act", self.o_out.rearrange("(k p) t -> p k t", p=128)[:, :, t0:t0 + T], o3, self.tk_a[0:2 * KD], [self.t_out])


SEQ = 8192
NSEQ = 2
NTILE = SEQ // T
WSEL = 834
C_POOL, C_Z, C_X, C_B, C_C, C_Q, C_K, C_V, C_DT = 0, 128, 256, 384, 512, 640, 704, 768, 832
NMC = 38
NEG = -30000.0


class Mixer:
    def __init__(self, P, nseq=NSEQ, ntile=NTILE):
        self.P = P
        self.nseq, self.ntile = nseq, ntile
        ntok = nseq * SEQ
        self.u_d = P.dram("uT", [D, ntok], BF16, "ExternalInput")
        self.w32 = P.dram("wsel", [D, WSEL], F32, "ExternalInput")
        self.wbf_d = P.dram("wsel_bf", [D, WSEL], BF16)
        self.pw_d = P.dram("poolw", [128, 64], F32, "ExternalInput")
        self.mc_d = P.dram("mc", [128, NMC], F32, "ExternalInput")
        self.invc_d = P.dram("invc", [128, T], F32, "ExternalInput")
        self.cf_d = P.dram("cf", [128, 4 * 128], F32, "ExternalInput")
        self.cb_d = P.dram("cb", [128, 3 * 128 + 4 * T], BF16, "ExternalInput")
        self.out_d = P.dram("mixo", [256, ntok], BF16, "ExternalOutput")
        self.t_out = Tk("mixo")
        self.twd = [Tk() for _ in range(KD)]
        A = P.alloc
        self.wsb, self.twsb = A("wsb", KD * WSEL, BF16)
        self.ub = [A("ub%d" % i, KD * T, BF16) for i in range(2)]
        self.QT, _ = A("QT", SEQ, BF16)
        self.KT, _ = A("KT", SEQ, BF16)
        self.V, _ = A("V", 64 * 64, BF16)
        self.tQT = [Tk() for _ in range(NTILE)]
        self.tKT = [Tk() for _ in range(NTILE)]
        self.tV = [Tk() for _ in range(NTILE)]
        self.pbk = 0
        self.mc, self.tmc = A("mc", NMC)
        self.invc, self.tinvc = A("invc", T)
        self.cf, self.tcf = A("cf", 4 * 128)
        self.cb, self.tcb = A("cb", 3 * 128 + 4 * T, BF16)
        self.pw32, self.tpw32 = A("pw32", 64)
        self.pwb, self.tpwb = A("pwb", 64, BF16)
        self.Abc, self.tAbc = A("Abc", 8)
        self.ve2 = [A("ve%d" % p, 15 + T) for p in range(2)]
        self.s = [A("s%d" % i, 15 + T) for i in range(4)]
        self.res, self.tres = A("res", T)
        self.pdiff, self.tpdiff = A("pdiff", T, BF16)
        self.po, self.tpo = A("po", T, BF16)
        self.xe2 = [[A("xe%d%d" % (p, i), 3 + T) for i in range(3)] for p in range(2)]
        self.acc = [A("acc%d" % i, T) for i in range(3)]
        self.xc, self.txc = A("xc", T)
        self.BTb, self.tBTb = A("BTb", T, BF16)
        self.CTf, self.tCTf = A("CTf", T)
        self.CTb, self.tCTb = A("CTb", T, BF16)
        self.sz2 = [A("sz%d" % p, T) for p in range(2)]
        self.szs = A("szs", T)
        self.dtr2 = [A("dtr%d" % p, 8) for p in range(2)]
        self.dx, self.tdx = A("dx", 8)
        self.dax, self.tdax = A("dax", 8)
        self.dt, self.tdt = A("dt", 8)
        self.aa, self.taa = A("aa", 8)
        self.abc = [[A("abc%d%d" % (sl, i), 128) for i in range(2)] for sl in range(2)]
        self.nacs = [A("nacs%d" % sl, 2) for sl in range(2)]
        self.d2 = [A("d2%d" % sl, 2) for sl in range(2)]
        self.w2 = [A("w2%d" % sl, 2) for sl in range(2)]
        self.dtw = [A("dtw%d" % sl, 2) for sl in range(2)]
        self.E = [[A("E%d%d" % (sl, i), 128) for i in range(2)] for sl in range(2)]
        self.Dm = [[A("Dm%d%d" % (sl, i), 128) for i in range(2)] for sl in range(2)]
        self.M = [[A("M%d%d" % (sl, i), 128, BF16) for i in range(2)] for sl in range(2)]
        self.Cs = [[A("Cs%d%d" % (sl, i), 128, BF16) for i in range(2)] for sl in range(2)]
        self.xdtp = [[A("xdtp%d%d" % (sl, i), 128, BF16) for i in range(2)] for sl in range(2)]
        self.xdtw = [A("xdtw%d" % sl, 128, BF16) for sl in range(2)]
        self.Btok = [A("Btok%d" % sl, 128, BF16) for sl in range(2)]
        self.S, self.tS = A("S", 128)
        self.Sbp = [A("Sbp%d" % i, 128, BF16) for i in range(2)]
        self.yt, self.tyt = A("yt", T)
        self.yg, self.tyg = A("yg", T, BF16)
        self.ez = [A("ez%d" % i, T) for i in range(2)]
        self.L = [A("L%d" % i, T, BF16) for i in range(4)]
        self.W = [A("W%d" % i, T, BF16) for i in range(4)]
        self.Lsum = [A("Lsum%d" % i, T, BF16) for i in range(3)]
        self.ob, self.tob = A("ob", T, BF16)
        self.blk = 0

    def setup(self):
        P = self.P
        for k in range(KD):
            P.dma("pool", self.wbf_d[k * 128:(k + 1) * 128, :], self.w32[k * 128:(k + 1) * 128, :], [], [self.twd[k]])
        P.dma("sp", v3(self.wsb, KD), self.wbf_d.rearrange("(k p) f -> p k f", p=128), self.twd, [self.twsb])
        P.dma("sp", self.mc, self.mc_d, [], [self.tmc])
        P.dma("sp", self.invc, self.invc_d, [], [self.tinvc])
        P.dma("sp", self.cf, self.cf_d, [], [self.tcf])
        P.dma("sp", self.cb, self.cb_d, [], [self.tcb])
        P.dma("sp", self.pw32, self.pw_d, [], [self.tpw32])
        P.op("act", lambda e: e.copy(self.pwb, self.pw32), [self.tpw32], [self.tpwb])
        P.op("act", lambda e: e.activation(self.Abc, self.mc[:, 30:38], AF.Exp), [self.tmc], [self.tAbc])
        P.op("dve", lambda e: e.tensor_scalar(self.Abc, self.Abc, -1.0, None, ALU.mult), [self.tAbc], [self.tAbc])
        for sl in range(2):
            for i in range(2):
                x, t = self.xdtp[sl][i]
                P.op("pool", lambda e, x=x: e.memset(x, 0.0), [], [t])
        self.triu = self.cf[:, 0:128]
        self.identf = self.cf[:, 128:256]
        self.smask = self.cf[:, 256:384]
        self.onesf = self.cf[:, 384:512]
        self.identb = self.cb[:, 0:128]
        self.ntril = self.cb[:, 128:256]
        self.nones = self.cb[:, 256:384]
        self.amask = [self.cb[:, 384 + j * T:384 + (j + 1) * T] for j in range(4)]

    def proj_units(self, b, i):
        P = self.P
        it = b * self.ntile + i
        tok0 = b * SEQ + i * T
        ub, tub = self.ub[it % 2]
        ub3 = v3(ub, KD)
        wsb3 = v3(self.wsb, KD)
        first = (i == 0)
        units = []
        pp = i % 2
        ve, tve = self.ve2[pp]
        xes = self.xe2[pp]
        sz, tsz = self.sz2[pp]
        dtr, tdtr = self.dtr2[pp]

        def u_dma():
            P.dma("sp", ub3, self.u_d.rearrange("(k p) t -> p k t", p=128)[:, :, tok0:tok0 + T], [], [tub])
            if first:
                P.op("pool", lambda e: e.memset(ve[:, 0:15], 0.0), [], [tve])
                for g in range(3):
                    xe, txe = xes[g]
                    P.op("pool", lambda e, xe=xe: e.memset(xe[:, 0:3], 0.0), [], [txe])
                P.op("pool", lambda e: e.memset(self.S, 0.0), [], [self.tS])
                for h in range(2):
                    sb, tsb = self.Sbp[h]
                    P.op("pool", lambda e, sb=sb: e.memset(sb, 0.0), [], [tsb])
        units.append(u_dma)

        def bank():
            bnk = 3
            self.pbk += 1
            return bnk

        def fm(c0, ncols, evac):
            def unit():
                bnk = bank()
                ps = P.psum[bnk]
                for k in range(KD):
                    P.op("pe", lambda e, k=k: e.matmul(ps[0:ncols, :], wsb3[:, k, c0:c0 + ncols], ub3[:, k, :],
                                                       start=(k == 0), stop=(k == KD - 1)),
                         [self.twsb, tub], [P.pbank[bnk]])
                evac(ps, P.pbank[bnk])
            units.append(unit)

        fm(C_POOL, 128, lambda ps, tp: P.op("dve", lambda e: e.tensor_copy(ve[:, 15:15 + T], ps[:, :]), [tp], [tve]))
        fm(C_Z, 128, lambda ps, tp: P.op("dve", lambda e: e.tensor_copy(sz, ps[:, :]), [tp], [tsz]))
        for g, c0 in enumerate((C_X, C_B, C_C)):
            xe, txe = xes[g]
            fm(c0, 128, lambda ps, tp, xe=xe, txe=txe: P.op(
                "dve", lambda e: e.tensor_copy(xe[:, 3:3 + T], ps[:, :]), [tp], [txe]))
        fm(C_Q, 64, lambda ps, tp: P.op("dve", lambda e: e.tensor_scalar(
            self.QT[0:64, i * T:(i + 1) * T], ps[0:64, :], 0.125, None, ALU.mult), [tp], [self.tQT[i]]))
        fm(C_K, 64, lambda ps, tp: P.op("dve", lambda e: e.tensor_copy(
            self.KT[0:64, i * T:(i + 1) * T], ps[0:64, :]), [tp], [self.tKT[i]]))

        def tm():
            bnk = bank()
            ps = P.psum[bnk]
            for j in range(4):
                for k in range(KD):
                    P.op("pe", lambda e, k=k, j=j: e.matmul(ps[:, j * 66:(j + 1) * 66], ub3[:, k, j * 128:(j + 1) * 128],
                                                            wsb3[:, k, C_V:C_V + 66], start=(k == 0), stop=(k == KD - 1)),
                         [self.twsb, tub], [P.pbank[bnk]])
            V3 = self.V.rearrange("p (n d) -> p n d", d=64)
            ps3 = ps[:, 0:264].rearrange("p (j c) -> p j c", c=66)
            P.op("dve", lambda e: e.tensor_copy(V3[:, i * 4:(i + 1) * 4, :], ps3[:, :, 0:64]), [P.pbank[bnk]], [self.tV[i]])
            P.op("dve", lambda e: e.tensor_copy(dtr.rearrange("p (j c) -> p j c", c=2), ps3[:, :, 64:66]),
                 [P.pbank[bnk]], [tdtr])
        units.append(tm)
        return units

    def mid(self, b, i):
        P = self.P
        tok0 = b * SEQ + i * T
        first = (i == 0)
        pp = i % 2
        ve, tve = self.ve2[pp]
        ven, tven = self.ve2[1 - pp]
        xes, xesn = self.xe2[pp], self.xe2[1 - pp]
        self.cur_sz = self.sz2[pp]
        dtr, tdtr = self.dtr2[pp]
        sh = [1, 2, 4, 8]
        lo = [1, 3, 7, 15]
        prev, tprev = ve, tve
        for q in range(4):
            s, ts = self.s[q]
            P.op("pool", lambda e, s=s, prev=prev, q=q: e.tensor_tensor(
                s[:, lo[q]:15 + T], prev[:, lo[q]:15 + T], prev[:, lo[q] - sh[q]:15 + T - sh[q]], ALU.add),
                [tprev], [ts])
            prev, tprev = s, ts
        s0, ts0 = self.s[0]
        P.op("dve", lambda e: e.tensor_scalar(self.res, s0[:, 15:15 + T], self.mc[:, 16:17], None, ALU.mult),
             [ts0, self.tmc], [self.tres])
        for q in range(1, 4):
            s, ts = self.s[q]
            P.op("dve", lambda e, s=s, q=q: e.scalar_tensor_tensor(self.res, s[:, 15:15 + T], self.mc[:, 16 + q:17 + q],
                                                                    self.res, ALU.mult, ALU.add),
                 [ts, self.tmc, self.tres], [self.tres])
        if first:
            P.op("dve", lambda e: e.tensor_tensor(self.res, self.res, self.invc, ALU.mult), [self.tres, self.tinvc], [self.tres])
            P.op("dve", lambda e: e.tensor_tensor(self.pdiff, self.res, ve[:, 15:15 + T], ALU.subtract),
                 [self.tres, tve], [self.tpdiff])
        else:
            P.op("dve", lambda e: e.scalar_tensor_tensor(self.pdiff, self.res, self.mc[:, 20:21], ve[:, 15:15 + T],
                                                          ALU.mult, ALU.subtract),
                 [self.tres, self.tmc, tve], [self.tpdiff])
        P.op("pool", lambda e: e.tensor_copy(ven[:, 0:15], ve[:, T:T + 15]), [tve], [tven])
        bnk = 6
        ps = P.psum[bnk]
        P.op("pe", lambda e, ps=ps: e.matmul(ps[0:64, :], self.pwb, self.pdiff, start=True, stop=True),
             [self.tpwb, self.tpdiff], [P.pbank[bnk]])
        P.op("dve", lambda e, ps=ps: e.tensor_scalar(self.po[0:64, :], ps[0:64, :], self.mc[0:64, 15:16], None, ALU.mult),
             [P.pbank[bnk], self.tmc], [self.tpo])
        P.dma("act", self.out_d[0:64, tok0:tok0 + T], self.po[0:64, :], [self.tpo], [self.t_out])
        yield
        for g in range(3):
            xe, txe = xes[g]
            xen, txen = xesn[g]
            acc, tacc = self.acc[g]
            P.op("dve", lambda e, xe=xe, acc=acc, g=g: e.tensor_scalar(
                acc, xe[:, 3:3 + T], self.mc[:, 4 * g + 3:4 * g + 4], self.mc[:, 12 + g:13 + g], ALU.mult, ALU.add),
                [txe, self.tmc], [tacc])
            for kk in (2, 1, 0):
                P.op("dve", lambda e, xe=xe, acc=acc, g=g, kk=kk: e.scalar_tensor_tensor(
                    acc, xe[:, kk:kk + T], self.mc[:, 4 * g + kk:4 * g + kk + 1], acc, ALU.mult, ALU.add),
                    [txe, self.tmc, tacc], [tacc])
            P.op("pool", lambda e, xe=xe, xen=xen: e.tensor_copy(xen[:, 0:3], xe[:, T:T + 3]), [txe], [txen])
            yield
        P.op("act", lambda e: e.activation(self.xc, self.acc[0][0], AF.Silu), [self.acc[0][1]], [self.txc])
        P.op("act", lambda e: e.activation(self.BTb, self.acc[1][0], AF.Silu), [self.acc[1][1]], [self.tBTb])
        P.op("act", lambda e: e.activation(self.CTf, self.acc[2][0], AF.Silu), [self.acc[2][1]], [self.tCTf])
        zraw, tzraw = self.cur_sz
        P.op("act", lambda e: e.activation(self.szs[0], zraw, AF.Silu), [tzraw], [self.szs[1]])
        P.op("dve", lambda e: e.tensor_copy(self.CTb, self.CTf), [self.tCTf], [self.tCTb])
        P.op("dve", lambda e: e.tensor_tensor(self.dx, dtr, self.mc[:, 22:30], ALU.add), [tdtr, self.tmc], [self.tdx])
        P.op("dve", lambda e: e.scalar_tensor_tensor(self.dax, self.dx, -1.0, self.dx, ALU.mult, ALU.max), [self.tdx], [self.tdax])
        P.op("act", lambda e: e.activation(self.dax, self.dax, AF.Exp, scale=-1.0), [self.tdax], [self.tdax])
        P.op("act", lambda e: e.activation(self.dax, self.dax, AF.Ln, bias=1.0), [self.tdax], [self.tdax])
        P.op("dve", lambda e: e.scalar_tensor_tensor(self.dt, self.dx, 0.0, self.dax, ALU.max, ALU.add),
             [self.tdx, self.tdax], [self.tdt])
        P.op("dve", lambda e: e.tensor_tensor(self.aa, self.dt, self.Abc, ALU.mult), [self.tdt, self.tAbc], [self.taa])
        b4, b5, b6 = P.psum[4], P.psum[5], P.psum[6]
        t4, t5, t6 = P.pbank[4], P.pbank[5], P.pbank[6]
        yield
        self.two_slot = False
        if self.two_slot:
            for pair in ((0, 1), (2, 3)):
                for st in range(6):
                    for ci in pair:
                        self.ssd_stage(st, ci)
                    yield
                for ci in pair:
                    self.ssd_rec(ci)
                    yield
        else:
            for ci in range(4):
                for st in range(6):
                    self.ssd_stage(st, ci)
                    yield
                self.ssd_rec(ci)
                yield
        self.post(b, i, tok0)

    NMID = 34

    def mid_units(self, b, i):
        gen = self.mid(b, i)
        return [(lambda: next(gen, None)) for _ in range(self.NMID + 2)]

    def ssd_stage(self, st, ci):
        P = self.P
        sl = ci % 2
        bA, bB = (4, 5) if (sl == 0 or not self.two_slot) else (2, 3)
        b4, b5 = P.psum[bA], P.psum[bB]
        t4, t5 = P.pbank[bA], P.pbank[bB]
        c0 = ci * 128
        abc = self.abc[sl]
        E, Dm, M, Cs, xdtp = self.E[sl], self.Dm[sl], self.M[sl], self.Cs[sl], self.xdtp[sl]
        nacs, tnacs = self.nacs[sl]
        d2, td2 = self.d2[sl]
        w2, tw2 = self.w2[sl]
        dtw, tdtw = self.dtw[sl]
        xdtw, txdtw = self.xdtw[sl]
        Btok, tBtok = self.Btok[sl]
        btp = b5[:, 392:456].bitcast(BF16)
        if st == 0:
            for h in range(2):
                a_, ta_ = abc[h]
                P.op("dve", lambda e, a_=a_, h=h: e.tensor_scalar(
                    a_, self.onesf, self.aa[:, ci * 2 + h:ci * 2 + h + 1], None, ALU.mult), [self.tcf, self.taa], [ta_])
        elif st == 1:
            for h in range(2):
                a_, ta_ = abc[h]
                P.op("pe", lambda e, a_=a_, h=h: e.matmul(b4[:, h * 128:(h + 1) * 128], a_, self.triu, start=True, stop=True),
                     [ta_, self.tcf], [t4])
            for h in range(2):
                a_, ta_ = abc[h]
                P.op("pe", lambda e, a_=a_, h=h: e.matmul(b4[:, 256 + h * 128:256 + (h + 1) * 128], a_, self.triu,
                                                          start=True, stop=False), [ta_, self.tcf], [t4])
                P.op("pe", lambda e, h=h: e.matmul(b4[:, 256 + h * 128:256 + (h + 1) * 128], self.identf, self.smask,
                                                   start=False, stop=True), [self.tcf], [t4])
            P.op("pe", lambda e: e.matmul(b5[:, 256:258], self.triu, self.aa[:, ci * 2:ci * 2 + 2], start=True, stop=True),
                 [self.tcf, self.taa], [t5])
            P.op("pe", lambda e: e.matmul(b5[:, 0:128], self.BTb[:, c0:c0 + 128], self.CTb[:, c0:c0 + 128], start=True, stop=True),
                 [self.tBTb, self.tCTb], [t5])
            P.op("pe", lambda e: e.transpose(b5[:, 128:256], self.xc[:, c0:c0 + 128], self.identf), [self.txc, self.tcf], [t5])
            P.op("pe", lambda e: e.transpose(btp, self.BTb[:, c0:c0 + 128], self.identb), [self.tBTb, self.tcb], [t5])
        elif st == 2:
            P.op("dve", lambda e: e.tensor_scalar(nacs, b5[:, 256:258], -1.0, None, ALU.mult), [t5], [tnacs])
            P.op("act", lambda e: e.copy(Btok, btp), [t5], [tBtok])
        elif st == 3:
            for h in range(2):
                E_, tE = E[h]
                Dm_, tDm = Dm[h]
                P.op("act", lambda e, E_=E_, h=h: e.activation(E_, b4[:, h * 128:(h + 1) * 128], AF.Exp), [t4], [tE])
                P.op("act", lambda e, Dm_=Dm_, h=h: e.activation(Dm_, b4[:, 256 + h * 128:256 + (h + 1) * 128], AF.Exp,
                                                                bias=nacs[:, h:h + 1]), [t4, tnacs], [tDm])
                P.op("dve", lambda e, h=h: e.tensor_tensor(d2[:, h:h + 1], b4[:, h * 128 + 127:h * 128 + 128],
                                                          nacs[:, h:h + 1], ALU.add), [t4, tnacs], [td2])
        elif st == 4:
            P.op("act", lambda e: e.activation(w2, d2, AF.Exp), [td2], [tw2])
            P.op("dve", lambda e: e.tensor_tensor(dtw, self.dt[:, ci * 2:ci * 2 + 2], w2, ALU.mult), [self.tdt, tw2], [tdtw])
        elif st == 5:
            for h in range(2):
                M_, tM = M[h]
                Dm_, tDm = Dm[h]
                E_, tE = E[h]
                Cs_, tCs = Cs[h]
                xp, txp = xdtp[h]
                P.op("dve", lambda e, M_=M_, Dm_=Dm_: e.tensor_tensor(M_, b5[:, 0:128], Dm_, ALU.mult), [t5, tDm], [tM])
                P.op("dve", lambda e, Cs_=Cs_, E_=E_: e.tensor_tensor(Cs_, self.CTf[:, c0:c0 + 128], E_, ALU.mult),
                     [self.tCTf, tE], [tCs])
                P.op("dve", lambda e, xp=xp, h=h: e.tensor_scalar(
                    xp[:, h * 64:(h + 1) * 64], b5[:, 128 + h * 64:128 + (h + 1) * 64],
                    self.dt[:, ci * 2 + h:ci * 2 + h + 1], None, ALU.mult), [t5, self.tdt], [txp])
                P.op("dve", lambda e, h=h: e.tensor_scalar(
                    xdtw[:, h * 64:(h + 1) * 64], b5[:, 128 + h * 64:128 + (h + 1) * 64],
                    dtw[:, h:h + 1], None, ALU.mult), [t5, tdtw], [txdtw])

    def ssd_rec(self, ci):
        P = self.P
        sl = ci % 2
        bB = 5 if (sl == 0 or not self.two_slot) else 3
        b5, t5 = P.psum[bB], P.pbank[bB]
        b6, t6 = P.psum[6], P.pbank[6]
        c0 = ci * 128
        E, M, Cs, xdtp = self.E[sl], self.M[sl], self.Cs[sl], self.xdtp[sl]
        xdtw, txdtw = self.xdtw[sl]
        Btok, tBtok = self.Btok[sl]
        seqm = [(xdtp[0], M[0]), (xdtp[1], M[1]), (self.Sbp[0], Cs[0]), (self.Sbp[1], Cs[1])]
        for n, ((l, tl), (r, tr)) in enumerate(seqm):
            P.op("pe", lambda e, l=l, r=r, n=n: e.matmul(b6[:, c0:c0 + 128], l, r, start=(n == 0), stop=(n == 3)),
                 [tl, tr], [t6])
        P.op("pe", lambda e: e.matmul(b5[:, 264:392], Btok, xdtw, start=True, stop=True), [tBtok, txdtw], [t5])
        for h in range(2):
            E_, tE = E[h]
            sb, tsb = self.Sbp[h]
            P.op("dve", lambda e, E_=E_, h=h: e.scalar_tensor_tensor(
                self.S[:, h * 64:(h + 1) * 64], self.S[:, h * 64:(h + 1) * 64], E_[:, 127:128],
                b5[:, 264 + h * 64:264 + (h + 1) * 64], ALU.mult, ALU.add), [self.tS, tE, t5], [self.tS])
            P.op("dve", lambda e, sb=sb, h=h: e.tensor_copy(sb[:, h * 64:(h + 1) * 64], self.S[:, h * 64:(h + 1) * 64]),
                 [self.tS], [tsb])

    def post(self, b, i, tok0):
        P = self.P
        b6, t6 = P.psum[6], P.pbank[6]
        P.op("dve", lambda e: e.scalar_tensor_tensor(self.yt, self.xc, self.mc[:, 21:22], b6[:, :], ALU.mult, ALU.add),
             [self.txc, self.tmc, t6], [self.tyt])
        if b == 0 and i == 0:
            P.dump("yt", self.yt, self.tyt); P.dump("S", self.S, self.tS)
        sz, tsz = self.szs
        P.op("dve", lambda e: e.tensor_tensor(self.yg, self.yt, sz, ALU.mult), [self.tyt, tsz], [self.tyg])
        P.dma("act", self.out_d[64:192, tok0:tok0 + T], self.yg, [self.tyg], [self.t_out])

    def attention(self, b, i, filler):
        P = self.P
        tok0 = b * SEQ + i * T
        b7, t7 = P.psum[7], P.pbank[7]
        nblk = 4 * i + 4
        qs = self.QT[0:64, i * T:(i + 1) * T]
        V3 = self.V.rearrange("p (n d) -> p n d", d=64)
        abanks = [0, 1, 2]

        def st_z(n):
            kb = nblk - 1 - n
            j = kb - 4 * i
            diag = j >= 0
            ks = self.KT[0:64, kb * 128:(kb + 1) * 128]
            ab = abanks[n % len(abanks)]
            pa, ta = P.psum[ab], P.pbank[ab]
            ez, tez = self.ez[n % 2]
            L, tL = self.L[n % 4]
            P.op("pe", lambda e: e.matmul(pa[:, :], ks, qs, start=True, stop=False), [self.tKT[kb // 4], self.tQT[i]], [ta])
            if diag:
                P.op("pe", lambda e: e.matmul(pa[:, :], self.identb, self.amask[j], start=False, stop=False), [self.tcb], [ta])
            P.op("act", lambda e: e.activation(ez, pa[:, :], AF.Exp), [ta], [tez])
            P.op("act", lambda e: e.activation(L, ez, AF.Ln, bias=1.0), [tez], [tL])

        def st_a(n):
            kb = nblk - 1 - n
            ab = abanks[n % len(abanks)]
            pa, ta = P.psum[ab], P.pbank[ab]
            L, tL = self.L[n % 4]
            W, tW = self.W[n % 4]
            P.op("pe", lambda e: e.matmul(pa[:, :], self.ntril, L, start=False, stop=(n == 0)), [self.tcb, tL], [ta])
            ls, tls = self.Lsum[n % 3]
            ln_, tln = self.Lsum[(n + 1) % 3]
            if n > 0:
                P.op("pe", lambda e: e.matmul(pa[:, :], self.nones, ls, start=False, stop=True), [self.tcb, tls], [ta])
            P.op("act", lambda e: e.activation(W, pa[:, :], AF.Exp), [ta], [tW])
            if kb > 0:
                if n == 0:
                    P.op("dve", lambda e: e.tensor_copy(ln_, L), [tL], [tln])
                else:
                    P.op("dve", lambda e: e.tensor_tensor(ln_, ls, L, ALU.add), [tL, tls], [tln])

        def st_v(n):
            kb = nblk - 1 - n
            W, tW = self.W[n % 4]
            P.op("pe", lambda e: e.matmul(b7[0:64, :], V3[:, kb, :], W, start=(n == 0), stop=(n == nblk - 1)),
                 [self.tV[kb // 4], tW], [t7])

        units = list(filler)
        per = -(-len(units) // nblk) if units else 0
        SK = 2
        for sidx in range(nblk + 2 * SK):
            if sidx < nblk:
                st_z(sidx)
            if SK <= sidx < nblk + SK:
                st_a(sidx - SK)
            if sidx >= 2 * SK:
                st_v(sidx - 2 * SK)
            for _ in range(per):
                if units:
                    units.pop(0)()
        while units:
            units.pop(0)()
        P.op("act", lambda e: e.copy(self.ob[0:64, :], b7[0:64, :]), [t7], [self.tob])
        P.dma("act", self.out_d[192:256, tok0:tok0 + T], self.ob[0:64, :], [self.tob], [self.t_out])

    @staticmethod
    def merge(mid_u, proj_u):
        out = []
        proj_u = list(proj_u)
        for k, m in enumerate(mid_u):
            out.append(m)
            if k % 3 == 2 and proj_u:
                out.append(proj_u.pop(0))
        return out + proj_u

    def emit(self):
        self.setup()
        nt = self.ntile
        for b in range(self.nseq):
            for u in self.proj_units(b, 0):
                u()
            for u in self.merge(self.mid_units(b, 0), self.proj_units(b, 1) if nt > 1 else []):
                u()
            for i in range(nt):
                mu = self.mid_units(b, i + 1) if i + 1 < nt else []
                pu = self.proj_units(b, i + 2) if i + 2 < nt else []
                self.attention(b, i, self.merge(mu, pu))


def colsT(v):
    return np.ascontiguousarray(np.asarray(v, np.float32).reshape(-1, 128).T)


_CONST = {}


def mixer_consts():
    if "cf" not in _CONST:
        k = np.arange(128)
        triu = (k[:, None] <= k[None, :]).astype(np.float32)
        ident = np.eye(128, dtype=np.float32)
        smask = np.where(k[:, None] > k[None, :], NEG, 0.0).astype(np.float32)
        ones = np.ones((128, 128), np.float32)
        _CONST["cf"] = np.concatenate([triu, ident, smask, ones], 1)
        ntril = -(k[:, None] >= k[None, :]).astype(np.float32)
        t = np.arange(T)
        am = [np.where(128 * j + k[:, None] >= t[None, :], NEG, 0.0).astype(np.float32) for j in range(4)]
        _CONST["cb"] = np.concatenate([ident, ntril, -ones] + am, 1).astype(ml_dtypes.bfloat16)
    return _CONST["cf"], _CONST["cb"]


def mixer_inputs(c, w_in, pool_w, pool_scale, conv_w, conv_b, dt_bias, a_log, d_skip):
    g = c // 2
    bc = c // 4
    XB = 1536
    colsel = np.concatenate([
        np.arange(128 * g, 128 * g + 128),
        np.arange(512 + 128 * c, 512 + 128 * c + 128),
        np.arange(XB + 128 * c, XB + 128 * c + 128),
        np.arange(XB + 1024 + 128 * bc, XB + 1024 + 128 * bc + 128),
        np.arange(XB + 1280 + 128 * bc, XB + 1280 + 128 * bc + 128),
        np.arange(3088 + 64 * c, 3088 + 64 * c + 64),
        np.arange(3600 + 64 * c, 3600 + 64 * c + 64),
        np.arange(4112 + 64 * c, 4112 + 64 * c + 64),
        np.arange(3072 + 2 * c, 3072 + 2 * c + 2),
    ])
    wsel = np.ascontiguousarray(w_in[:, colsel])
    poolw = np.ascontiguousarray(pool_w[g][:, 64 * (c % 2):64 * (c % 2) + 64])
    mc = np.zeros((128, NMC), np.float32)
    chx = np.arange(128 * c, 128 * c + 128)
    chB = np.arange(1024 + 128 * bc, 1024 + 128 * bc + 128)
    chC = np.arange(1280 + 128 * bc, 1280 + 128 * bc + 128)
    for gi, ch in enumerate((chx, chB, chC)):
        for kk in range(4):
            mc[:, 4 * gi + kk] = conv_w[kk, ch]
        mc[:, 12 + gi] = conv_b[ch]
    mc[0:64, 15] = pool_scale[128 * g + 64 * (c % 2):128 * g + 64 * (c % 2) + 64]
    mc[:, 16 + g] = 1.0
    w = 2 ** (g + 1)
    mc[:, 20] = 1.0 / w
    mc[0:64, 21] = d_skip[2 * c]
    mc[64:128, 21] = d_skip[2 * c + 1]
    for j in range(4):
        for h in range(2):
            mc[:, 22 + 2 * j + h] = dt_bias[2 * c + h]
            mc[:, 30 + 2 * j + h] = a_log[2 * c + h]
    invc = np.broadcast_to(1.0 / np.minimum(np.arange(1, T + 1), w).astype(np.float32), (128, T)).copy()
    cf, cb = mixer_consts()
    return {"wsel": wsel, "poolw": poolw, "mc": mc, "invc": invc, "cf": cf, "cb": cb}


_PROGS = {}


def get_chain(n_ffn, has_mix, epi):
    key = ("chain", n_ffn, has_mix, epi)
    if key not in _PROGS:
        P = Prog(arena_f32=53000)
        Chain(P, n_ffn, has_mix, epi).emit()
        _PROGS[key] = P.finish()
    return _PROGS[key]


def get_mixer():
    key = ("mixer",)
    if key not in _PROGS:
        P = Prog()
        Mixer(P).emit()
        _PROGS[key] = P.finish()
    return _PROGS[key]


NCORE = 8


def kernel(x, ffn1_norm, ffn1_w_gate, ffn1_w_up, ffn1_w_down, mix_norm, w_in, pool_w, pool_scale,
           conv_w, conv_b, dt_bias, a_log, d_skip, ssd_norm, w_out, ffn2_norm, ffn2_w_gate,
           ffn2_w_up, ffn2_w_down, final_norm):
    f = lambda a: np.asarray(a, dtype=np.float32)
    x = f(x)
    depth = w_in.shape[0]
    xt = x.reshape(-1, D)
    cores = list(range(NCORE))
    z8 = np.zeros((128, 8), np.float32)
    nc = get_chain(1, False, "u")
    cv = np.concatenate([colsT(f(ffn1_norm[0])), colsT(f(mix_norm[0])), z8], 1)
    maps = []
    for c in cores:
        maps.append({"h_in": np.ascontiguousarray(xt[c * NT:(c + 1) * NT].T), "cvec": cv,
                     "wg0": f(ffn1_w_gate[0]), "wu0": f(ffn1_w_up[0]), "wd0": f(ffn1_w_down[0])})
    res = run_bass_kernel_spmd(nc, maps, core_ids=cores)
    h = [res.results[c]["h_out"] for c in cores]
    u = [res.results[c]["u_out"] for c in cores]
    out = None
    for l in range(depth):
        uT = np.ascontiguousarray(np.concatenate([np.asarray(a) for a in u], axis=1))
        nc = get_mixer()
        maps = []
        for c in cores:
            m = mixer_inputs(c, f(w_in[l]), f(pool_w[l]), f(pool_scale[l]), f(conv_w[l]), f(conv_b[l]),
                             f(dt_bias[l]), f(a_log[l]), f(d_skip[l]))
            m["uT"] = uT
            maps.append(m)
        res = run_bass_kernel_spmd(nc, maps, core_ids=cores)
        mixT = np.empty((D, NCORE * NT), dtype=ml_dtypes.bfloat16)
        for c in cores:
            mo = np.asarray(res.results[c]["mixo"])
            mixT[64 * c:64 * c + 64] = mo[0:64]
            mixT[512 + 128 * c:512 + 128 * c + 128] = mo[64:192]
            mixT[1536 + 64 * c:1536 + 64 * c + 64] = mo[192:256]
        last = (l == depth - 1)
        if not last:
            nc = get_chain(2, True, "u")
            cv = np.concatenate([colsT(f(ffn2_norm[l])), colsT(f(ffn1_norm[l + 1])), colsT(f(mix_norm[l + 1])),
                                 colsT(f(ssd_norm[l]))], 1)
        else:
            nc = get_chain(1, True, "final")
            cv = np.concatenate([colsT(f(ffn2_norm[l])), colsT(f(final_norm)), colsT(f(ssd_norm[l]))], 1)
        maps = []
        for c in cores:
            m = {"h_in": h[c], "cvec": cv, "mixT": np.ascontiguousarray(mixT[:, c * NT:(c + 1) * NT]),
                 "wout": f(w_out[l]),
                 "wg0": f(ffn2_w_gate[l]), "wu0": f(ffn2_w_up[l]), "wd0": f(ffn2_w_down[l])}
            if not last:
                m.update({"wg1": f(ffn1_w_gate[l + 1]), "wu1": f(ffn1_w_up[l + 1]), "wd1": f(ffn1_w_down[l + 1])})
            maps.append(m)
        res = run_bass_kernel_spmd(nc, maps, core_ids=cores)
        if not last:
            h = [res.results[c]["h_out"] for c in cores]
            u = [res.results[c]["u_out"] for c in cores]
        else:
            out = np.concatenate([np.asarray(res.results[c]["o_out"]).T for c in cores], axis=0)
    return np.ascontiguousarray(out.reshape(x.shape).astype(np.float32))
```
